# Optimizing a Trainium2 kernel written in Bass

```python
import math
import jax, jax.numpy as jnp
from jax import lax
import numpy as np

D_MODEL = 2048
BATCH = 32
SEQ = 256
DEPTH = 2
DEC_BATCH = 4
DEC_SEQ = 2048
PAST_LEN = 512

GRID_W = 64
N_EVEN = (DEPTH + 1) // 2
N_ODD = DEPTH // 2
N_MOD = 6
EPS = 1e-6
MLSTM_HEADS = 4
MLSTM_DH = D_MODEL // (2 * MLSTM_HEADS)
MLSTM_W = MLSTM_HEADS * MLSTM_DH
MLSTM_CHUNK = 128
N_GATES = 4 * MLSTM_HEADS
POOL_GROUPS = 4
POOL_WINDOWS = (2, 4, 8, 16)
POOL_W = D_MODEL // 2
POOL_GW = POOL_W // POOL_GROUPS
EVEN_IN = 4 * MLSTM_W + POOL_W + N_GATES
EVEN_OUT = MLSTM_W + POOL_W
DA_HEADS = 8
DA_DH = D_MODEL // (4 * DA_HEADS)
DA_DV = 2 * DA_DH
DA_W = DA_HEADS * DA_DV
Q_BLOCK = 128
ROPE_BASE = 10000.0
GM_GROUPS = 8
GM_CHUNK = 128
GM_W = D_MODEL // 2
GM_GW = GM_W // GM_GROUPS
ODD_IN = 3 * DA_W + 2 * GM_W
ODD_OUT = DA_W + GM_W
PEER_HEADS = 8
PEER_NKEYS = 128
PEER_EXPERTS = PEER_NKEYS * PEER_NKEYS
PEER_DQ = 256
PEER_HALF = PEER_DQ // 2
PEER_TOPK = 16
PEER_TOKEN_BLOCK = 128

kernel_name = 'hybrid_mlstm_pool_diffattn_gmlp_peer_step'


def rms(x):
    xf = x.astype(jnp.float32)
    return xf * lax.rsqrt(jnp.mean(xf * xf, axis=-1, keepdims=True) + EPS)


def rms_norm(x, gain):
    return (rms(x) * gain).astype(x.dtype)


def adaln(cond, w, b):
    return (jax.nn.silu(cond) @ w + b).reshape(cond.shape[0], N_MOD, D_MODEL)


def modulate(x, gain, mod, i):
    return rms_norm(x, gain) * (1.0 + mod[:, None, i + 1]) + mod[:, None, i]


def mlstm_chunkwise(q, k, v, ig, flog, C0, n0, m0):
    B, H, T, _ = q.shape
    nc = T // MLSTM_CHUNK

    def chunks(a):
        return jnp.moveaxis(a.reshape(B, H, nc, MLSTM_CHUNK, *a.shape[3:]), 2, 0)

    lower = jnp.tril(jnp.ones((MLSTM_CHUNK, MLSTM_CHUNK), dtype=bool))

    def step(carry, xs):
        C, n, m = carry
        qc, kc, vc, ic, fc = xs
        b = jnp.cumsum(fc, axis=-1)
        dmat = jnp.where(lower, b[..., :, None] - b[..., None, :] + ic[..., None, :], -jnp.inf)
        inter = b + m[..., None]
        m_t = jnp.maximum(inter, jnp.max(dmat, axis=-1))
        w = jnp.exp(dmat - m_t[..., None])
        a = jnp.exp(inter - m_t)
        s = jnp.einsum('bhtd,bhsd->bhts', qc, kc) * w
        num = a[..., None] * jnp.einsum('bhtd,bhdv->bhtv', qc, C) + jnp.einsum('bhts,bhsv->bhtv', s, vc)
        den = a * jnp.einsum('bhtd,bhd->bht', qc, n) + jnp.sum(s, axis=-1)
        h = num / jnp.maximum(jnp.abs(den), jnp.exp(-m_t))[..., None]
        b_end = b[..., -1]
        g = b_end[..., None] - b + ic
        m_new = jnp.maximum(b_end + m, jnp.max(g, axis=-1))
        decay = jnp.exp(b_end + m - m_new)
        ws = jnp.exp(g - m_new[..., None])
        C_new = decay[..., None, None] * C + jnp.einsum('bhsd,bhsv->bhdv', kc * ws[..., None], vc)
        n_new = decay[..., None] * n + jnp.einsum('bhs,bhsd->bhd', ws, kc)
        return (C_new, n_new, m_new), h

    carry0 = (C0.astype(jnp.float32), n0.astype(jnp.float32), m0.astype(jnp.float32))
    (C, n, m), hs = lax.scan(step, carry0, (chunks(q), chunks(k), chunks(v), chunks(ig), chunks(flog)))
    h = jnp.moveaxis(hs, 0, 2).reshape(B, H, T, -1)
    return h, C, n, m


def mlstm_mixer(q, k, v, o, gates, gain, C0, n0, m0):
    B, T, _ = q.shape

    def heads(a):
        return a.reshape(B, T, MLSTM_HEADS, MLSTM_DH).transpose(0, 2, 1, 3)

    qh, kh, vh = heads(q), heads(k) * (MLSTM_DH ** -0.5), heads(v)
    g = gates.astype(jnp.float32).reshape(B, T, 4, MLSTM_HEADS).transpose(2, 0, 3, 1)
    ig, flog = g[:2], jax.nn.log_sigmoid(g[2:])
    h_f, C_f, n_f, m_f = mlstm_chunkwise(qh, kh, vh, ig[0], flog[0], C0[:, 0], n0[:, 0], m0[:, 0])
    rev = lambda a: jnp.flip(a, axis=2)
    h_b, C_b, n_b, m_b = mlstm_chunkwise(rev(qh), rev(kh), rev(vh), rev(ig[1]), rev(flog[1]),
                                         C0[:, 1], n0[:, 1], m0[:, 1])
    h = rms_norm(h_f + rev(h_b), gain.reshape(MLSTM_HEADS, 1, MLSTM_DH))
    h = h.transpose(0, 2, 1, 3).reshape(B, T, MLSTM_W) * jax.nn.sigmoid(o)
    return h, jnp.stack([C_f, C_b], 1), jnp.stack([n_f, n_b], 1), jnp.stack([m_f, m_b], 1)


def pool_mixer(x, w_grp, scale):
    B, T, _ = x.shape
    xf = x.astype(jnp.float32).reshape(B, T, POOL_GROUPS, POOL_GW)
    S = jnp.concatenate([jnp.zeros((B, 1, POOL_GROUPS, POOL_GW), jnp.float32), jnp.cumsum(xf, axis=1)], axis=1)
    t = jnp.arange(T)
    outs = []
    for g, w in enumerate(POOL_WINDOWS):
        lo = jnp.clip(t - w // 2, 0, T)
        hi = jnp.clip(t - w // 2 + w, 0, T)
        cnt = (hi - lo).astype(jnp.float32)
        Sg = S[:, :, g]
        outs.append((Sg[:, hi] - Sg[:, lo]) / cnt[None, :, None] - xf[:, :, g])
    p = jnp.stack(outs, axis=2)
    y = jnp.einsum('btgc,gcd->btgd', p, w_grp).reshape(B, T, POOL_W)
    return y * scale


def even_mixer(xn, w_in, b_gate, gain, pool_w, pool_scale, w_out, C0, n0, m0):
    z = xn @ w_in
    q, k, v, o, p, gates = jnp.split(
        z, [MLSTM_W, 2 * MLSTM_W, 3 * MLSTM_W, 4 * MLSTM_W, 4 * MLSTM_W + POOL_W], axis=-1)
    h_m, C, n, m = mlstm_mixer(q, k, v, o, gates + b_gate, gain, C0, n0, m0)
    h_p = pool_mixer(p, pool_w, pool_scale)
    return jnp.concatenate([h_m, h_p], axis=-1) @ w_out, C, n, m


def axial_rope_angles(n_tokens):
    rows = n_tokens // GRID_W
    row = jnp.repeat(jnp.arange(rows), GRID_W).astype(jnp.float32)
    col = (jnp.arange(rows * GRID_W) % GRID_W).astype(jnp.float32)
    n_freq = DA_DH // 4
    inv = ROPE_BASE ** (-jnp.arange(n_freq, dtype=jnp.float32) / n_freq)
    return row[:, None] * inv, col[:, None] * inv


def rotate_pairs(x, ang):
    x1, x2 = jnp.split(x, 2, axis=-1)
    cos, sin = jnp.cos(ang), jnp.sin(ang)
    return jnp.concatenate([x1 * cos - x2 * sin, x1 * sin + x2 * cos], axis=-1)


def apply_axial_rope(x, ang_row, ang_col):
    ar = ang_row[None, :, None, None, :]
    ac = ang_col[None, :, None, None, :]
    xr, xc = jnp.split(x, 2, axis=-1)
    return jnp.concatenate([rotate_pairs(xr, ar), rotate_pairs(xc, ac)], axis=-1).astype(x.dtype)


def odd_project(xn, w_in, qk_gain, rope):
    B, T, _ = xn.shape
    z = xn @ w_in
    q, k, v, gu, gv = jnp.split(z, [DA_W, 2 * DA_W, 3 * DA_W, 3 * DA_W + GM_W], axis=-1)
    q = rms_norm(q.reshape(B, T, DA_HEADS, 2, DA_DH), qk_gain[0])
    k = rms_norm(k.reshape(B, T, DA_HEADS, 2, DA_DH), qk_gain[1])
    if rope is not None:
        q = apply_axial_rope(q, *rope)
        k = apply_axial_rope(k, *rope)
    q = q.transpose(0, 2, 1, 3, 4)
    k = k.transpose(0, 2, 1, 3, 4)
    v = v.reshape(B, T, DA_HEADS, DA_DV).transpose(0, 2, 1, 3)
    return q, k, v, jax.nn.gelu(gu), jax.nn.gelu(gv)


def diff_lambda(lp, lam_init):
    lp = lp.astype(jnp.float32)
    return jnp.exp(jnp.sum(lp[0] * lp[1])) - jnp.exp(jnp.sum(lp[2] * lp[3])) + lam_init


def diff_attention(q, k, v, lam, lam_init, subln):
    B, H, Tq = q.shape[:3]
    nb = Tq // Q_BLOCK
    qb = jnp.moveaxis(q.reshape(B, H, nb, Q_BLOCK, 2, DA_DH), 2, 0)

    def one_block(qq):
        s = jnp.einsum('bhqmd,bhkmd->bhmqk', qq, k).astype(jnp.float32) * (DA_DH ** -0.5)
        p = jax.nn.softmax(s, axis=-1)
        a = p[:, :, 0] - lam * p[:, :, 1]
        return jnp.einsum('bhqk,bhkv->bhqv', a, v)

    o = jnp.moveaxis(lax.map(one_block, qb), 0, 2).reshape(B, H, Tq, DA_DV)
    o = rms_norm(o, subln) * (1.0 - lam_init)
    return o.transpose(0, 2, 1, 3).reshape(B, Tq, DA_W)


def chunk_mlp(u, v, ws, b):
    B, T, _ = u.shape
    nc = T // GM_CHUNK
    vg = rms(v.reshape(B, nc, GM_CHUNK, GM_GROUPS, GM_GW))
    mixed = jnp.einsum('gts,bnsgc->bntgc', ws, vg) + b.T[None, None, :, :, None]
    return u * mixed.reshape(B, T, GM_W)


def odd_finish(attn, gu, gv, gm_ws, gm_b, w_out):
    return jnp.concatenate([attn, chunk_mlp(gu, gv, gm_ws, gm_b)], axis=-1) @ w_out


def peer(x, wq, subkeys, U, V):
    B, T, Dm = x.shape
    N = B * T
    xf = x.reshape(N, Dm)
    q = (xf @ wq).reshape(N, PEER_HEADS, 2, PEER_HALF)
    s = jnp.einsum('nhpd,hpkd->nhpk', q, subkeys).astype(jnp.float32)
    sv, si = lax.top_k(s, PEER_TOPK)
    cand = (sv[:, :, 0, :, None] + sv[:, :, 1, None, :]).reshape(N, PEER_HEADS, PEER_TOPK * PEER_TOPK)
    cidx = (si[:, :, 0, :, None] * PEER_NKEYS + si[:, :, 1, None, :]).reshape(N, PEER_HEADS, PEER_TOPK * PEER_TOPK)
    top, pos = lax.top_k(cand, PEER_TOPK)
    nb = N // PEER_TOKEN_BLOCK
    idx = jnp.take_along_axis(cidx, pos, axis=-1).reshape(nb, PEER_TOKEN_BLOCK, PEER_HEADS * PEER_TOPK)
    gate = jax.nn.softmax(top, axis=-1).reshape(nb, PEER_TOKEN_BLOCK, PEER_HEADS * PEER_TOPK)
    xb = xf.reshape(nb, PEER_TOKEN_BLOCK, Dm)

    def one_block(args):
        xx, ii, gg = args
        act = jax.nn.gelu(jnp.einsum('tkd,td->tk', jnp.take(U, ii, axis=0), xx).astype(jnp.float32))
        return jnp.einsum('tk,tkd->td', gg * act, jnp.take(V, ii, axis=0))

    return lax.map(one_block, (xb, idx, gate)).reshape(B, T, Dm)


def setup_inputs(seed: int = 0) -> dict:
    key = jax.random.key(seed)
    ks = iter(jax.random.split(key, 48))
    nrm = lambda shape, scale=1.0: scale * jax.random.normal(next(ks), shape, jnp.float32)
    gain = lambda shape: 1.0 + nrm(shape, 0.02)
    f_bias = jnp.linspace(3.0, 6.0, 2 * MLSTM_HEADS, dtype=jnp.float32)
    b_gate_even = jnp.concatenate(
        [nrm((N_EVEN, 2 * MLSTM_HEADS), 0.1), f_bias + nrm((N_EVEN, 2 * MLSTM_HEADS), 0.1)], axis=-1)
    x_prompt = nrm((BATCH, SEQ, D_MODEL))
    x_sample = nrm((DEC_BATCH, DEC_SEQ, D_MODEL))
    state_mlstm_C = nrm((DEC_BATCH, N_EVEN, 2, MLSTM_HEADS, MLSTM_DH, MLSTM_DH), 0.1)
    state_mlstm_n = nrm((DEC_BATCH, N_EVEN, 2, MLSTM_HEADS, MLSTM_DH), 0.1)
    state_mlstm_m = jax.random.uniform(next(ks), (DEC_BATCH, N_EVEN, 2, MLSTM_HEADS), jnp.float32, 0.0, 4.0)
    cache_da_k = nrm((DEC_BATCH, N_ODD, DA_HEADS, PAST_LEN, 2 * DA_DH))
    cache_da_v = nrm((DEC_BATCH, N_ODD, DA_HEADS, PAST_LEN, DA_DV))
    return {
        'x_prompt': x_prompt,
        'x_sample': x_sample,
        'state_mlstm_C': state_mlstm_C,
        'state_mlstm_n': state_mlstm_n,
        'state_mlstm_m': state_mlstm_m,
        'cache_da_k': cache_da_k,
        'cache_da_v': cache_da_v,
        'c': nrm((DEC_BATCH, D_MODEL)),
        'c_ctx': nrm((D_MODEL,)),
        'norm_mix': gain((DEPTH, D_MODEL)),
        'norm_ffn': gain((DEPTH, D_MODEL)),
        'w_mod': nrm((DEPTH, D_MODEL, N_MOD * D_MODEL), D_MODEL ** -0.5),
        'b_mod': nrm((DEPTH, N_MOD * D_MODEL), 0.02),
        'w_in_even': nrm((N_EVEN, D_MODEL, EVEN_IN), D_MODEL ** -0.5),
        'b_gate_even': b_gate_even,
        'mlstm_gain': gain((N_EVEN, MLSTM_W)),
        'pool_w': nrm((N_EVEN, POOL_GROUPS, POOL_GW, POOL_GW), POOL_GW ** -0.5),
        'pool_scale': gain((N_EVEN, POOL_W)),
        'w_out_even': nrm((N_EVEN, EVEN_OUT, D_MODEL), EVEN_OUT ** -0.5),
        'w_in_odd': nrm((N_ODD, D_MODEL, ODD_IN), D_MODEL ** -0.5),
        'qk_gain': gain((N_ODD, 2, DA_DH)),
        'da_lambda': nrm((N_ODD, 4, DA_DH), 0.1),
        'da_subln': gain((N_ODD, DA_DV)),
        'gm_ws': nrm((N_ODD, GM_GROUPS, GM_CHUNK, GM_CHUNK), GM_CHUNK ** -0.5),
        'gm_b': gain((N_ODD, GM_GROUPS, GM_CHUNK)),
        'w_out_odd': nrm((N_ODD, ODD_OUT, D_MODEL), ODD_OUT ** -0.5),
        'peer_wq': nrm((DEPTH, D_MODEL, PEER_HEADS * PEER_DQ), D_MODEL ** -0.5),
        'peer_subkeys': nrm((DEPTH, PEER_HEADS, 2, PEER_NKEYS, PEER_HALF), PEER_HALF ** -0.5),
        'peer_u': nrm((DEPTH, PEER_EXPERTS, D_MODEL), D_MODEL ** -0.5),
        'peer_v': nrm((DEPTH, PEER_EXPERTS, D_MODEL), (PEER_HEADS * PEER_TOPK) ** -0.5),
    }


def reference(x_prompt, x_sample, state_mlstm_C, state_mlstm_n, state_mlstm_m, cache_da_k, cache_da_v,
              c, c_ctx, norm_mix, norm_ffn, w_mod, b_mod, w_in_even, b_gate_even, mlstm_gain, pool_w,
              pool_scale, w_out_even, w_in_odd, qk_gain, da_lambda, da_subln, gm_ws, gm_b, w_out_odd,
              peer_wq, peer_subkeys, peer_u, peer_v):
    B, S = x_prompt.shape[:2]
    Bd, T = x_sample.shape[:2]
    P = cache_da_k.shape[3]
    rope = axial_rope_angles(T)
    hc, hl = x_prompt, x_sample
    new_C, new_n, new_m, new_k, new_v = [], [], [], [], []
    for l in range(DEPTH):
        j = l // 2
        mod_c = adaln(c_ctx[None], w_mod[l], b_mod[l])
        mod_l = adaln(c, w_mod[l], b_mod[l])
        xc = modulate(hc, norm_mix[l], mod_c, 0)
        xl = modulate(hl, norm_mix[l], mod_l, 0)
        if l % 2 == 0:
            zC = jnp.zeros((B, 2, MLSTM_HEADS, MLSTM_DH, MLSTM_DH), jnp.float32)
            zn = jnp.zeros((B, 2, MLSTM_HEADS, MLSTM_DH), jnp.float32)
            zm = jnp.zeros((B, 2, MLSTM_HEADS), jnp.float32)
            oc, C_c, n_c, m_c = even_mixer(xc, w_in_even[j], b_gate_even[j], mlstm_gain[j], pool_w[j],
                                           pool_scale[j], w_out_even[j], zC, zn, zm)
            ol, _, _, _ = even_mixer(xl, w_in_even[j], b_gate_even[j], mlstm_gain[j], pool_w[j],
                                     pool_scale[j], w_out_even[j], state_mlstm_C[:, j],
                                     state_mlstm_n[:, j], state_mlstm_m[:, j])
            new_C.append(C_c)
            new_n.append(n_c)
            new_m.append(m_c)
        else:
            lam_init = 0.8 - 0.6 * math.exp(-0.3 * l)
            lam = diff_lambda(da_lambda[j], lam_init)
            qc, kc, vc, uc, gvc = odd_project(xc, w_in_odd[j], qk_gain[j], None)
            ac = diff_attention(qc, kc, vc, lam, lam_init, da_subln[j])
            oc = odd_finish(ac, uc, gvc, gm_ws[j], gm_b[j], w_out_odd[j])
            new_k.append(kc.reshape(B, DA_HEADS, S, 2 * DA_DH))
            new_v.append(vc)
            ql, kl, vl, ul, gvl = odd_project(xl, w_in_odd[j], qk_gain[j], rope)
            k_all = jnp.concatenate([kl, cache_da_k[:, j].reshape(Bd, DA_HEADS, P, 2, DA_DH)], axis=2)
            v_all = jnp.concatenate([vl, cache_da_v[:, j]], axis=2)
            al = diff_attention(ql, k_all, v_all, lam, lam_init, da_subln[j])
            ol = odd_finish(al, ul, gvl, gm_ws[j], gm_b[j], w_out_odd[j])
        hc = hc + mod_c[:, None, 2] * oc
        hl = hl + mod_l[:, None, 2] * ol
        xc = modulate(hc, norm_ffn[l], mod_c, 3)
        xl = modulate(hl, norm_ffn[l], mod_l, 3)
        hc = hc + mod_c[:, None, 5] * peer(xc, peer_wq[l], peer_subkeys[l], peer_u[l], peer_v[l])
        hl = hl + mod_l[:, None, 5] * peer(xl, peer_wq[l], peer_subkeys[l], peer_u[l], peer_v[l])
    return (hc, hl, jnp.stack(new_C, axis=1), jnp.stack(new_n, axis=1), jnp.stack(new_m, axis=1),
            jnp.stack(new_k, axis=1), jnp.stack(new_v, axis=1))
```

```python
import math
from contextlib import ExitStack
import numpy as np
import concourse.bass as bass
import concourse.mybir as mybir
from concourse.bass_utils import run_bass_kernel_spmd

F32 = mybir.dt.float32
BF16 = mybir.dt.bfloat16
AF = mybir.ActivationFunctionType
ALU = mybir.AluOpType
AX = mybir.AxisListType

D = 2048
NT = 16
TOK = 2048
EPS = 1e-6
NEG = -1.0e30
RD = 8


class Buf:
    __slots__ = ("w", "r", "name")

    def __init__(self, name=""):
        self.w = {}
        self.r = {}
        self.name = name


class Tile:
    def __init__(self, t, name, nbuf=1):
        self.t = t
        self.b = Buf(name)
        self.bs = [Buf(name + str(i)) for i in range(nbuf)] if nbuf > 1 else None

    def __getitem__(self, k):
        return self.t[k]


class KB:
    def __init__(self):
        self.nc = bass.Bass("TRN2", target_bir_lowering=False)
        nc = self.nc
        self.es = ExitStack()
        self.eng = {"pe": nc.tensor, "act": nc.scalar, "dve": nc.vector, "pool": nc.gpsimd, "sp": nc.sync}
        self.sems = {}
        self.cnt = {}
        for e in ["pe", "act", "dve", "pool"]:
            self.sems["c_" + e] = self.es.enter_context(nc.semaphore("c_" + e))
            self.cnt["c_" + e] = 0
        self.dq = ["sp", "pool", "act"]
        self.dcnt = {q: 0 for q in self.dq}
        for q in self.dq:
            for i in range(RD):
                k = "d_%s_%d" % (q, i)
                self.sems[k] = self.es.enter_context(nc.semaphore(k))
                self.cnt[k] = 0
        self.waited = {e: {} for e in self.eng}
        self.nins = 0

    def _nid(self):
        self._n = getattr(self, "_n", 0) + 1
        return self._n

    def sb(self, name, shape, dt=F32, es=None, nbuf=1):
        t = (es or self.es).enter_context(self.nc.sbuf_tensor("sb%d_%s" % (self._nid(), name), list(shape), dt))
        return Tile(t, name, nbuf)

    def ps(self, name, shape, dt=F32, es=None):
        t = (es or self.es).enter_context(self.nc.psum_tensor("pp%d_%s" % (self._nid(), name), list(shape), dt))
        return Tile(t, name)

    def dram(self, name, shape, dt=F32, kind="Internal", nbuf=1):
        t = self.nc.dram_tensor(name if kind != "Internal" else "dr%d_%s" % (self._nid(), name), list(shape), dt, kind=kind).ap()
        return Tile(t, name, nbuf)

    def _wait(self, e, key, val):
        if val <= 0:
            return
        if e == "pe" and key == "c_pe":
            return
        if self.waited[e].get(key, 0) >= val:
            return
        self.eng[e].wait_ge(self.sems[key], val)
        self.waited[e][key] = val

    def _deps(self, e, R, W):
        for b in R:
            for k, v in b.w.items():
                self._wait(e, k, v)
        own = "c_" + e
        for b in W:
            for k, v in b.w.items():
                if k != own:
                    self._wait(e, k, v)
            for k, v in b.r.items():
                self._wait(e, k, v)

    def _mark(self, key, val, R, W):
        for b in R:
            if b.r.get(key, 0) < val:
                b.r[key] = val
        for b in W:
            b.w = {key: val}
            b.r = {}

    @staticmethod
    def _bufs(xs):
        out = []
        for x in xs:
            if isinstance(x, Tile):
                out.append(x.b)
            elif x is not None:
                out.append(x)
        return out

    def op(self, e, fn, R=(), W=()):
        R = self._bufs(R)
        W = self._bufs(W)
        self._deps(e, R, W)
        key = "c_" + e
        self.cnt[key] += 1
        fn(self.eng[e]).then_inc(self.sems[key], 1)
        self._mark(key, self.cnt[key], R, W)
        self.nins += 1

    def dma(self, q, out, in_, R=(), W=(), **kw):
        if q == "act":
            q = "sp"
        R = self._bufs(R)
        W = self._bufs(W)
        self._deps(q, R, W)
        n = self.dcnt[q]
        slot = n % RD
        val = 16 * (n // RD + 1)
        key = "d_%s_%d" % (q, slot)
        self._wait(q, key, val - 16)
        self.dcnt[q] += 1
        self.cnt[key] = val
        self.eng[q].dma_start(out=out, in_=in_, **kw).then_inc(self.sems[key], 16)
        self._mark(key, val, R, W)
        self.nins += 1

    def barrier(self, engines=None):
        for e in (engines or list(self.eng)):
            for k, v in self.cnt.items():
                self._wait(e, k, v)

    def finish(self):
        self.barrier(["sp"])
        self.es.close()


W_SPEC = [
    ("norm_mix", (2, D)), ("norm_ffn", (2, D)), ("w_mod", (2, D, 6 * D)), ("b_mod", (2, 6 * D)),
    ("w_in_even", (1, D, 5136)), ("b_gate_even", (1, 16)), ("mlstm_gain", (1, 1024)),
    ("pool_w", (1, 4, 256, 256)), ("pool_scale", (1, 1024)), ("w_out_even", (1, D, D)),
    ("w_in_odd", (1, D, 5120)), ("qk_gain", (1, 2, 64)), ("da_lambda", (1, 4, 64)), ("da_subln", (1, 128)),
    ("gm_ws", (1, 8, 128, 128)), ("gm_b", (1, 8, 128)), ("w_out_odd", (1, D, D)),
    ("peer_wq", (2, D, D)), ("peer_subkeys", (2, 8, 2, 128, 128)), ("peer_u", (2, 16384, D)),
    ("peer_v", (2, 16384, D)),
]
C_SPEC = [
    ("x", (TOK, D)), ("cond_pc", (128, 16)), ("consts", (128, 8, 128)),
    ("keep", (128, 2, 16)), ("C0", (2, 4, 256, 256)), ("n0", (2, 4, 256)), ("m0", (128, 8)),
    ("amask", (128, 16, 20)), ("ropec", (TOK, 64)), ("ropes", (TOK, 64)),
    ("cache_k", (8, 512, 128)), ("cache_v", (8, 512, 128)), ("poolm", (16, 3, 4, 128, 128)),
]
O_SPEC = [
    ("y", (TOK, D)), ("stC", (8, 2, 4, 256, 256)), ("stn", (8, 2, 4, 256)), ("stm", (8, 2, 4)),
    ("ck", (16, 8, 128, 128)), ("cv", (16, 8, 128, 128)),
]


class LazyIn(dict):
    def __init__(self, k):
        super().__init__()
        self.k = k
        self.shapes = dict(W_SPEC + C_SPEC)

    def __missing__(self, name):
        t = self.k.dram(name, self.shapes[name], F32, kind="ExternalInput")
        self[name] = t
        return t


class Prog:
    def __init__(self, stages=("all",), dbg=()):
        self.k = KB()
        k = self.k
        self.stages = stages
        self.I = LazyIn(k)
        self.O = {}
        for name, shp in O_SPEC:
            self.O[name] = k.dram(name, shp, F32, kind="ExternalOutput", nbuf=16)
        self.dbg = {}
        for name, shp, dt in dbg:
            self.dbg[name] = k.dram("dbg_" + name, shp, dt, kind="ExternalOutput", nbuf=16)
        self.h = [self.I["x"]] + [k.dram("h%d" % i, (TOK, D), F32, nbuf=16) for i in range(1, 4)] + [self.O["y"]]
        self.h[0].bs = [Buf("x%d" % i) for i in range(16)]
        self.setup_consts()

    def setup_consts(self):
        k = self.k
        self.cst = k.sb("cst", (128, 8, 128), F32)
        k.dma("sp", self.cst[:], self.I["consts"][:, :, :], W=[self.cst])
        c = self.cst
        self.ident = c[:, 0, :]
        self.ones = c[:, 1, :]
        self.triF = c[:, 2, :]
        self.triB = c[:, 3, :]
        self.maskF = c[:, 4, :]
        self.maskB = c[:, 5, :]
        self.cstb = k.sb("cstb", (128, 2, 128), BF16)
        k.op("dve", lambda e: e.tensor_copy(out=self.cstb[:], in_=c[:, 0:2, :]), R=[c], W=[self.cstb])
        self.identb = self.cstb[:, 0, :]
        self.onesb = self.cstb[:, 1, :]
        self.actT = None
        self._act_es = None
        self.skT = k.sb("skT", (128, 16, 128), F32)
        self.psb = [k.ps("psb%d" % i, (128, 512), F32) for i in range(8)]
        self.condsb = k.sb("condsb", (128, 16), F32)
        k.dma("sp", self.condsb[:], self.I["cond_pc"][:, :], W=[self.condsb])
        self.sc = k.sb("sc", (128, 16), F32)
        k.op("act", lambda e: e.activation(out=self.sc[:], in_=self.condsb[:], func=AF.Silu),
             R=[self.condsb], W=[self.sc])
        self.gate = [k.sb("gateM", (128, D), F32), k.sb("gateF", (128, D), F32)]
        self.acol = [k.sb("acolM", (128, 16), F32), k.sb("acolF", (128, 16), F32)]
        self.bcol = [k.sb("bcolM", (128, 16), F32), k.sb("bcolF", (128, 16), F32)]

    def alloc_act(self):
        self._act_es = ExitStack()
        self.actT = self.k.sb("actT", (128, 16, TOK), BF16, es=self._act_es, nbuf=16)

    def free_act(self):
        self.k.barrier()
        self._act_es.close()
        self.actT = None

    def rstd(self, s, inv_n, eps=EPS):
        k = self.k
        k.op("dve", lambda e: e.tensor_scalar(out=s[:], in0=s[:], scalar1=inv_n, scalar2=eps,
                                              op0=ALU.mult, op1=ALU.add), R=[s], W=[s])
        k.op("act", lambda e: e.activation(out=s[:], in_=s[:], func=AF.Sqrt), R=[s], W=[s])
        k.op("dve", lambda e: e.reciprocal(out=s[:], in_=s[:]), R=[s], W=[s])

    def phase_mod(self, l):
        k = self.k
        with ExitStack() as es:
            wb = [k.sb("mw%d" % i, (128, 16, 512), F32, es=es) for i in range(2)]
            bb = [k.sb("mb%d" % i, (128, 512), F32, es=es) for i in range(2)]
            self.screp = k.sb("screp", (128, 16, 128), F32, es=es)
            for c_ in range(16):
                k.op("dve", lambda e, c_=c_: e.tensor_copy(out=self.screp[:, c_, :],
                                                          in_=self.sc[:, c_:c_ + 1].to_broadcast([128, 128])),
                     R=[self.sc], W=[self.screp])
            srow = k.sb("srow", (128, D), F32, es=es)
            arow = k.sb("arow", (128, D), F32, es=es)
            grow = k.sb("grow", (128, D), F32, es=es)
            wm = self.I["w_mod"].t[l].rearrange("(c p) n -> p c n", p=128)
            bm = self.I["b_mod"].t[l]
            blk = 0
            for sub in range(2):
                gsrc = self.I["norm_mix" if sub == 0 else "norm_ffn"].t[l]
                k.dma("sp", grow[:], gsrc.partition_broadcast(128), W=[grow])
                for i in range(3):
                    for j in range(4):
                        n0 = (sub * 3 + i) * D + j * 512
                        w = wb[blk % 2]
                        b = bb[blk % 2]
                        k.dma("sp", w[:, 0:8, :], wm[:, 0:8, n0:n0 + 512], W=[w])
                        k.dma("act", w[:, 8:16, :], wm[:, 8:16, n0:n0 + 512], W=[w])
                        k.dma("sp", b[:], bm[n0:n0 + 512].partition_broadcast(128), W=[b])
                        ps = self.psb[blk % 2]
                        for c in range(16):
                            k.op("pe", lambda e, c=c, ps=ps, w=w: e.matmul(ps[:], lhsT=self.screp[:, c, :], rhs=w[:, c, :],
                                                                        start=(c == 0), stop=(c == 15)),
                                 R=[self.screp, w], W=[ps])
                        dst = [srow, arow, self.gate[sub]][i]
                        k.op("dve", lambda e, ps=ps, b=b, dst=dst, j=j: e.tensor_tensor(
                            out=dst[:, j * 512:(j + 1) * 512], in0=ps[:], in1=b[:], op=ALU.add),
                            R=[ps, b], W=[dst])
                        blk += 1
                k.op("dve", lambda e: e.scalar_tensor_tensor(out=arow[:], in0=arow[:], scalar=1.0, in1=grow[:],
                                                             op0=ALU.add, op1=ALU.mult), R=[arow, grow], W=[arow])
                for src, dst in ((arow, self.acol[sub]), (srow, self.bcol[sub])):
                    for c in range(16):
                        ps = self.psb[2 + (c % 2)]
                        k.op("pe", lambda e, ps=ps, src=src, c=c: e.transpose(ps[:, 0:128], src[:, c * 128:(c + 1) * 128],
                                                                          self.ident), R=[src, self.cst], W=[ps])
                        k.op("dve", lambda e, ps=ps, dst=dst, c=c: e.tensor_copy(out=dst[:, c:c + 1], in_=ps[:, 0:1]),
                             R=[ps], W=[dst])
            k.barrier()

    def phase_norm(self, hsrc, sub):
        k = self.k
        with ExitStack() as es:
            ht = [k.sb("nh%d" % i, (128, D), F32, es=es) for i in range(2)]
            junk = k.sb("njunk", (128, D), F32, es=es)
            hs = [k.sb("nhs%d" % i, (128, D), BF16, es=es) for i in range(2)]
            ss = [k.sb("nss%d" % i, (128, 1), F32, es=es) for i in range(2)]
            pst = [k.ps("npst%d" % i, (128, 1024), BF16, es=es) for i in range(2)] if False else None
            for T in range(NT):
                h = ht[T % 2]
                s = ss[T % 2]
                hb = hs[T % 2]
                k.dma("sp", h[:, 0:1024], hsrc.t[T * 128:(T + 1) * 128, 0:1024], R=[hsrc.bs[T]], W=[h])
                k.dma("act", h[:, 1024:2048], hsrc.t[T * 128:(T + 1) * 128, 1024:2048], R=[hsrc.bs[T]], W=[h])
                k.op("act", lambda e, h=h, s=s: e.activation(out=junk[:], in_=h[:], func=AF.Square, accum_out=s[:]),
                     R=[h], W=[junk, s])
                self.rstd(s, 1.0 / D)
                k.op("dve", lambda e, h=h, s=s, hb=hb: e.tensor_scalar(out=hb[:], in0=h[:], scalar1=s[:, 0:1], scalar2=None,
                                                                  op0=ALU.mult), R=[h, s], W=[hb])
                for c in range(16):
                    ps = self.psb[c % 4]
                    psv = ps.t[:].bitcast(BF16)
                    k.op("pe", lambda e, psv=psv, hb=hb, c=c: e.transpose(psv[:, 0:128], hb[:, c * 128:(c + 1) * 128],
                                                                      self.identb), R=[hb, self.cstb], W=[ps])
                    eng = "act" if c % 2 == 0 else "dve"
                    dst = self.actT.t[:, c, T * 128:(T + 1) * 128]
                    if eng == "act":
                        k.op("act", lambda e, psv=psv, dst=dst, c=c: e.activation(
                            out=dst, in_=psv[:, 0:128], func=AF.Identity, scale=self.acol[sub][:, c:c + 1],
                            bias=self.bcol[sub][:, c:c + 1]), R=[ps, self.acol[sub], self.bcol[sub]], W=[self.actT.bs[T]])
                    else:
                        k.op("dve", lambda e, psv=psv, dst=dst, c=c: e.tensor_scalar(
                            out=dst, in0=psv[:, 0:128], scalar1=self.acol[sub][:, c:c + 1],
                            scalar2=self.bcol[sub][:, c:c + 1], op0=ALU.mult, op1=ALU.add),
                            R=[ps, self.acol[sub], self.bcol[sub]], W=[self.actT.bs[T]])
            k.barrier()

    def proj(self, wsrc, ncols, col_plan, es_outer=None):
        k = self.k
        with ExitStack() as es:
            wb = [k.sb("pw%d" % i, (128, 16, 512), BF16, es=es) for i in range(2)]
            wv = wsrc.rearrange("(c p) n -> p c n", p=128)
            bi = 0
            pi = 0
            for (c0, width, mode, sink) in col_plan:
                w = wb[bi % 2]
                bi += 1
                for cc in range(0, 16, 4):
                    k.dma("pool", w[:, cc:cc + 4, 0:width], wv[:, cc:cc + 4, c0:c0 + width], W=[w])
                if mode == "tok":
                    for T in range(NT):
                        ps = self.psb[4 + pi % 4]
                        pi += 1
                        for c in range(16):
                            k.op("pe", lambda e, ps=ps, w=w, c=c, T=T: e.matmul(
                                ps[:, 0:width], lhsT=self.actT.t[:, c, T * 128:(T + 1) * 128], rhs=w[:, c, 0:width],
                                start=(c == 0), stop=(c == 15)), R=[self.actT.bs[T], w], W=[ps])
                        sink(T, ps, width, c0)
                else:
                    for m in range(width // 128):
                        for tg in range(4):
                            ps = self.psb[4 + pi % 4]
                            pi += 1
                            for c in range(16):
                                k.op("pe", lambda e, ps=ps, w=w, c=c, m=m, tg=tg: e.matmul(
                                    ps[:, :], lhsT=w[:, c, m * 128:(m + 1) * 128],
                                    rhs=self.actT.t[:, c, tg * 512:(tg + 1) * 512],
                                    start=(c == 0), stop=(c == 15)),
                                    R=[self.actT.bs[tg * 4 + i] for i in range(4)] + [w], W=[ps])
                            sink(m, tg, ps, c0)
            k.barrier()


def build_program(stages=("all",), dbg=()):
    p = Prog(stages, dbg)
    return p


def make_consts():
    c = np.zeros((128, 8, 128), np.float32)
    i = np.arange(128)
    c[:, 0, :] = np.eye(128)
    c[:, 1, :] = 1.0
    c[:, 2, :] = (i[:, None] <= i[None, :])
    c[:, 3, :] = (i[:, None] >= i[None, :])
    c[:, 4, :] = np.where(i[None, :] <= i[:, None], 0.0, NEG)
    c[:, 5, :] = np.where(i[None, :] >= i[:, None], 0.0, NEG)
    return c


def core_inputs(inp, core):
    prompt = core < 4
    m = {}
    f32 = np.float32
    if prompt:
        m["x"] = np.ascontiguousarray(inp["x_prompt"][8 * core:8 * core + 8].reshape(TOK, D))
        cond = inp["c_ctx"]
        L = 256
    else:
        b = core - 4
        m["x"] = np.ascontiguousarray(inp["x_sample"][b])
        cond = inp["c"][b]
        L = 2048
    m["cond_pc"] = np.ascontiguousarray(np.asarray(cond).reshape(16, 128).T)
    m["consts"] = make_consts()
    keep = np.ones((128, 2, 16), f32)
    T = np.arange(16)
    if prompt:
        keep[:, 0, :] = (T % 2 == 0)
        keep[:, 1, :] = (T % 2 == 1)
        m["C0"] = np.zeros((2, 4, 256, 256), f32)
        m["n0"] = np.zeros((2, 4, 256), f32)
        m["m0"] = np.zeros((128, 8), f32)
        m["cache_k"] = np.zeros((8, 512, 128), f32)
        m["cache_v"] = np.zeros((8, 512, 128), f32)
        am = np.full((16, 20), -30000.0, f32)
        for t in range(16):
            am[t, (t // 2) * 2:(t // 2) * 2 + 2] = 0.0
        m["ropec"] = np.ones((TOK, 64), f32)
        m["ropes"] = np.zeros((TOK, 64), f32)
    else:
        m["C0"] = np.ascontiguousarray(inp["state_mlstm_C"][b, 0])
        m["n0"] = np.ascontiguousarray(inp["state_mlstm_n"][b, 0])
        m["m0"] = np.ascontiguousarray(np.broadcast_to(np.asarray(inp["state_mlstm_m"][b, 0]).reshape(1, 8), (128, 8)))
        m["cache_k"] = np.ascontiguousarray(inp["cache_da_k"][b, 0])
        m["cache_v"] = np.ascontiguousarray(inp["cache_da_v"][b, 0])
        am = np.zeros((16, 20), f32)
        t = np.arange(TOK)
        row = (t // 64).astype(f32)
        col = (t % 64).astype(f32)
        inv = (np.float32(10000.0) ** (-np.arange(16, dtype=f32) / np.float32(16))).astype(f32)
        ar = (row[:, None] * inv).astype(f32)
        ac = (col[:, None] * inv).astype(f32)
        m["ropec"] = np.concatenate([np.cos(ar), np.cos(ar), np.cos(ac), np.cos(ac)], 1).astype(f32)
        m["ropes"] = np.concatenate([-np.sin(ar), np.sin(ar), -np.sin(ac), np.sin(ac)], 1).astype(f32)
    m["keep"] = keep
    m["amask"] = np.ascontiguousarray(np.broadcast_to(am[None], (128, 16, 20)))
    m["poolm"] = make_poolm(L)
    return m


_POOLM = {}


def make_poolm(L):
    if L in _POOLM:
        return _POOLM[L]
    pm = np.zeros((16, 3, 4, 128, 128), np.float32)
    pos = np.arange(TOK)
    seq0 = (pos // L) * L
    for g, w in enumerate((2, 4, 8, 16)):
        p = pos - seq0
        lo = np.clip(p - w // 2, 0, L) + seq0
        hi = np.clip(p - w // 2 + w, 0, L) + seq0
        cnt = (hi - lo).astype(np.float32)
        A = np.zeros((TOK, TOK), np.float32)
        for t in range(TOK):
            A[lo[t]:hi[t], t] = np.float32(1.0) / cnt[t]
            A[t, t] -= 1.0
        for T in range(16):
            for r in range(3):
                Tn = T + r - 1
                if 0 <= Tn < 16:
                    pm[T, r, g] = A[Tn * 128:(Tn + 1) * 128, T * 128:(T + 1) * 128]
    _POOLM[L] = pm
    return pm


_PROG = {}


def kernel(**inputs):
    inp = {k_: np.asarray(v) for k_, v in inputs.items()}
    if "p" not in _PROG:
        _PROG["p"] = build_full()
    p = _PROG["p"]
    wnames = [n for n, _ in W_SPEC]
    in_maps = []
    for core in range(8):
        m = core_inputs(inp, core)
        for n in wnames:
            m[n] = inp[n]
        in_maps.append({n: np.ascontiguousarray(m[n], dtype=np.float32) for n in p.I})
    res = run_bass_kernel_spmd(p.k.nc, in_maps, core_ids=list(range(8)))
    r = res.results
    y_prompt = np.concatenate([r[c]["y"].reshape(8, 256, D) for c in range(4)], 0)
    y_sample = np.stack([r[c]["y"] for c in range(4, 8)], 0)
    nC = np.concatenate([r[c]["stC"] for c in range(4)], 0)[:, None]
    nn = np.concatenate([r[c]["stn"] for c in range(4)], 0)[:, None]
    nm = np.concatenate([r[c]["stm"] for c in range(4)], 0)[:, None]

    def cache(name):
        out = []
        for c in range(4):
            a = r[c][name].reshape(8, 2, 8, 128, 128).transpose(0, 2, 1, 3, 4).reshape(8, 8, 256, 128)
            out.append(a)
        return np.concatenate(out, 0)[:, None]
    f = lambda a: np.ascontiguousarray(a, dtype=np.float32)
    return (f(y_prompt), f(y_sample), f(nC), f(nn), f(nm), f(cache("ck")), f(cache("cv")))


def _col_from_row(self, row_ap, n, dst, es):
    k = self.k
    tmp = k.sb("cfr_%d" % self._uid(), (128, n * 128), F32, es=es)
    k.dma("sp", tmp[:], row_ap.partition_broadcast(128), W=[tmp])
    for c in range(n):
        ps = self.psb[c % 2]
        k.op("pe", lambda e, ps=ps, c=c: e.transpose(ps[:, 0:128], tmp[:, c * 128:(c + 1) * 128], self.ident),
             R=[tmp, self.cst], W=[ps])
        k.op("dve", lambda e, ps=ps, c=c: e.tensor_copy(out=dst[:, c:c + 1], in_=ps[:, 0:1]), R=[ps], W=[dst])


def _uid(self):
    self._u = getattr(self, "_u", 0) + 1
    return self._u


def _residual_sink(self, hold, hnew, sub, es):
    k = self.k
    hb = [k.sb("rs_h%d_%d" % (i, self._uid()), (128, 512), F32, es=es) for i in range(2)]
    tb = [k.sb("rs_t%d_%d" % (i, self._uid()), (128, 512), F32, es=es) for i in range(2)]
    cnt = [0]

    def sink(T, ps, width, c0):
        i = cnt[0] % 2
        cnt[0] += 1
        h, t = hb[i], tb[i]
        k.dma("sp", h[:, 0:width], hold.t[T * 128:(T + 1) * 128, c0:c0 + width], R=[hold.bs[T]], W=[h])
        k.op("dve", lambda e: e.tensor_tensor(out=t[:, 0:width], in0=ps[:, 0:width],
                                              in1=self.gate[sub][:, c0:c0 + width], op=ALU.mult),
             R=[ps, self.gate[sub]], W=[t])
        k.op("pool", lambda e: e.tensor_tensor(out=t[:, 0:width], in0=t[:, 0:width], in1=h[:, 0:width], op=ALU.add),
             R=[t, h], W=[t])
        k.dma("sp", hnew.t[T * 128:(T + 1) * 128, c0:c0 + width], t[:, 0:width], R=[t], W=[hnew.bs[T]])
    return sink


def _phase_outproj(self, wsrc, hold, hnew, sub):
    with ExitStack() as es:
        sink = self._residual_sink(hold, hnew, sub, es)
        self.proj(wsrc, D, [(j * 512, 512, "tok", sink) for j in range(4)])


Prog._col_from_row = _col_from_row
Prog._uid = _uid
Prog._residual_sink = _residual_sink
Prog.phase_outproj = _phase_outproj


def _top16(self, src_ap, dst16, scr, R, es_bufs):
    k = self.k
    k.op("dve", lambda e: e.max(out=dst16[:, 0:8], in_=src_ap), R=R, W=[dst16])
    k.op("dve", lambda e: e.match_replace(out=scr[:], in_to_replace=dst16[:, 0:8], in_values=src_ap, imm_value=-1.0),
         R=R + [dst16], W=[scr])
    k.op("dve", lambda e: e.max(out=dst16[:, 8:16], in_=scr[:]), R=[scr], W=[dst16])


Prog._top16 = _top16


def _phase_peer_prep(self, l):
    k = self.k
    if not hasattr(self, "GS"):
        self.GS = k.dram("GS", (128, 128, TOK), BF16, nbuf=128)
        self.VB = k.dram("VB", (128, 128, D), BF16, nbuf=128)
        self.qTs = k.dram("qTs", (16, 128, TOK), F32, nbuf=16)
    U = self.I["peer_u"].t[l].rearrange("(i j) d -> i j d", j=128)
    V = self.I["peer_v"].t[l].rearrange("(i j) d -> i j d", j=128)
    P = self.psb
    with ExitStack() as es:
        ub = [k.sb("pu%d" % i, (128, D), BF16, es=es) for i in range(2)]
        vb = [k.sb("pv%d" % i, (128, D), BF16, es=es) for i in range(2)]
        ut = [k.sb("put%d" % i, (128, 16, 128), BF16, es=es) for i in range(2)]
        gsb = [k.sb("pgs%d" % i, (128, TOK), BF16, es=es) for i in range(2)]
        sk = k.sb("psk", (128, 128), F32, es=es)
        for m in range(16):
            k.dma("sp", sk[:], self.I["peer_subkeys"].t[l, m // 2, m % 2], W=[sk])
            ps = P[m % 2]
            k.op("pe", lambda e, ps=ps: e.transpose(ps[:, 0:128], sk[:], self.ident), R=[sk, self.cst], W=[ps])
            k.op("act", lambda e, ps=ps, m=m: e.copy(out=self.skT[:, m, :], in_=ps[:, 0:128]), R=[ps], W=[self.skT])
        nh = 0
        for i in range(128):
            u, v, t, g = ub[i % 2], vb[i % 2], ut[i % 2], gsb[i % 2]
            k.dma("pool", u[:], U[i], W=[u])
            k.dma("pool", v[:], V[i], W=[v])
            k.dma("act", self.VB.t[i], v[:], R=[v], W=[self.VB.bs[i]])
            for c in range(16):
                ps = P[c // 8]
                psv = ps.t[:].bitcast(BF16)
                k.op("pe", lambda e, psv=psv, u=u, c=c: e.transpose(psv[:, (c % 8) * 128:(c % 8 + 1) * 128],
                                                                   u[:, c * 128:(c + 1) * 128], self.identb),
                     R=[u, self.cstb], W=[ps])
                if c % 8 == 7:
                    dst = t[:, c - 7:c + 1, :]
                    src = psv[:, 0:1024].rearrange("p (c j) -> p c j", j=128)
                    if c == 7:
                        k.op("act", lambda e, src=src, dst=dst: e.copy(out=dst, in_=src), R=[ps], W=[t])
                    else:
                        k.op("dve", lambda e, src=src, dst=dst: e.tensor_copy(out=dst, in_=src), R=[ps], W=[t])
            for half in range(2):
                pa, pb2 = P[2 + 2 * (nh % 3)], P[3 + 2 * (nh % 3)]
                nh += 1
                for q4, pbank in ((0, pa), (1, pb2)):
                    tg = half * 2 + q4
                    for c in range(16):
                        k.op("pe", lambda e, c=c, tg=tg, pbank=pbank, t=t: e.matmul(
                            pbank[:, :], lhsT=t[:, c, :], rhs=self.actT.t[:, c, tg * 512:(tg + 1) * 512],
                            start=(c == 0), stop=(c == 15)),
                            R=[t] + [self.actT.bs[tg * 4 + x] for x in range(4)], W=[pbank])
                    k.op("act", lambda e, tg=tg, pbank=pbank, g=g: e.activation(
                        out=g[:, tg * 512:(tg + 1) * 512], in_=pbank[:, :], func=AF.Gelu_apprx_tanh), R=[pbank], W=[g])
            k.dma("sp", self.GS.t[i], g[:], R=[g], W=[self.GS.bs[i]])
        k.barrier()


def _phase_peer_q(self, l):
    k = self.k
    with ExitStack() as es:
        st = [k.sb("pq%d" % i, (128, 512), F32, es=es) for i in range(2)]
        cnt = [0]

        def sink(m, tg, ps, c0):
            s = st[cnt[0] % 2]
            cnt[0] += 1
            mm = c0 // 128 + m
            eng = "act" if cnt[0] % 2 else "dve"
            if eng == "act":
                k.op("act", lambda e: e.copy(out=s[:], in_=ps[:]), R=[ps], W=[s])
            else:
                k.op("dve", lambda e: e.tensor_copy(out=s[:], in_=ps[:]), R=[ps], W=[s])
            k.dma("sp", self.qTs.t[mm, :, tg * 512:(tg + 1) * 512], s[:], R=[s], W=[self.qTs.bs[mm]])
        self.proj(self.I["peer_wq"].t[l], D, [(j * 512, 512, "feat", sink) for j in range(4)])


def _phase_peer_main(self, l, hold, hnew, tiles=None):
    k = self.k
    IB = 4
    NBLK = 128 // IB
    tiles = list(tiles if tiles is not None else range(NT))
    with ExitStack() as es:
        sets = []
        for si in range(2):
            S = {}
            S["qt"] = k.sb("eq%d" % si, (128, 16, 128), F32, es=es)
            S["s_all"] = k.sb("es%d" % si, (128, 16, 128), F32, es=es)
            S["e_all"] = k.sb("ee%d" % si, (128, 16, 128), F32, es=es)
            S["mx"] = k.sb("emx%d" % si, (128, 16), F32, es=es)
            S["ev"] = k.sb("eev%d" % si, (128, 16, 16), F32, es=es)
            S["ct"] = k.sb("ect%d" % si, (128, 8, 16), F32, es=es)
            S["rz"] = k.sb("erz%d" % si, (128, 8), F32, es=es)
            S["dg"] = k.sb("edg%d" % si, (128, 8, 128), BF16, es=es)
            sets.append(S)
        scr_sh = k.sb("escr", (128, 256), F32, es=es)
        cE_sh = k.sb("ecE", (128, 8, 256), F32, es=es)
        for S in sets:
            S["scr"] = scr_sh
            S["cE"] = cE_sh
        HA = 7
        Ea = [k.sb("eEa%d" % i, (128, HA, IB, 128), F32, es=es) for i in range(2)]
        Ed = [k.sb("eEd%d" % i, (128, 8 - HA, IB, 128), F32, es=es) for i in range(2)]
        Gb = [k.sb("eG%d" % i, (128, 8, IB, 128), BF16, es=es) for i in range(2)]
        vb = [k.sb("ev%d" % i, (128, IB, D), BF16, es=es) for i in range(3)]
        ge = [k.sb("ege%d" % i, (128, IB, 128), BF16, es=es) for i in range(3)]
        at = [k.sb("eat%d" % i, (128, IB, 128), BF16, es=es) for i in range(2)]
        sink = self._residual_sink(hold, hnew, 1, es)
        psO = self.psb[0:4]
        psG = self.psb[4:6]
        psS = self.psb[6:8]

        def preamble(T, S):
            tsl = slice(T * 128, (T + 1) * 128)
            qt, s_all, e_all, scr, mx, ev, cE, ct, rz, dg = (S[n] for n in ("qt", "s_all", "e_all", "scr", "mx", "ev", "cE", "ct", "rz", "dg"))
            k.dma("sp", qt[:], self.qTs.t[:, :, tsl].rearrange("m p t -> p m t"), R=self.qTs.bs, W=[qt])
            for half in range(2):
                for mm in range(8):
                    m = half * 8 + mm
                    ps = psS[mm // 4]
                    k.op("pe", lambda e, ps=ps, m=m, mm=mm: e.matmul(ps[:, (mm % 4) * 128:(mm % 4 + 1) * 128], lhsT=qt[:, m, :],
                                                              rhs=self.skT[:, m, :], start=True, stop=True),
                         R=[qt, self.skT], W=[ps])
                for g in range(2):
                    k.op("act", lambda e, g=g, half=half: e.copy(out=s_all[:, half * 8 + g * 4:half * 8 + (g + 1) * 4, :],
                                                                 in_=psS[g][:, :].rearrange("p (m k) -> p m k", k=128)),
                         R=[psS[g]], W=[s_all])
            k.op("dve", lambda e: e.tensor_reduce(out=mx[:], in_=s_all[:], axis=AX.X, op=ALU.max), R=[s_all], W=[mx])
            k.op("dve", lambda e: e.tensor_scalar(out=mx[:], in0=mx[:], scalar1=-1.0, scalar2=None, op0=ALU.mult),
                 R=[mx], W=[mx])
            for m in range(16):
                k.op("act", lambda e, m=m: e.activation(out=e_all[:, m, :], in_=s_all[:, m, :], func=AF.Exp,
                                                        bias=mx[:, m:m + 1]), R=[s_all, mx], W=[e_all])
            for m in range(16):
                k.op("dve", lambda e, m=m: e.max(out=ev[:, m, 0:8], in_=e_all[:, m, :]), R=[e_all], W=[ev])
                k.op("dve", lambda e, m=m: e.match_replace(out=scr[:, 0:128], in_to_replace=ev[:, m, 0:8],
                                                           in_values=e_all[:, m, :], imm_value=-1.0),
                     R=[e_all, ev], W=[scr])
                k.op("dve", lambda e, m=m: e.max(out=ev[:, m, 8:16], in_=scr[:, 0:128]), R=[scr], W=[ev])
                k.op("dve", lambda e, m=m: e.scalar_tensor_tensor(out=e_all[:, m, :], in0=e_all[:, m, :],
                                                                  scalar=ev[:, m, 15:16], in1=e_all[:, m, :],
                                                                  op0=ALU.is_ge, op1=ALU.mult),
                     R=[e_all, ev], W=[e_all])
            for h in range(HA):
                for a_ in range(16):
                    k.op("act", lambda e, h=h, a_=a_: e.activation(
                        out=cE[:, h, a_ * 16:(a_ + 1) * 16], in_=ev[:, 2 * h + 1, :], func=AF.Identity,
                        scale=ev[:, 2 * h, a_:a_ + 1]), R=[ev], W=[cE])
            for h in range(8):
                if h >= HA:
                    k.op("dve", lambda e, h=h: e.tensor_tensor(
                        out=cE[:, h, :].rearrange("p (a b) -> p a b", b=16),
                        in0=ev[:, 2 * h, :].unsqueeze(2).to_broadcast([128, 16, 16]),
                        in1=ev[:, 2 * h + 1, :].unsqueeze(1).to_broadcast([128, 16, 16]), op=ALU.mult),
                        R=[ev], W=[cE])
                k.op("dve", lambda e, h=h: e.max(out=ct[:, h, 0:8], in_=cE[:, h, :]), R=[cE], W=[ct])
                k.op("dve", lambda e, h=h: e.match_replace(out=scr[:], in_to_replace=ct[:, h, 0:8], in_values=cE[:, h, :],
                                                           imm_value=-1.0), R=[cE, ct], W=[scr])
                k.op("dve", lambda e, h=h: e.max(out=ct[:, h, 8:16], in_=scr[:]), R=[scr], W=[ct])
            k.op("dve", lambda e: e.tensor_reduce(out=rz[:], in_=ct[:], axis=AX.X, op=ALU.add), R=[ct], W=[rz])
            k.op("dve", lambda e: e.reciprocal(out=rz[:], in_=rz[:]), R=[rz], W=[rz])
            for h in range(8):
                k.op("dve", lambda e, h=h: e.tensor_scalar(out=dg[:, h, :], in0=self.ident, scalar1=rz[:, h:h + 1],
                                                           scalar2=None, op0=ALU.mult), R=[rz, self.cst], W=[dg])
            if "pe_s" in self.dbg and T == 0:
                k.dma("sp", self.dbg["pe_s"].t[:, :, :], s_all[:], R=[s_all], W=[self.dbg["pe_s"]])
                k.dma("sp", self.dbg["pe_e"].t[:, :, :], e_all[:], R=[e_all], W=[self.dbg["pe_e"]])
                k.dma("sp", self.dbg["pe_ct"].t[:, :, :], ct[:], R=[ct], W=[self.dbg["pe_ct"]])
                k.dma("sp", self.dbg["pe_rz"].t[:, :], rz[:], R=[rz], W=[self.dbg["pe_rz"]])

        nb = [0]

        def front(T, S, ib):
            tsl = slice(T * 128, (T + 1) * 128)
            e_all, ct, dg = S["e_all"], S["ct"], S["dg"]
            n = nb[0]
            nb[0] += 1
            v, gt = vb[n % 3], ge[n % 3]
            pg, a_t = psG[n % 2], at[n % 2]
            EA, ED, G = Ea[n % 2], Ed[n % 2], Gb[n % 2]
            i0 = ib * IB
            e4 = e_all[:].rearrange("p (h q) k -> p h q k", q=2)
            k.dma("sp", gt[:], self.GS.t[i0:i0 + IB, :, tsl].rearrange("i j t -> j i t"),
                  R=self.GS.bs[i0:i0 + IB], W=[gt])
            k.dma("act", v[:], self.VB.t[i0:i0 + IB].rearrange("i j d -> j i d"),
                  R=self.VB.bs[i0:i0 + IB], W=[v])
            for h in range(HA):
                for ii in range(IB):
                    k.op("act", lambda e, h=h, ii=ii: e.activation(
                        out=EA[:, h, ii, :], in_=e4[:, h, 1, :], func=AF.Identity,
                        scale=e4[:, h, 0, i0 + ii:i0 + ii + 1]), R=[e_all], W=[EA])
            k.op("dve", lambda e: e.tensor_tensor(
                out=ED[:], in0=e4[:, HA:8, 0, i0:i0 + IB].unsqueeze(3).to_broadcast([128, 8 - HA, IB, 128]),
                in1=e4[:, HA:8, 1, :].unsqueeze(2).to_broadcast([128, 8 - HA, IB, 128]), op=ALU.mult),
                R=[e_all], W=[ED])
            for h in range(8):
                Eh = EA[:, h] if h < HA else ED[:, h - HA]
                Et = EA if h < HA else ED
                k.op("dve", lambda e, h=h, Eh=Eh: e.scalar_tensor_tensor(
                    out=G[:, h], in0=Eh, scalar=ct[:, h, 15:16], in1=Eh, op0=ALU.is_ge, op1=ALU.mult),
                    R=[Et, ct], W=[G])
            for ii in range(IB):
                for h in range(8):
                    k.op("pe", lambda e, h=h, ii=ii: e.matmul(
                        pg[:, ii * 128:(ii + 1) * 128], lhsT=G[:, h, ii, :], rhs=dg[:, h, :],
                        start=(h == 0), stop=(h == 7)), R=[G, dg], W=[pg])
            return (i0, v, gt, pg, a_t)

        def back(st):
            i0, v, gt, pg, a_t = st
            k.op("dve", lambda e: e.tensor_tensor(
                out=a_t[:].rearrange("p i t -> p (i t)"), in0=gt[:].rearrange("p i t -> p (i t)"),
                in1=pg[:, 0:IB * 128], op=ALU.mult), R=[gt, pg], W=[a_t])
            for ii in range(IB):
                i = i0 + ii
                for dblk in range(4):
                    k.op("pe", lambda e, ii=ii, dblk=dblk, i=i: e.matmul(
                        psO[dblk][:, :], lhsT=a_t[:, ii, :], rhs=v[:, ii, dblk * 512:(dblk + 1) * 512],
                        start=(i == 0), stop=(i == 127), skip_group_check=True), R=[a_t, v], W=[psO[dblk]])

        preamble(tiles[0], sets[0])
        for ti, T in enumerate(tiles):
            S = sets[ti % 2]
            pending = None
            for ib in range(NBLK):
                st = front(T, S, ib)
                if pending is not None:
                    back(pending)
                pending = st
                if ib == NBLK // 2 and ti + 1 < len(tiles):
                    preamble(tiles[ti + 1], sets[(ti + 1) % 2])
            back(pending)
            for dblk in range(4):
                sink(T, psO[dblk], 512, dblk * 512)
        k.barrier()


Prog.phase_peer_prep = _phase_peer_prep
Prog.phase_peer_q = _phase_peer_q
Prog.phase_peer_main = _phase_peer_main


def _phase_even(self, hold, hnew):
    k = self.k
    qTs = k.dram("e_qT", (8, 128, TOK), BF16, nbuf=8)
    kTs = k.dram("e_kT", (8, 128, TOK), BF16, nbuf=8)
    ks = k.dram("e_k", (TOK, 1024), BF16, nbuf=16)
    vs = k.dram("e_v", (TOK, 1024), BF16, nbuf=16)
    os_ = k.dram("e_o", (TOK, 1024), F32, nbuf=16)
    pps = k.dram("e_p", (TOK, 1024), F32, nbuf=16)
    gs = k.dram("e_g", (TOK, 16), F32, nbuf=16)
    hF = k.dram("e_hF", (TOK, 1024), F32, nbuf=16)
    with ExitStack() as es:
        sb16 = [k.sb("ep_b%d" % i, (128, 512), BF16, es=es) for i in range(2)]
        sf32 = [k.sb("ep_f%d" % i, (128, 512), F32, es=es) for i in range(2)]
        cnt = [0]

        def sink_feat(m, tg, ps, c0):
            s = sb16[cnt[0] % 2]
            cnt[0] += 1
            isk = c0 >= 1024
            dst = kTs if isk else qTs
            mm = (c0 - (1024 if isk else 0)) // 128 + m
            k.op("act", lambda e: e.activation(out=s[:], in_=ps[:], func=AF.Identity, scale=(0.0625 if isk else 1.0)),
                 R=[ps], W=[s])
            k.dma("sp", dst.t[mm, :, tg * 512:(tg + 1) * 512], s[:], R=[s], W=[dst.bs[mm]])

        def sink_tok(T, ps, width, c0):
            i = cnt[0] % 2
            cnt[0] += 1
            tsl = slice(T * 128, (T + 1) * 128)
            if c0 < 3072:
                s = sb16[i]
                dst, col, sc = (ks, c0 - 1024, 0.0625) if c0 < 2048 else (vs, c0 - 2048, 1.0)
                k.op("act", lambda e: e.activation(out=s[:, 0:width], in_=ps[:, 0:width], func=AF.Identity, scale=sc),
                     R=[ps], W=[s])
            else:
                s = sf32[i]
                dst, col = (os_, c0 - 3072) if c0 < 4096 else ((pps, c0 - 4096) if c0 < 5120 else (gs, 0))
                k.op("dve", lambda e: e.tensor_copy(out=s[:, 0:width], in_=ps[:, 0:width]), R=[ps], W=[s])
            k.dma("sp", dst.t[tsl, col:col + width], s[:, 0:width], R=[s], W=[dst.bs[T]])
        plan = [(0, 512, "feat", sink_feat), (512, 512, "feat", sink_feat),
                (1024, 512, "feat", sink_feat), (1536, 512, "feat", sink_feat)]
        plan += [(c0, 512, "tok", sink_tok) for c0 in range(1024, 5120, 512)]
        plan += [(5120, 16, "tok", sink_tok)]
        self.proj(self.I["w_in_even"].t[0], 5136, plan)
    with ExitStack() as es:
        Cn = [[k.sb("Cn%d%d" % (d, h), (128, 2, 257), F32, es=es) for h in range(4)] for d in range(2)]
        Cb = [[k.sb("Cb%d%d" % (d, h), (128, 2, 257), BF16, es=es) for h in range(4)] for d in range(2)]
        mrep = k.sb("mrep", (128, 8), F32, es=es)
        keep = k.sb("keep", (128, 2, 16), F32, es=es)
        bg = k.sb("bg", (128, 16), F32, es=es)
        gcol = k.sb("gcol", (128, 8), F32, es=es)
        pscol = k.sb("pscol", (128, 8), F32, es=es)
        pw = k.sb("pw", (128, 4, 2, 256), BF16, es=es)
        k.dma("sp", mrep[:], self.I["m0"].t[:, :], W=[mrep])
        k.dma("sp", keep[:], self.I["keep"].t[:, :, :], W=[keep])
        k.dma("sp", bg[:], self.I["b_gate_even"].t[0].partition_broadcast(128), W=[bg])
        self._col_from_row(self.I["mlstm_gain"].t[0], 8, gcol, es)
        self._col_from_row(self.I["pool_scale"].t[0], 8, pscol, es)
        for g in range(4):
            for cc in range(2):
                k.dma("pool", pw[:, g, cc, :], self.I["pool_w"].t[0, g, cc * 128:(cc + 1) * 128, :], W=[pw])
        for d in range(2):
            for h in range(4):
                for cc in range(2):
                    k.dma("sp", Cn[d][h][:, cc, 0:256], self.I["C0"].t[d, h, cc * 128:(cc + 1) * 128, :], W=[Cn[d][h]])
                    k.dma("sp", Cn[d][h][:, cc, 256:257],
                          self.I["n0"].t[d, h, cc * 128:(cc + 1) * 128].rearrange("(p o) -> p o", o=1), W=[Cn[d][h]])
                k.op("dve", lambda e, d=d, h=h: e.tensor_copy(out=Cb[d][h][:], in_=Cn[d][h][:]), R=[Cn[d][h]], W=[Cb[d][h]])
        qTt = [k.sb("m_q%d" % i, (128, 8, 128), BF16, es=es) for i in range(2)]
        kTt = [k.sb("m_kT%d" % i, (128, 8, 128), BF16, es=es) for i in range(2)]
        kt = [k.sb("m_k%d" % i, (128, 1024), BF16, es=es) for i in range(2)]
        vx = [k.sb("m_v%d" % i, (128, 4, 257), BF16, es=es) for i in range(2)]
        gt = [k.sb("m_g%d" % i, (128, 16), F32, es=es) for i in range(2)]
        for i in range(2):
            k.op("pool", lambda e, i=i: e.memset(vx[i][:, :, 256:257], 1.0), W=[vx[i]])
        sm = {n: k.sb("m_" + n, (128, w), F32, es=es) for n, w in
              [("gg", 16), ("ab", 4), ("ex", 4), ("mn", 4), ("fl", 4), ("bsb", 8), ("r", 4), ("rmax", 1), ("dmax", 1),
               ("inter", 1), ("mt", 1), ("nmt", 1), ("a", 1), ("eneg", 1), ("den", 1), ("mm", 1), ("nmm", 1),
               ("dec", 1), ("ws", 1), ("mnew", 1), ("ss", 4)]}
        SH = []
        for hh in range(4):
            H = {n: k.sb("mh%d_%s" % (hh, n), (128, 1), F32, es=es) for n in
                 ("rmax", "dmax", "inter", "mt", "nmt", "a", "eneg", "den", "mm", "nmm", "dec", "ws", "mnew")}
            H["dg"] = k.sb("mh%d_dg" % hh, (128, 128), F32, es=es)
            H["dmat"] = k.sb("mh%d_dmat" % hh, (128, 128), F32, es=es)
            H["wt"] = k.sb("mh%d_w" % hh, (128, 128), F32, es=es)
            H["smat"] = k.sb("mh%d_smat" % hh, (128, 128), BF16, es=es)
            H["smT"] = k.sb("mh%d_smT" % hh, (128, 128), BF16, es=es)
            H["qca"] = k.sb("mh%d_qca" % hh, (128, 257), F32, es=es)
            H["num"] = k.sb("mh%d_num" % hh, (128, 257), F32, es=es)
            H["kws"] = k.sb("mh%d_kws" % hh, (128, 256), BF16, es=es)
            SH.append(H)
        hsum = k.sb("m_hsum", (128, 4, 256), F32, es=es)
        ot = k.sb("m_o", (128, 1024), F32, es=es)
        hn = k.sb("m_hn", (128, 1024), F32, es=es)
        hm = k.sb("m_hm", (128, 1024), BF16, es=es)
        junk = k.sb("m_junk", (128, 256), F32, es=es)
        pt = [k.sb("m_pt%d" % i, (128, 1024), F32, es=es) for i in range(3)]
        pm = k.sb("m_pm", (128, 3, 4, 128), F32, es=es)
        pld = k.sb("m_pld", (128, 2, 128), BF16, es=es)
        P = self.psb
        step = 0
        for d in range(2):
            tri = self.triF if d == 0 else self.triB
            msk = self.maskF if d == 0 else self.maskB
            for T in (range(NT) if d == 0 else range(NT - 1, -1, -1)):
                tsl = slice(T * 128, (T + 1) * 128)
                i = step % 2
                step += 1
                q_, kT_, k_, v_, g_ = qTt[i], kTt[i], kt[i], vx[i], gt[i]
                k.dma("sp", q_[:], qTs.t[:, :, tsl].rearrange("m p t -> p m t"), R=qTs.bs, W=[q_])
                k.dma("act", kT_[:], kTs.t[:, :, tsl].rearrange("m p t -> p m t"), R=kTs.bs, W=[kT_])
                k.dma("sp", k_[:], ks.t[tsl, :], R=[ks.bs[T]], W=[k_])
                k.dma("act", v_[:, :, 0:256], vs.t[tsl, :].rearrange("t (h d) -> t h d", d=256), R=[vs.bs[T]], W=[v_])
                k.dma("sp", g_[:], gs.t[tsl, :], R=[gs.bs[T]], W=[g_])
                S = sm
                k.op("dve", lambda e: e.tensor_tensor(out=S["gg"][:], in0=g_[:], in1=bg[:], op=ALU.add), R=[g_, bg], W=[S["gg"]])
                fg = S["gg"][:, 8 + d * 4:12 + d * 4]
                ig = S["gg"][:, d * 4:d * 4 + 4]
                k.op("dve", lambda e: e.tensor_scalar(out=S["ab"][:], in0=fg, scalar1=-1.0, scalar2=None, op0=ALU.mult), R=[S["gg"]], W=[S["ab"]])
                k.op("dve", lambda e: e.tensor_tensor(out=S["ab"][:], in0=S["ab"][:], in1=fg, op=ALU.max), R=[S["gg"], S["ab"]], W=[S["ab"]])
                k.op("act", lambda e: e.activation(out=S["ex"][:], in_=S["ab"][:], func=AF.Exp, scale=-1.0), R=[S["ab"]], W=[S["ex"]])
                k.op("act", lambda e: e.activation(out=S["ex"][:], in_=S["ex"][:], func=AF.Ln, bias=1.0), R=[S["ex"]], W=[S["ex"]])
                k.op("dve", lambda e: e.tensor_scalar(out=S["mn"][:], in0=fg, scalar1=0.0, scalar2=None, op0=ALU.min), R=[S["gg"]], W=[S["mn"]])
                k.op("dve", lambda e: e.tensor_tensor(out=S["fl"][:], in0=S["mn"][:], in1=S["ex"][:], op=ALU.subtract), R=[S["mn"], S["ex"]], W=[S["fl"]])
                k.op("pe", lambda e: e.matmul(P[0][:, 0:4], lhsT=tri, rhs=S["fl"][:], start=True, stop=True), R=[S["fl"], self.cst], W=[P[0]])
                k.op("pe", lambda e: e.matmul(P[0][:, 4:8], lhsT=self.ones, rhs=S["fl"][:], start=True, stop=True), R=[S["fl"], self.cst], W=[P[0]])
                k.op("dve", lambda e: e.tensor_copy(out=S["bsb"][:], in_=P[0][:, 0:8]), R=[P[0]], W=[S["bsb"]])
                k.op("dve", lambda e: e.tensor_tensor(out=S["r"][:], in0=ig, in1=S["bsb"][:, 0:4], op=ALU.subtract), R=[S["gg"], S["bsb"]], W=[S["r"]])
                if d == 1:
                    k.dma("sp", hsum[:], hF.t[tsl, :].rearrange("t (h d) -> t h d", d=256), R=[hF.bs[T]], W=[hsum])
                def head_body(h):
                    H = SH[h]
                    col = d * 4 + h
                    C_, Cb_ = Cn[d][h], Cb[d][h]
                    b_h = S["bsb"][:, h:h + 1]
                    be_h = S["bsb"][:, 4 + h:5 + h]
                    m_h = mrep[:, col:col + 1]
                    dg, dmat, wt, smat, smT, qca, num, kws = (H[n] for n in ("dg", "dmat", "wt", "smat", "smT", "qca", "num", "kws"))
                    k.op("dve", lambda e: e.tensor_scalar(out=dg[:], in0=self.ident, scalar1=S["r"][:, h:h + 1], scalar2=None, op0=ALU.mult), R=[S["r"], self.cst], W=[dg])
                    yield
                    k.op("pe", lambda e: e.matmul(P[1][:, 0:128], lhsT=self.ones, rhs=dg[:], start=True, stop=True), R=[dg, self.cst], W=[P[1]])
                    k.op("dve", lambda e: e.tensor_reduce(out=H["rmax"][:], in_=P[1][:, 0:128], axis=AX.X, op=ALU.max), R=[P[1]], W=[H["rmax"]])
                    k.op("dve", lambda e: e.scalar_tensor_tensor(out=dmat[:], in0=P[1][:, 0:128], scalar=b_h, in1=msk, op0=ALU.add, op1=ALU.add), R=[P[1], S["bsb"], self.cst], W=[dmat])
                    yield
                    k.op("dve", lambda e: e.tensor_reduce(out=H["dmax"][:], in_=dmat[:], axis=AX.X, op=ALU.max), R=[dmat], W=[H["dmax"]])
                    k.op("dve", lambda e: e.tensor_tensor(out=H["inter"][:], in0=b_h, in1=m_h, op=ALU.add), R=[S["bsb"], mrep], W=[H["inter"]])
                    yield
                    k.op("dve", lambda e: e.tensor_tensor(out=H["mt"][:], in0=H["inter"][:], in1=H["dmax"][:], op=ALU.max), R=[H["inter"], H["dmax"]], W=[H["mt"]])
                    k.op("dve", lambda e: e.tensor_scalar(out=H["nmt"][:], in0=H["mt"][:], scalar1=-1.0, scalar2=None, op0=ALU.mult), R=[H["mt"]], W=[H["nmt"]])
                    yield
                    k.op("act", lambda e: e.activation(out=wt[:], in_=dmat[:], func=AF.Exp, bias=H["nmt"][:, 0:1]), R=[dmat, H["nmt"]], W=[wt])
                    k.op("act", lambda e: e.activation(out=H["a"][:], in_=H["inter"][:], func=AF.Exp, bias=H["nmt"][:, 0:1]), R=[H["inter"], H["nmt"]], W=[H["a"]])
                    k.op("act", lambda e: e.activation(out=H["eneg"][:], in_=H["mt"][:], func=AF.Exp, scale=-1.0), R=[H["mt"]], W=[H["eneg"]])
                    yield
                    k.op("dve", lambda e: e.tensor_tensor(out=H["mm"][:], in0=m_h, in1=H["rmax"][:], op=ALU.max), R=[mrep, H["rmax"]], W=[H["mm"]])
                    k.op("dve", lambda e: e.tensor_scalar(out=H["nmm"][:], in0=H["mm"][:], scalar1=-1.0, scalar2=None, op0=ALU.mult), R=[H["mm"]], W=[H["nmm"]])
                    yield
                    k.op("act", lambda e: e.activation(out=H["dec"][:], in_=m_h, func=AF.Exp, bias=H["nmm"][:, 0:1]), R=[mrep, H["nmm"]], W=[H["dec"]])
                    k.op("act", lambda e: e.activation(out=H["ws"][:], in_=S["r"][:, h:h + 1], func=AF.Exp, bias=H["nmm"][:, 0:1]), R=[S["r"], H["nmm"]], W=[H["ws"]])
                    k.op("dve", lambda e: e.tensor_tensor(out=H["mnew"][:], in0=H["mm"][:], in1=be_h, op=ALU.add), R=[H["mm"], S["bsb"]], W=[H["mnew"]])
                    yield
                    for cc in range(2):
                        k.op("pe", lambda e, cc=cc: e.matmul(P[2][:, 0:128], lhsT=q_[:, h * 2 + cc, :], rhs=kT_[:, h * 2 + cc, :], start=(cc == 0), stop=(cc == 1)), R=[q_, kT_], W=[P[2]])
                    k.op("dve", lambda e: e.tensor_tensor(out=smat[:], in0=P[2][:, 0:128], in1=wt[:], op=ALU.mult), R=[P[2], wt], W=[smat])
                    yield
                    p3v = P[3].t[:].bitcast(BF16)
                    k.op("pe", lambda e: e.transpose(p3v[:, 0:128], smat[:], self.identb), R=[smat, self.cstb], W=[P[3]])
                    k.op("act", lambda e: e.copy(out=smT[:], in_=p3v[:, 0:128]), R=[P[3]], W=[smT])
                    yield
                    for cc in range(2):
                        k.op("pe", lambda e, cc=cc: e.matmul(P[4][:, 0:257], lhsT=q_[:, h * 2 + cc, :], rhs=Cb_[:, cc, :], start=(cc == 0), stop=(cc == 1)), R=[q_, Cb_], W=[P[4]])
                    k.op("act", lambda e: e.activation(out=qca[:], in_=P[4][:, 0:257], func=AF.Identity, scale=H["a"][:, 0:1]), R=[P[4], H["a"]], W=[qca])
                    yield
                    k.op("pe", lambda e: e.matmul(P[5][:, 0:257], lhsT=smT[:], rhs=v_[:, h, :], start=True, stop=True), R=[smT, v_], W=[P[5]])
                    k.op("dve", lambda e: e.tensor_tensor(out=num[:], in0=P[5][:, 0:257], in1=qca[:], op=ALU.add), R=[P[5], qca], W=[num])
                    yield
                    k.op("dve", lambda e: e.tensor_scalar(out=H["den"][:], in0=num[:, 256:257], scalar1=-1.0, scalar2=None, op0=ALU.mult), R=[num], W=[H["den"]])
                    k.op("dve", lambda e: e.tensor_tensor(out=H["den"][:], in0=H["den"][:], in1=num[:, 256:257], op=ALU.max), R=[num, H["den"]], W=[H["den"]])
                    k.op("dve", lambda e: e.tensor_tensor(out=H["den"][:], in0=H["den"][:], in1=H["eneg"][:], op=ALU.max), R=[H["eneg"], H["den"]], W=[H["den"]])
                    k.op("dve", lambda e: e.reciprocal(out=H["den"][:], in_=H["den"][:]), R=[H["den"]], W=[H["den"]])
                    yield
                    if d == 0:
                        k.op("dve", lambda e: e.tensor_scalar(out=hsum[:, h, :], in0=num[:, 0:256], scalar1=H["den"][:, 0:1], scalar2=None, op0=ALU.mult), R=[num, H["den"]], W=[hsum])
                    else:
                        k.op("dve", lambda e: e.scalar_tensor_tensor(out=hsum[:, h, :], in0=num[:, 0:256], scalar=H["den"][:, 0:1], in1=hsum[:, h, :], op0=ALU.mult, op1=ALU.add), R=[num, H["den"], hsum], W=[hsum])
                    k.op("dve", lambda e: e.tensor_scalar(out=kws[:], in0=k_[:, h * 256:(h + 1) * 256], scalar1=H["ws"][:, 0:1], scalar2=None, op0=ALU.mult), R=[k_, H["ws"]], W=[kws])
                    yield
                    for cc in range(2):
                        pu = P[6 + cc]
                        k.op("pe", lambda e, cc=cc, pu=pu: e.matmul(pu[:, 0:257], lhsT=kws[:, cc * 128:(cc + 1) * 128], rhs=v_[:, h, :], start=True, stop=True), R=[kws, v_], W=[pu])
                        k.op("dve", lambda e, cc=cc, pu=pu: e.scalar_tensor_tensor(out=C_[:, cc, :], in0=C_[:, cc, :], scalar=H["dec"][:, 0:1], in1=pu[:, 0:257], op0=ALU.mult, op1=ALU.add), R=[C_, H["dec"], pu], W=[C_])
                        yield
                    if (d == 0 and T % 2 == 1) or (d == 1 and T % 2 == 0):
                        slot = T // 2
                        for cc in range(2):
                            k.dma("sp", self.O["stC"].t[slot, d, h, cc * 128:(cc + 1) * 128, :], C_[:, cc, 0:256], R=[C_], W=[Buf()])
                            k.dma("sp", self.O["stn"].t[slot, d, h, cc * 128:(cc + 1) * 128].rearrange("(p o) -> p o", o=1), C_[:, cc, 256:257], R=[C_], W=[Buf()])
                        k.dma("sp", self.O["stm"].t[slot, d, h:h + 1].rearrange("(p o) -> p o", o=1), H["mnew"][0:1, 0:1], R=[H["mnew"]], W=[Buf()])
                    kf = keep[:, d, T:T + 1]
                    k.op("dve", lambda e: e.tensor_scalar(out=C_[:], in0=C_[:], scalar1=kf, scalar2=None, op0=ALU.mult), R=[C_, keep], W=[C_])
                    k.op("act", lambda e: e.copy(out=Cb_[:], in_=C_[:]), R=[C_], W=[Cb_])
                    k.op("dve", lambda e: e.tensor_tensor(out=m_h, in0=H["mnew"][:], in1=kf, op=ALU.mult), R=[H["mnew"], keep], W=[mrep])
                    yield

                gens = [head_body(h) for h in range(4)]
                while gens:
                    for g_ in list(gens):
                        try:
                            next(g_)
                        except StopIteration:
                            gens.remove(g_)
                if d == 0:
                    k.dma("sp", hF.t[tsl, :].rearrange("t (h d) -> t h d", d=256), hsum[:], R=[hsum], W=[hF.bs[T]])
                    rs = [r for r in range(3) if 0 <= T + r - 1 < NT]
                    for r in rs:
                        Tn = T + r - 1
                        k.dma("act", pt[r][:], pps.t[Tn * 128:(Tn + 1) * 128, :], R=[pps.bs[Tn]], W=[pt[r]])
                    k.dma("sp", pm[:], self.I["poolm"].t[T].rearrange("r g s t -> s r g t"), W=[pm])
                    for g in range(4):
                        for cc in range(2):
                            for r in rs:
                                k.op("pe", lambda e, g=g, cc=cc, r=r: e.matmul(P[1][:, 256 + cc * 128:256 + (cc + 1) * 128], lhsT=pt[r][:, g * 256 + cc * 128:g * 256 + (cc + 1) * 128], rhs=pm[:, r, g, :], start=(r == rs[0]), stop=(r == rs[-1])), R=[pt[r], pm], W=[P[1]])
                        k.op("act", lambda e: e.copy(out=pld[:].rearrange("p c t -> p (c t)"), in_=P[1][:, 256:512]), R=[P[1]], W=[pld])
                        for dd in range(2):
                            for cc in range(2):
                                k.op("pe", lambda e, g=g, cc=cc, dd=dd: e.matmul(P[2][:, 256:384], lhsT=pw[:, g, cc, dd * 128:(dd + 1) * 128], rhs=pld[:, cc, :], start=(cc == 0), stop=(cc == 1)), R=[pw, pld], W=[P[2]])
                            c = 8 + g * 2 + dd
                            k.op("dve", lambda e, c=c, g=g, dd=dd: e.tensor_scalar(out=self.actT.t[:, c, tsl], in0=P[2][:, 256:384], scalar1=pscol[:, g * 2 + dd:g * 2 + dd + 1], scalar2=None, op0=ALU.mult), R=[P[2], pscol], W=[self.actT.bs[T]])
                else:
                    for h in range(4):
                        k.op("act", lambda e, h=h: e.activation(out=junk[:], in_=hsum[:, h, :], func=AF.Square, accum_out=S["ss"][:, h:h + 1]), R=[hsum], W=[junk, S["ss"]])
                    self.rstd(S["ss"], 1.0 / 256)
                    k.dma("act", ot[:], os_.t[tsl, :], R=[os_.bs[T]], W=[ot])
                    k.op("act", lambda e: e.activation(out=ot[:], in_=ot[:], func=AF.Sigmoid), R=[ot], W=[ot])
                    for h in range(4):
                        k.op("dve", lambda e, h=h: e.tensor_scalar(out=hn[:, h * 256:(h + 1) * 256], in0=hsum[:, h, :], scalar1=S["ss"][:, h:h + 1], scalar2=None, op0=ALU.mult), R=[hsum, S["ss"]], W=[hn])
                    k.op("dve", lambda e: e.tensor_tensor(out=hm[:], in0=hn[:], in1=ot[:], op=ALU.mult), R=[hn, ot], W=[hm])
                    for c in range(8):
                        pb_ = P[3 + (c % 2) * 2]
                        pv = pb_.t[:].bitcast(BF16)
                        k.op("pe", lambda e, c=c, pv=pv: e.transpose(pv[:, 0:128], hm[:, c * 128:(c + 1) * 128], self.identb), R=[hm, self.cstb], W=[pb_])
                        k.op("act", lambda e, c=c, pv=pv: e.activation(out=self.actT.t[:, c, tsl], in_=pv[:, 0:128], func=AF.Identity, scale=gcol[:, c:c + 1]), R=[pb_, gcol], W=[self.actT.bs[T]])
        k.barrier()
    self.phase_outproj(self.I["w_out_even"].t[0], hold, hnew, 0)


Prog.phase_even = _phase_even


def _phase_odd(self, hold, hnew, l=1):
    k = self.k
    lam_init = 0.8 - 0.6 * math.exp(-0.3 * l)
    zq = k.dram("o_q", (TOK, 1024), F32, nbuf=16)
    zk = k.dram("o_k", (TOK, 1024), F32, nbuf=16)
    zu = k.dram("o_gu", (TOK, 1024), F32, nbuf=16)
    zv = k.dram("o_gv", (TOK, 1024), F32, nbuf=16)
    qTs = k.dram("o_qT", (8, 128, TOK), BF16, nbuf=8)
    kTs = k.dram("o_kT", (8, 128, 2560), BF16, nbuf=8)
    ck, cv = self.O["ck"], self.O["cv"]
    P = self.psb
    with ExitStack() as es:
        sf32 = [k.sb("op_f%d" % i, (128, 512), F32, es=es) for i in range(2)]
        cnt = [0]

        def sink_tok(T, ps, width, c0):
            s = sf32[cnt[0] % 2]
            cnt[0] += 1
            tsl = slice(T * 128, (T + 1) * 128)
            if cnt[0] % 2:
                k.op("act", lambda e: e.copy(out=s[:], in_=ps[:]), R=[ps], W=[s])
            else:
                k.op("dve", lambda e: e.tensor_copy(out=s[:], in_=ps[:]), R=[ps], W=[s])
            j = c0 // 1024
            col = c0 % 1024
            if j == 2:
                h0 = col // 128
                k.dma("sp", cv.t[T, h0:h0 + 4].rearrange("h t d -> t h d"), s[:].rearrange("t (h d) -> t h d", d=128),
                      R=[s], W=[cv.bs[T]])
            else:
                dst = [zq, zk, None, zu, zv][j]
                k.dma("sp", dst.t[tsl, col:col + 512], s[:], R=[s], W=[dst.bs[T]])
        self.proj(self.I["w_in_odd"].t[0], 5120, [(c0, 512, "tok", sink_tok) for c0 in range(0, 5120, 512)])
    with ExitStack() as es:
        gq = k.sb("o_gq", (128, 2, 64), F32, es=es)
        k.dma("sp", gq[:].rearrange("p a d -> p (a d)"), self.I["qk_gain"].t[0].rearrange("a d -> (a d)").partition_broadcast(128), W=[gq])
        xt = [k.sb("o_x%d" % i, (128, 16, 64), F32, es=es) for i in range(2)]
        t1 = k.sb("o_t1", (128, 16, 64), F32, es=es)
        sw = k.sb("o_sw", (128, 16, 64), F32, es=es)
        xb = k.sb("o_xb", (128, 16, 64), BF16, es=es)
        ss = k.sb("o_ss", (128, 16), F32, es=es)
        rc = k.sb("o_rc", (128, 64), F32, es=es)
        rs = k.sb("o_rs", (128, 64), F32, es=es)
        tb = [k.sb("o_tb%d" % i, (128, 128), BF16, es=es) for i in range(2)]
        ckf = k.sb("o_ckf", (128, 128), F32, es=es)
        n = 0
        for T in range(NT):
            tsl = slice(T * 128, (T + 1) * 128)
            k.dma("sp", rc[:], self.I["ropec"].t[tsl, :], W=[rc])
            k.dma("sp", rs[:], self.I["ropes"].t[tsl, :], W=[rs])
            for which, src in ((0, zq), (1, zk)):
                x = xt[which]
                k.dma("act", x[:].rearrange("p a d -> p (a d)"), src.t[tsl, :], R=[src.bs[T]], W=[x])
                k.op("dve", lambda e, x=x: e.tensor_tensor(out=t1[:], in0=x[:], in1=x[:], op=ALU.mult), R=[x], W=[t1])
                k.op("dve", lambda e: e.tensor_reduce(out=ss[:], in_=t1[:], axis=AX.X, op=ALU.add), R=[t1], W=[ss])
                self.rstd(ss, 1.0 / 64)
                k.op("dve", lambda e, x=x: e.tensor_tensor(out=x[:], in0=x[:], in1=ss[:].unsqueeze(2).to_broadcast([128, 16, 64]), op=ALU.mult), R=[x, ss], W=[x])
                k.op("dve", lambda e, x=x, which=which: e.tensor_tensor(out=x[:], in0=x[:], in1=gq[:, which, :].unsqueeze(1).to_broadcast([128, 16, 64]), op=ALU.mult), R=[x, gq], W=[x])
                xv = x[:].rearrange("p a (b c d) -> p a b c d", b=2, c=2)
                sv = sw[:].rearrange("p a (b c d) -> p a b c d", b=2, c=2)
                for b in range(2):
                    k.op("pool", lambda e, b=b: e.tensor_copy(out=sv[:, :, b, 0, :], in_=xv[:, :, b, 1, :]), R=[x], W=[sw])
                    k.op("pool", lambda e, b=b: e.tensor_copy(out=sv[:, :, b, 1, :], in_=xv[:, :, b, 0, :]), R=[x], W=[sw])
                k.op("dve", lambda e, x=x: e.tensor_tensor(out=t1[:], in0=x[:], in1=rc[:].unsqueeze(1).to_broadcast([128, 16, 64]), op=ALU.mult), R=[x, rc], W=[t1])
                k.op("dve", lambda e: e.tensor_tensor(out=sw[:], in0=sw[:], in1=rs[:].unsqueeze(1).to_broadcast([128, 16, 64]), op=ALU.mult), R=[sw, rs], W=[sw])
                k.op("dve", lambda e, x=x: e.tensor_tensor(out=x[:], in0=t1[:], in1=sw[:], op=ALU.add), R=[t1, sw], W=[x])
                k.op("act", lambda e, x=x: e.copy(out=xb[:], in_=x[:]), R=[x], W=[xb])
                if which == 1:
                    k.dma("sp", ck.t[T].rearrange("h t d -> t h d"), x[:].rearrange("p (h m) d -> p h (m d)", m=2), R=[x], W=[ck.bs[T]])
                dstT = kTs if which else qTs
                for h in range(8):
                    pb_ = P[n % 4]
                    pv = pb_.t[:].bitcast(BF16)
                    t_ = tb[n % 2]
                    n += 1
                    k.op("pe", lambda e, h=h, pv=pv: e.transpose(pv[:, 0:128], xb[:, 2 * h:2 * h + 2, :].rearrange("p a d -> p (a d)"), self.identb), R=[xb, self.cstb], W=[pb_])
                    k.op("act", lambda e, pv=pv, t_=t_: e.copy(out=t_[:], in_=pv[:, 0:128]), R=[pb_], W=[t_])
                    k.dma("sp", dstT.t[h, :, tsl], t_[:], R=[t_], W=[dstT.bs[h]])
        for h in range(8):
            for j in range(4):
                pb_ = P[n % 4]
                pv = pb_.t[:].bitcast(BF16)
                t_ = tb[n % 2]
                n += 1
                k.dma("act", ckf[:], self.I["cache_k"].t[h, j * 128:(j + 1) * 128, :], W=[ckf])
                k.op("dve", lambda e: e.tensor_copy(out=xb[:, 0:2, :].rearrange("p a d -> p (a d)"), in_=ckf[:]), R=[ckf], W=[xb])
                k.op("pe", lambda e, pv=pv: e.transpose(pv[:, 0:128], xb[:, 0:2, :].rearrange("p a d -> p (a d)"), self.identb), R=[xb, self.cstb], W=[pb_])
                k.op("act", lambda e, pv=pv, t_=t_: e.copy(out=t_[:], in_=pv[:, 0:128]), R=[pb_], W=[t_])
                k.dma("sp", kTs.t[h, :, 2048 + j * 128:2048 + (j + 1) * 128], t_[:], R=[t_], W=[kTs.bs[h]])
        k.barrier()
    with ExitStack() as es:
        lamt = k.sb("a_lam", (128, 4, 64), F32, es=es)
        lp = k.sb("a_lp", (128, 2, 64), F32, es=es)
        ls = k.sb("a_ls", (128, 2), F32, es=es)
        nlam = k.sb("a_nlam", (128, 1), F32, es=es)
        k.dma("sp", lamt[:].rearrange("p a d -> p (a d)"), self.I["da_lambda"].t[0].rearrange("a d -> (a d)").partition_broadcast(128), W=[lamt])
        lv = lamt[:].rearrange("p (a b) d -> p a b d", b=2)
        k.op("dve", lambda e: e.tensor_tensor(out=lp[:], in0=lv[:, :, 0, :], in1=lv[:, :, 1, :], op=ALU.mult), R=[lamt], W=[lp])
        k.op("dve", lambda e: e.tensor_reduce(out=ls[:], in_=lp[:], axis=AX.X, op=ALU.add), R=[lp], W=[ls])
        k.op("act", lambda e: e.activation(out=ls[:], in_=ls[:], func=AF.Exp), R=[ls], W=[ls])
        k.op("dve", lambda e: e.tensor_tensor(out=nlam[:], in0=ls[:, 1:2], in1=ls[:, 0:1], op=ALU.subtract), R=[ls], W=[nlam])
        k.op("dve", lambda e: e.tensor_scalar(out=nlam[:], in0=nlam[:], scalar1=-lam_init, scalar2=None, op0=ALU.add), R=[nlam], W=[nlam])
        slcol = k.sb("a_sl", (128, 1), F32, es=es)
        self._col_from_row(self.I["da_subln"].t[0], 1, slcol, es)
        am = k.sb("a_am", (128, 16, 20), F32, es=es)
        k.dma("sp", am[:], self.I["amask"].t[:, :, :], W=[am])
        fm = k.sb("a_fm", (128, 16, 20), F32, es=es)
        k.op("dve", lambda e: e.tensor_scalar(out=fm[:], in0=am[:], scalar1=-1.0, scalar2=None, op0=ALU.is_ge), R=[am], W=[fm])
        slc = k.sb("a_slc", (128, 1), F32, es=es)
        k.op("dve", lambda e: e.tensor_scalar(out=slc[:], in0=slcol[:], scalar1=(1.0 - lam_init), scalar2=None, op0=ALU.mult), R=[slcol], W=[slc])
        kT = [k.sb("a_kT%d" % i, (128, 2560), BF16, es=es) for i in range(2)]
        qT = [k.sb("a_qT%d" % i, (128, TOK), BF16, es=es) for i in range(2)]
        vx = [k.sb("a_vx%d" % i, (128, 20, 128), BF16, es=es) for i in range(2)]
        pex = [k.sb("a_pe%d" % i, (128, 512), BF16, es=es) for i in range(3)]
        pT = [k.sb("a_pT%d" % i, (128, 4, 128), BF16, es=es) for i in range(3)]
        r0 = k.sb("a_r0", (128, 512), F32, es=es)
        r1 = k.sb("a_r1", (128, 512), F32, es=es)
        osb = k.sb("a_osb", (128, 512), F32, es=es)
        sq = k.sb("a_sq", (128, 512), F32, es=es)
        npt = 0
        for h in range(8):
            kT_, qT_, vx_ = kT[h % 2], qT[h % 2], vx[h % 2]
            k.dma("sp", kT_[:], kTs.t[h], R=[kTs.bs[h]], W=[kT_])
            k.dma("sp", qT_[:], qTs.t[h], R=[qTs.bs[h]], W=[qT_])
            k.dma("pool", vx_[:, 0:16, :], cv.t[:, h].rearrange("T t d -> t T d"), R=cv.bs, W=[vx_])
            k.dma("pool", vx_[:, 16:20, :], self.I["cache_v"].t[h].rearrange("(j t) d -> t j d", t=128), W=[vx_])
            for Tg in range(4):
                gsl = slice(Tg * 512, (Tg + 1) * 512)
                seq = [(m, kt) for m in range(2) for kt in range(20)]
                LA = 2
                for j in range(len(seq) + LA):
                    if j < len(seq):
                        m, kt = seq[j]
                        msl = slice(m * 64, (m + 1) * 64)
                        pbank = P[j % 3]
                        k.op("pe", lambda e, kt=kt, pbank=pbank, msl=msl: e.matmul(
                            pbank[:, :], lhsT=kT_[msl, kt * 128:(kt + 1) * 128], rhs=qT_[msl, gsl], start=True, stop=True),
                            R=[kT_, qT_], W=[pbank])
                    if j >= LA:
                        m, kt = seq[j - LA]
                        pbank = P[(j - LA) % 3]
                        px, p_ = pex[npt % 3], pT[npt % 3]
                        npt += 1
                        k.op("act", lambda e, pbank=pbank, px=px: e.activation(out=px[:], in_=pbank[:, :], func=AF.Exp, scale=0.125),
                             R=[pbank], W=[px])
                        k.op("dve", lambda e, kt=kt, px=px, p_=p_: e.tensor_tensor(
                            out=p_[:], in0=px[:].rearrange("p (a t) -> p a t", t=128),
                            in1=fm[:, Tg * 4:(Tg + 1) * 4, kt].unsqueeze(2).to_broadcast([128, 4, 128]), op=ALU.mult),
                            R=[px, fm], W=[p_])
                        pflat = p_[:].rearrange("p a t -> p (a t)")
                        k.op("pe", lambda e, kt=kt, pflat=pflat, m=m: e.matmul(
                            P[3 + m][:, :], lhsT=vx_[:, kt, :], rhs=pflat, start=(kt == 0), stop=(kt == 19)),
                            R=[p_, vx_], W=[P[3 + m]])
                        k.op("pe", lambda e, kt=kt, pflat=pflat, m=m: e.matmul(
                            P[5 + m][:, :], lhsT=self.onesb, rhs=pflat, start=(kt == 0), stop=(kt == 19)),
                            R=[p_, self.cstb], W=[P[5 + m]])
                k.op("dve", lambda e: e.reciprocal(out=r0[:], in_=P[5][:, :]), R=[P[5]], W=[r0])
                k.op("dve", lambda e: e.reciprocal(out=r1[:], in_=P[6][:, :]), R=[P[6]], W=[r1])
                k.op("dve", lambda e: e.tensor_scalar(out=r1[:], in0=r1[:], scalar1=nlam[:, 0:1], scalar2=None, op0=ALU.mult), R=[r1, nlam], W=[r1])
                k.op("dve", lambda e: e.tensor_tensor(out=osb[:], in0=P[3][:, :], in1=r0[:], op=ALU.mult), R=[P[3], r0], W=[osb])
                k.op("dve", lambda e: e.tensor_tensor(out=r1[:], in0=P[4][:, :], in1=r1[:], op=ALU.mult), R=[P[4], r1], W=[r1])
                k.op("dve", lambda e: e.tensor_tensor(out=osb[:], in0=osb[:], in1=r1[:], op=ALU.add), R=[osb, r1], W=[osb])
                k.op("act", lambda e: e.activation(out=sq[:], in_=osb[:], func=AF.Square), R=[osb], W=[sq])
                k.op("pe", lambda e: e.matmul(P[7][:, :], lhsT=self.ones, rhs=sq[:], start=True, stop=True), R=[sq, self.cst], W=[P[7]])
                k.op("dve", lambda e: e.tensor_scalar(out=r0[:], in0=P[7][:, :], scalar1=1.0 / 128, scalar2=EPS, op0=ALU.mult, op1=ALU.add), R=[P[7]], W=[r0])
                k.op("act", lambda e: e.activation(out=r0[:], in_=r0[:], func=AF.Sqrt), R=[r0], W=[r0])
                k.op("dve", lambda e: e.reciprocal(out=r0[:], in_=r0[:]), R=[r0], W=[r0])
                k.op("dve", lambda e: e.scalar_tensor_tensor(out=self.actT.t[:, h, gsl], in0=osb[:], scalar=slc[:, 0:1], in1=r0[:], op0=ALU.mult, op1=ALU.mult),
                     R=[osb, slc, r0], W=[self.actT.bs[Tg * 4 + i] for i in range(4)])
        k.barrier()
    with ExitStack() as es:
        wsT = k.sb("g_wsT", (128, 8, 128), BF16, es=es)
        wtmp = k.sb("g_wt", (128, 128), F32, es=es)
        bcol = k.sb("g_b", (128, 8), F32, es=es)
        k.dma("sp", bcol[:], self.I["gm_b"].t[0].rearrange("g t -> t g"), W=[bcol], allow_slow_non_contiguous=True)
        for g in range(8):
            k.dma("sp", wtmp[:], self.I["gm_ws"].t[0, g], W=[wtmp])
            k.op("pe", lambda e: e.transpose(P[0][:, 0:128], wtmp[:], self.ident), R=[wtmp, self.cst], W=[P[0]])
            k.op("act", lambda e, g=g: e.copy(out=wsT[:, g, :], in_=P[0][:, 0:128]), R=[P[0]], W=[wsT])
        gu = k.sb("g_u", (128, 1024), F32, es=es)
        gv = k.sb("g_v", (128, 8, 128), F32, es=es)
        t1 = k.sb("g_t1", (128, 8, 128), F32, es=es)
        ss = k.sb("g_ss", (128, 8), F32, es=es)
        vg = k.sb("g_vg", (128, 8, 128), BF16, es=es)
        ob = k.sb("g_ob", (128, 1024), BF16, es=es)
        for T in range(NT):
            tsl = slice(T * 128, (T + 1) * 128)
            k.dma("sp", gu[:], zu.t[tsl, :], R=[zu.bs[T]], W=[gu])
            k.dma("act", gv[:].rearrange("p g c -> p (g c)"), zv.t[tsl, :], R=[zv.bs[T]], W=[gv])
            k.op("act", lambda e: e.activation(out=gu[:], in_=gu[:], func=AF.Gelu_apprx_tanh), R=[gu], W=[gu])
            k.op("act", lambda e: e.activation(out=gv[:], in_=gv[:], func=AF.Gelu_apprx_tanh), R=[gv], W=[gv])
            k.op("dve", lambda e: e.tensor_tensor(out=t1[:], in0=gv[:], in1=gv[:], op=ALU.mult), R=[gv], W=[t1])
            k.op("dve", lambda e: e.tensor_reduce(out=ss[:], in_=t1[:], axis=AX.X, op=ALU.add), R=[t1], W=[ss])
            self.rstd(ss, 1.0 / 128)
            k.op("dve", lambda e: e.tensor_tensor(out=vg[:], in0=gv[:], in1=ss[:].unsqueeze(2).to_broadcast([128, 8, 128]), op=ALU.mult), R=[gv, ss], W=[vg])
            for g in range(8):
                pb_ = P[1 + g // 4]
                k.op("pe", lambda e, g=g, pb_=pb_: e.matmul(pb_[:, (g % 4) * 128:(g % 4 + 1) * 128], lhsT=wsT[:, g, :], rhs=vg[:, g, :], start=True, stop=True), R=[wsT, vg], W=[pb_])
            for g in range(8):
                pb_ = P[1 + g // 4]
                k.op("dve", lambda e, g=g, pb_=pb_: e.scalar_tensor_tensor(out=ob[:, g * 128:(g + 1) * 128], in0=pb_[:, (g % 4) * 128:(g % 4 + 1) * 128], scalar=bcol[:, g:g + 1], in1=gu[:, g * 128:(g + 1) * 128], op0=ALU.add, op1=ALU.mult), R=[pb_, bcol, gu], W=[ob])
            for g in range(8):
                pb_ = P[3 + g % 2]
                pv = pb_.t[:].bitcast(BF16)
                k.op("pe", lambda e, g=g, pv=pv: e.transpose(pv[:, 0:128], ob[:, g * 128:(g + 1) * 128], self.identb), R=[ob, self.cstb], W=[pb_])
                k.op("act", lambda e, g=g, pv=pv: e.copy(out=self.actT.t[:, 8 + g, tsl], in_=pv[:, 0:128]), R=[pb_], W=[self.actT.bs[T]])
        k.barrier()
    self.phase_outproj(self.I["w_out_odd"].t[0], hold, hnew, 0)


Prog.phase_odd = _phase_odd


def build_full():
    p = Prog()
    for l in range(2):
        hb = 2 * l
        p.phase_mod(l)
        p.alloc_act()
        p.phase_norm(p.h[hb], 0)
        if l == 0:
            p.phase_even(p.h[hb], p.h[hb + 1])
        else:
            p.phase_odd(p.h[hb], p.h[hb + 1])
        p.phase_norm(p.h[hb + 1], 1)
        p.phase_peer_prep(l)
        p.phase_peer_q(l)
        p.free_act()
        p.phase_peer_main(l, p.h[hb + 1], p.h[hb + 2])
    p.k.finish()
    return p
```

```python
import math
from contextlib import ExitStack
import numpy as np
import concourse.bass as bass
import concourse.mybir as mybir
from concourse.bass_utils import run_bass_kernel_spmd

F32 = mybir.dt.float32
BF16 = mybir.dt.bfloat16
AF = mybir.ActivationFunctionType
ALU = mybir.AluOpType
AX = mybir.AxisListType

D = 2048
NT = 16
TOK = 2048
EPS = 1e-6
NEG = -1.0e30
RD = 8


class Buf:
    __slots__ = ("w", "r", "name")

    def __init__(self, name=""):
        self.w = {}
        self.r = {}
        self.name = name


class Tile:
    def __init__(self, t, name, nbuf=1):
        self.t = t
        self.b = Buf(name)
        self.bs = [Buf(name + str(i)) for i in range(nbuf)] if nbuf > 1 else None

    def __getitem__(self, k):
        return self.t[k]


class KB:
    def __init__(self):
        self.nc = bass.Bass("TRN2", target_bir_lowering=False)
        nc = self.nc
        self.es = ExitStack()
        self.eng = {"pe": nc.tensor, "act": nc.scalar, "dve": nc.vector, "pool": nc.gpsimd, "sp": nc.sync}
        self.sems = {}
        self.cnt = {}
        for e in ["pe", "act", "dve", "pool"]:
            self.sems["c_" + e] = self.es.enter_context(nc.semaphore("c_" + e))
            self.cnt["c_" + e] = 0
        self.dq = ["sp", "pool", "act"]
        self.dcnt = {q: 0 for q in self.dq}
        for q in self.dq:
            for i in range(RD):
                k = "d_%s_%d" % (q, i)
                self.sems[k] = self.es.enter_context(nc.semaphore(k))
                self.cnt[k] = 0
        self.waited = {e: {} for e in self.eng}
        self.nins = 0

    def _nid(self):
        self._n = getattr(self, "_n", 0) + 1
        return self._n

    def sb(self, name, shape, dt=F32, es=None, nbuf=1):
        t = (es or self.es).enter_context(self.nc.sbuf_tensor("sb%d_%s" % (self._nid(), name), list(shape), dt))
        return Tile(t, name, nbuf)

    def ps(self, name, shape, dt=F32, es=None):
        t = (es or self.es).enter_context(self.nc.psum_tensor("pp%d_%s" % (self._nid(), name), list(shape), dt))
        return Tile(t, name)

    def dram(self, name, shape, dt=F32, kind="Internal", nbuf=1):
        t = self.nc.dram_tensor(name if kind != "Internal" else "dr%d_%s" % (self._nid(), name), list(shape), dt, kind=kind).ap()
        return Tile(t, name, nbuf)

    def _wait(self, e, key, val):
        if val <= 0:
            return
        if e == "pe" and key == "c_pe":
            return
        if self.waited[e].get(key, 0) >= val:
            return
        self.eng[e].wait_ge(self.sems[key], val)
        self.waited[e][key] = val

    def _deps(self, e, R, W):
        for b in R:
            for k, v in b.w.items():
                self._wait(e, k, v)
        own = "c_" + e
        for b in W:
            for k, v in b.w.items():
                if k != own:
                    self._wait(e, k, v)
            for k, v in b.r.items():
                self._wait(e, k, v)

    def _mark(self, key, val, R, W):
        for b in R:
            if b.r.get(key, 0) < val:
                b.r[key] = val
        for b in W:
            b.w = {key: val}
            b.r = {}

    @staticmethod
    def _bufs(xs):
        out = []
        for x in xs:
            if isinstance(x, Tile):
                out.append(x.b)
            elif x is not None:
                out.append(x)
        return out

    def op(self, e, fn, R=(), W=()):
        R = self._bufs(R)
        W = self._bufs(W)
        self._deps(e, R, W)
        key = "c_" + e
        self.cnt[key] += 1
        fn(self.eng[e]).then_inc(self.sems[key], 1)
        self._mark(key, self.cnt[key], R, W)
        self.nins += 1

    def dma(self, q, out, in_, R=(), W=(), **kw):
        if q == "act":
            q = "sp"
        R = self._bufs(R)
        W = self._bufs(W)
        self._deps(q, R, W)
        n = self.dcnt[q]
        slot = n % RD
        val = 16 * (n // RD + 1)
        key = "d_%s_%d" % (q, slot)
        self._wait(q, key, val - 16)
        self.dcnt[q] += 1
        self.cnt[key] = val
        self.eng[q].dma_start(out=out, in_=in_, **kw).then_inc(self.sems[key], 16)
        self._mark(key, val, R, W)
        self.nins += 1

    def barrier(self, engines=None):
        for e in (engines or list(self.eng)):
            for k, v in self.cnt.items():
                self._wait(e, k, v)

    def finish(self):
        self.barrier(["sp"])
        self.es.close()


W_SPEC = [
    ("norm_mix", (2, D)), ("norm_ffn", (2, D)), ("w_mod", (2, D, 6 * D)), ("b_mod", (2, 6 * D)),
    ("w_in_even", (1, D, 5136)), ("b_gate_even", (1, 16)), ("mlstm_gain", (1, 1024)),
    ("pool_w", (1, 4, 256, 256)), ("pool_scale", (1, 1024)), ("w_out_even", (1, D, D)),
    ("w_in_odd", (1, D, 5120)), ("qk_gain", (1, 2, 64)), ("da_lambda", (1, 4, 64)), ("da_subln", (1, 128)),
    ("gm_ws", (1, 8, 128, 128)), ("gm_b", (1, 8, 128)), ("w_out_odd", (1, D, D)),
    ("peer_wq", (2, D, D)), ("peer_subkeys", (2, 8, 2, 128, 128)), ("peer_u", (2, 16384, D)),
    ("peer_v", (2, 16384, D)),
]
C_SPEC = [
    ("x", (TOK, D)), ("cond_pc", (128, 16)), ("consts", (128, 8, 128)),
    ("keep", (128, 2, 16)), ("C0", (2, 4, 256, 256)), ("n0", (2, 4, 256)), ("m0", (128, 8)),
    ("amask", (128, 16, 20)), ("ropec", (TOK, 64)), ("ropes", (TOK, 64)),
    ("cache_k", (8, 512, 128)), ("cache_v", (8, 512, 128)), ("poolm", (16, 3, 4, 128, 128)),
]
O_SPEC = [
    ("y", (TOK, D)), ("stC", (8, 2, 4, 256, 256)), ("stn", (8, 2, 4, 256)), ("stm", (8, 2, 4)),
    ("ck", (16, 8, 128, 128)), ("cv", (16, 8, 128, 128)),
]


class LazyIn(dict):
    def __init__(self, k):
        super().__init__()
        self.k = k
        self.shapes = dict(W_SPEC + C_SPEC)

    def __missing__(self, name):
        t = self.k.dram(name, self.shapes[name], F32, kind="ExternalInput")
        self[name] = t
        return t


class Prog:
    def __init__(self, stages=("all",), dbg=()):
        self.k = KB()
        k = self.k
        self.stages = stages
        self.I = LazyIn(k)
        self.O = {}
        for name, shp in O_SPEC:
            self.O[name] = k.dram(name, shp, F32, kind="ExternalOutput", nbuf=16)
        self.dbg = {}
        for name, shp, dt in dbg:
            self.dbg[name] = k.dram("dbg_" + name, shp, dt, kind="ExternalOutput", nbuf=16)
        self.h = [self.I["x"]] + [k.dram("h%d" % i, (TOK, D), F32, nbuf=16) for i in range(1, 4)] + [self.O["y"]]
        self.h[0].bs = [Buf("x%d" % i) for i in range(16)]
        self.setup_consts()

    def setup_consts(self):
        k = self.k
        self.cst = k.sb("cst", (128, 8, 128), F32)
        k.dma("sp", self.cst[:], self.I["consts"][:, :, :], W=[self.cst])
        c = self.cst
        self.ident = c[:, 0, :]
        self.ones = c[:, 1, :]
        self.triF = c[:, 2, :]
        self.triB = c[:, 3, :]
        self.maskF = c[:, 4, :]
        self.maskB = c[:, 5, :]
        self.cstb = k.sb("cstb", (128, 2, 128), BF16)
        k.op("dve", lambda e: e.tensor_copy(out=self.cstb[:], in_=c[:, 0:2, :]), R=[c], W=[self.cstb])
        self.identb = self.cstb[:, 0, :]
        self.onesb = self.cstb[:, 1, :]
        self.actT = None
        self._act_es = None
        self.skT = k.sb("skT", (128, 16, 128), F32)
        self.psb = [k.ps("psb%d" % i, (128, 512), F32) for i in range(8)]
        self.condsb = k.sb("condsb", (128, 16), F32)
        k.dma("sp", self.condsb[:], self.I["cond_pc"][:, :], W=[self.condsb])
        self.sc = k.sb("sc", (128, 16), F32)
        k.op("act", lambda e: e.activation(out=self.sc[:], in_=self.condsb[:], func=AF.Silu),
             R=[self.condsb], W=[self.sc])
        self.gate = [k.sb("gateM", (128, D), F32), k.sb("gateF", (128, D), F32)]
        self.acol = [k.sb("acolM", (128, 16), F32), k.sb("acolF", (128, 16), F32)]
        self.bcol = [k.sb("bcolM", (128, 16), F32), k.sb("bcolF", (128, 16), F32)]

    def alloc_act(self):
        self._act_es = ExitStack()
        self.actT = self.k.sb("actT", (128, 16, TOK), BF16, es=self._act_es, nbuf=16)

    def free_act(self):
        self.k.barrier()
        self._act_es.close()
        self.actT = None

    def rstd(self, s, inv_n, eps=EPS):
        k = self.k
        k.op("dve", lambda e: e.tensor_scalar(out=s[:], in0=s[:], scalar1=inv_n, scalar2=eps,
                                              op0=ALU.mult, op1=ALU.add), R=[s], W=[s])
        k.op("act", lambda e: e.activation(out=s[:], in_=s[:], func=AF.Sqrt), R=[s], W=[s])
        k.op("dve", lambda e: e.reciprocal(out=s[:], in_=s[:]), R=[s], W=[s])

    def phase_mod(self, l):
        k = self.k
        with ExitStack() as es:
            wb = [k.sb("mw%d" % i, (128, 16, 512), F32, es=es) for i in range(2)]
            bb = [k.sb("mb%d" % i, (128, 512), F32, es=es) for i in range(2)]
            self.screp = k.sb("screp", (128, 16, 128), F32, es=es)
            for c_ in range(16):
                k.op("dve", lambda e, c_=c_: e.tensor_copy(out=self.screp[:, c_, :],
                                                          in_=self.sc[:, c_:c_ + 1].to_broadcast([128, 128])),
                     R=[self.sc], W=[self.screp])
            srow = k.sb("srow", (128, D), F32, es=es)
            arow = k.sb("arow", (128, D), F32, es=es)
            grow = k.sb("grow", (128, D), F32, es=es)
            wm = self.I["w_mod"].t[l].rearrange("(c p) n -> p c n", p=128)
            bm = self.I["b_mod"].t[l]
            blk = 0
            for sub in range(2):
                gsrc = self.I["norm_mix" if sub == 0 else "norm_ffn"].t[l]
                k.dma("sp", grow[:], gsrc.partition_broadcast(128), W=[grow])
                for i in range(3):
                    for j in range(4):
                        n0 = (sub * 3 + i) * D + j * 512
                        w = wb[blk % 2]
                        b = bb[blk % 2]
                        k.dma("sp", w[:, 0:8, :], wm[:, 0:8, n0:n0 + 512], W=[w])
                        k.dma("act", w[:, 8:16, :], wm[:, 8:16, n0:n0 + 512], W=[w])
                        k.dma("sp", b[:], bm[n0:n0 + 512].partition_broadcast(128), W=[b])
                        ps = self.psb[blk % 2]
                        for c in range(16):
                            k.op("pe", lambda e, c=c, ps=ps, w=w: e.matmul(ps[:], lhsT=self.screp[:, c, :], rhs=w[:, c, :],
                                                                        start=(c == 0), stop=(c == 15)),
                                 R=[self.screp, w], W=[ps])
                        dst = [srow, arow, self.gate[sub]][i]
                        k.op("dve", lambda e, ps=ps, b=b, dst=dst, j=j: e.tensor_tensor(
                            out=dst[:, j * 512:(j + 1) * 512], in0=ps[:], in1=b[:], op=ALU.add),
                            R=[ps, b], W=[dst])
                        blk += 1
                k.op("dve", lambda e: e.scalar_tensor_tensor(out=arow[:], in0=arow[:], scalar=1.0, in1=grow[:],
                                                             op0=ALU.add, op1=ALU.mult), R=[arow, grow], W=[arow])
                for src, dst in ((arow, self.acol[sub]), (srow, self.bcol[sub])):
                    for c in range(16):
                        ps = self.psb[2 + (c % 2)]
                        k.op("pe", lambda e, ps=ps, src=src, c=c: e.transpose(ps[:, 0:128], src[:, c * 128:(c + 1) * 128],
                                                                          self.ident), R=[src, self.cst], W=[ps])
                        k.op("dve", lambda e, ps=ps, dst=dst, c=c: e.tensor_copy(out=dst[:, c:c + 1], in_=ps[:, 0:1]),
                             R=[ps], W=[dst])
            k.barrier()

    def phase_norm(self, hsrc, sub):
        k = self.k
        with ExitStack() as es:
            ht = [k.sb("nh%d" % i, (128, D), F32, es=es) for i in range(2)]
            junk = k.sb("njunk", (128, D), F32, es=es)
            hs = [k.sb("nhs%d" % i, (128, D), BF16, es=es) for i in range(2)]
            ss = [k.sb("nss%d" % i, (128, 1), F32, es=es) for i in range(2)]
            pst = [k.ps("npst%d" % i, (128, 1024), BF16, es=es) for i in range(2)] if False else None
            for T in range(NT):
                h = ht[T % 2]
                s = ss[T % 2]
                hb = hs[T % 2]
                k.dma("sp", h[:, 0:1024], hsrc.t[T * 128:(T + 1) * 128, 0:1024], R=[hsrc.bs[T]], W=[h])
                k.dma("act", h[:, 1024:2048], hsrc.t[T * 128:(T + 1) * 128, 1024:2048], R=[hsrc.bs[T]], W=[h])
                k.op("act", lambda e, h=h, s=s: e.activation(out=junk[:], in_=h[:], func=AF.Square, accum_out=s[:]),
                     R=[h], W=[junk, s])
                self.rstd(s, 1.0 / D)
                k.op("dve", lambda e, h=h, s=s, hb=hb: e.tensor_scalar(out=hb[:], in0=h[:], scalar1=s[:, 0:1], scalar2=None,
                                                                  op0=ALU.mult), R=[h, s], W=[hb])
                for c in range(16):
                    ps = self.psb[c % 4]
                    psv = ps.t[:].bitcast(BF16)
                    k.op("pe", lambda e, psv=psv, hb=hb, c=c: e.transpose(psv[:, 0:128], hb[:, c * 128:(c + 1) * 128],
                                                                      self.identb), R=[hb, self.cstb], W=[ps])
                    eng = "act" if c % 2 == 0 else "dve"
                    dst = self.actT.t[:, c, T * 128:(T + 1) * 128]
                    if eng == "act":
                        k.op("act", lambda e, psv=psv, dst=dst, c=c: e.activation(
                            out=dst, in_=psv[:, 0:128], func=AF.Identity, scale=self.acol[sub][:, c:c + 1],
                            bias=self.bcol[sub][:, c:c + 1]), R=[ps, self.acol[sub], self.bcol[sub]], W=[self.actT.bs[T]])
                    else:
                        k.op("dve", lambda e, psv=psv, dst=dst, c=c: e.tensor_scalar(
                            out=dst, in0=psv[:, 0:128], scalar1=self.acol[sub][:, c:c + 1],
                            scalar2=self.bcol[sub][:, c:c + 1], op0=ALU.mult, op1=ALU.add),
                            R=[ps, self.acol[sub], self.bcol[sub]], W=[self.actT.bs[T]])
            k.barrier()

    def proj(self, wsrc, ncols, col_plan, es_outer=None):
        k = self.k
        with ExitStack() as es:
            wb = [k.sb("pw%d" % i, (128, 16, 512), BF16, es=es) for i in range(2)]
            wv = wsrc.rearrange("(c p) n -> p c n", p=128)
            bi = 0
            pi = 0
            for (c0, width, mode, sink) in col_plan:
                w = wb[bi % 2]
                bi += 1
                for cc in range(0, 16, 4):
                    k.dma("pool", w[:, cc:cc + 4, 0:width], wv[:, cc:cc + 4, c0:c0 + width], W=[w])
                if mode == "tok":
                    for T in range(NT):
                        ps = self.psb[4 + pi % 4]
                        pi += 1
                        for c in range(16):
                            k.op("pe", lambda e, ps=ps, w=w, c=c, T=T: e.matmul(
                                ps[:, 0:width], lhsT=self.actT.t[:, c, T * 128:(T + 1) * 128], rhs=w[:, c, 0:width],
                                start=(c == 0), stop=(c == 15)), R=[self.actT.bs[T], w], W=[ps])
                        sink(T, ps, width, c0)
                else:
                    for m in range(width // 128):
                        for tg in range(4):
                            ps = self.psb[4 + pi % 4]
                            pi += 1
                            for c in range(16):
                                k.op("pe", lambda e, ps=ps, w=w, c=c, m=m, tg=tg: e.matmul(
                                    ps[:, :], lhsT=w[:, c, m * 128:(m + 1) * 128],
                                    rhs=self.actT.t[:, c, tg * 512:(tg + 1) * 512],
                                    start=(c == 0), stop=(c == 15)),
                                    R=[self.actT.bs[tg * 4 + i] for i in range(4)] + [w], W=[ps])
                            sink(m, tg, ps, c0)
            k.barrier()


def build_program(stages=("all",), dbg=()):
    p = Prog(stages, dbg)
    return p


def make_consts():
    c = np.zeros((128, 8, 128), np.float32)
    i = np.arange(128)
    c[:, 0, :] = np.eye(128)
    c[:, 1, :] = 1.0
    c[:, 2, :] = (i[:, None] <= i[None, :])
    c[:, 3, :] = (i[:, None] >= i[None, :])
    c[:, 4, :] = np.where(i[None, :] <= i[:, None], 0.0, NEG)
    c[:, 5, :] = np.where(i[None, :] >= i[:, None], 0.0, NEG)
    return c


def core_inputs(inp, core):
    prompt = core < 4
    m = {}
    f32 = np.float32
    if prompt:
        m["x"] = np.ascontiguousarray(inp["x_prompt"][8 * core:8 * core + 8].reshape(TOK, D))
        cond = inp["c_ctx"]
        L = 256
    else:
        b = core - 4
        m["x"] = np.ascontiguousarray(inp["x_sample"][b])
        cond = inp["c"][b]
        L = 2048
    m["cond_pc"] = np.ascontiguousarray(np.asarray(cond).reshape(16, 128).T)
    m["consts"] = make_consts()
    keep = np.ones((128, 2, 16), f32)
    T = np.arange(16)
    if prompt:
        keep[:, 0, :] = (T % 2 == 0)
        keep[:, 1, :] = (T % 2 == 1)
        m["C0"] = np.zeros((2, 4, 256, 256), f32)
        m["n0"] = np.zeros((2, 4, 256), f32)
        m["m0"] = np.zeros((128, 8), f32)
        m["cache_k"] = np.zeros((8, 512, 128), f32)
        m["cache_v"] = np.zeros((8, 512, 128), f32)
        am = np.full((16, 20), -30000.0, f32)
        for t in range(16):
            am[t, (t // 2) * 2:(t // 2) * 2 + 2] = 0.0
        m["ropec"] = np.ones((TOK, 64), f32)
        m["ropes"] = np.zeros((TOK, 64), f32)
    else:
        m["C0"] = np.ascontiguousarray(inp["state_mlstm_C"][b, 0])
        m["n0"] = np.ascontiguousarray(inp["state_mlstm_n"][b, 0])
        m["m0"] = np.ascontiguousarray(np.broadcast_to(np.asarray(inp["state_mlstm_m"][b, 0]).reshape(1, 8), (128, 8)))
        m["cache_k"] = np.ascontiguousarray(inp["cache_da_k"][b, 0])
        m["cache_v"] = np.ascontiguousarray(inp["cache_da_v"][b, 0])
        am = np.zeros((16, 20), f32)
        t = np.arange(TOK)
        row = (t // 64).astype(f32)
        col = (t % 64).astype(f32)
        inv = (np.float32(10000.0) ** (-np.arange(16, dtype=f32) / np.float32(16))).astype(f32)
        ar = (row[:, None] * inv).astype(f32)
        ac = (col[:, None] * inv).astype(f32)
        m["ropec"] = np.concatenate([np.cos(ar), np.cos(ar), np.cos(ac), np.cos(ac)], 1).astype(f32)
        m["ropes"] = np.concatenate([-np.sin(ar), np.sin(ar), -np.sin(ac), np.sin(ac)], 1).astype(f32)
    m["keep"] = keep
    m["amask"] = np.ascontiguousarray(np.broadcast_to(am[None], (128, 16, 20)))
    m["poolm"] = make_poolm(L)
    return m


_POOLM = {}


def make_poolm(L):
    if L in _POOLM:
        return _POOLM[L]
    pm = np.zeros((16, 3, 4, 128, 128), np.float32)
    pos = np.arange(TOK)
    seq0 = (pos // L) * L
    for g, w in enumerate((2, 4, 8, 16)):
        p = pos - seq0
        lo = np.clip(p - w // 2, 0, L) + seq0
        hi = np.clip(p - w // 2 + w, 0, L) + seq0
        cnt = (hi - lo).astype(np.float32)
        A = np.zeros((TOK, TOK), np.float32)
        for t in range(TOK):
            A[lo[t]:hi[t], t] = np.float32(1.0) / cnt[t]
            A[t, t] -= 1.0
        for T in range(16):
            for r in range(3):
                Tn = T + r - 1
                if 0 <= Tn < 16:
                    pm[T, r, g] = A[Tn * 128:(Tn + 1) * 128, T * 128:(T + 1) * 128]
    _POOLM[L] = pm
    return pm


_PROG = {}


def kernel(**inputs):
    inp = {k_: np.asarray(v) for k_, v in inputs.items()}
    if "p" not in _PROG:
        _PROG["p"] = build_full()
    p = _PROG["p"]
    wnames = [n for n, _ in W_SPEC]
    in_maps = []
    for core in range(8):
        m = core_inputs(inp, core)
        for n in wnames:
            m[n] = inp[n]
        in_maps.append({n: np.ascontiguousarray(m[n], dtype=np.float32) for n in p.I})
    res = run_bass_kernel_spmd(p.k.nc, in_maps, core_ids=list(range(8)))
    r = res.results
    y_prompt = np.concatenate([r[c]["y"].reshape(8, 256, D) for c in range(4)], 0)
    y_sample = np.stack([r[c]["y"] for c in range(4, 8)], 0)
    nC = np.concatenate([r[c]["stC"] for c in range(4)], 0)[:, None]
    nn = np.concatenate([r[c]["stn"] for c in range(4)], 0)[:, None]
    nm = np.concatenate([r[c]["stm"] for c in range(4)], 0)[:, None]

    def cache(name):
        out = []
        for c in range(4):
            a = r[c][name].reshape(8, 2, 8, 128, 128).transpose(0, 2, 1, 3, 4).reshape(8, 8, 256, 128)
            out.append(a)
        return np.concatenate(out, 0)[:, None]
    f = lambda a: np.ascontiguousarray(a, dtype=np.float32)
    return (f(y_prompt), f(y_sample), f(nC), f(nn), f(nm), f(cache("ck")), f(cache("cv")))


def _col_from_row(self, row_ap, n, dst, es):
    k = self.k
    tmp = k.sb("cfr_%d" % self._uid(), (128, n * 128), F32, es=es)
    k.dma("sp", tmp[:], row_ap.partition_broadcast(128), W=[tmp])
    for c in range(n):
        ps = self.psb[c % 2]
        k.op("pe", lambda e, ps=ps, c=c: e.transpose(ps[:, 0:128], tmp[:, c * 128:(c + 1) * 128], self.ident),
             R=[tmp, self.cst], W=[ps])
        k.op("dve", lambda e, ps=ps, c=c: e.tensor_copy(out=dst[:, c:c + 1], in_=ps[:, 0:1]), R=[ps], W=[dst])


def _uid(self):
    self._u = getattr(self, "_u", 0) + 1
    return self._u


def _residual_sink(self, hold, hnew, sub, es):
    k = self.k
    hb = [k.sb("rs_h%d_%d" % (i, self._uid()), (128, 512), F32, es=es) for i in range(2)]
    tb = [k.sb("rs_t%d_%d" % (i, self._uid()), (128, 512), F32, es=es) for i in range(2)]
    cnt = [0]

    def sink(T, ps, width, c0):
        i = cnt[0] % 2
        cnt[0] += 1
        h, t = hb[i], tb[i]
        k.dma("sp", h[:, 0:width], hold.t[T * 128:(T + 1) * 128, c0:c0 + width], R=[hold.bs[T]], W=[h])
        k.op("dve", lambda e: e.tensor_tensor(out=t[:, 0:width], in0=ps[:, 0:width],
                                              in1=self.gate[sub][:, c0:c0 + width], op=ALU.mult),
             R=[ps, self.gate[sub]], W=[t])
        k.op("pool", lambda e: e.tensor_tensor(out=t[:, 0:width], in0=t[:, 0:width], in1=h[:, 0:width], op=ALU.add),
             R=[t, h], W=[t])
        k.dma("sp", hnew.t[T * 128:(T + 1) * 128, c0:c0 + width], t[:, 0:width], R=[t], W=[hnew.bs[T]])
    return sink


def _phase_outproj(self, wsrc, hold, hnew, sub):
    with ExitStack() as es:
        sink = self._residual_sink(hold, hnew, sub, es)
        self.proj(wsrc, D, [(j * 512, 512, "tok", sink) for j in range(4)])


Prog._col_from_row = _col_from_row
Prog._uid = _uid
Prog._residual_sink = _residual_sink
Prog.phase_outproj = _phase_outproj


def _top16(self, src_ap, dst16, scr, R, es_bufs):
    k = self.k
    k.op("dve", lambda e: e.max(out=dst16[:, 0:8], in_=src_ap), R=R, W=[dst16])
    k.op("dve", lambda e: e.match_replace(out=scr[:], in_to_replace=dst16[:, 0:8], in_values=src_ap, imm_value=-1.0),
         R=R + [dst16], W=[scr])
    k.op("dve", lambda e: e.max(out=dst16[:, 8:16], in_=scr[:]), R=[scr], W=[dst16])


Prog._top16 = _top16


def _phase_peer_prep(self, l):
    k = self.k
    if not hasattr(self, "GS"):
        self.GS = k.dram("GS", (128, 128, TOK), BF16, nbuf=128)
        self.VB = k.dram("VB", (128, 128, D), BF16, nbuf=128)
        self.qTs = k.dram("qTs", (16, 128, TOK), F32, nbuf=16)
    U = self.I["peer_u"].t[l].rearrange("(i j) d -> i j d", j=128)
    V = self.I["peer_v"].t[l].rearrange("(i j) d -> i j d", j=128)
    P = self.psb
    with ExitStack() as es:
        ub = [k.sb("pu%d" % i, (128, D), BF16, es=es) for i in range(2)]
        vb = [k.sb("pv%d" % i, (128, D), BF16, es=es) for i in range(2)]
        ut = [k.sb("put%d" % i, (128, 16, 128), BF16, es=es) for i in range(2)]
        gsb = [k.sb("pgs%d" % i, (128, TOK), BF16, es=es) for i in range(2)]
        sk = k.sb("psk", (128, 128), F32, es=es)
        for m in range(16):
            k.dma("sp", sk[:], self.I["peer_subkeys"].t[l, m // 2, m % 2], W=[sk])
            ps = P[m % 2]
            k.op("pe", lambda e, ps=ps: e.transpose(ps[:, 0:128], sk[:], self.ident), R=[sk, self.cst], W=[ps])
            k.op("act", lambda e, ps=ps, m=m: e.copy(out=self.skT[:, m, :], in_=ps[:, 0:128]), R=[ps], W=[self.skT])
        nh = 0
        for i in range(128):
            u, v, t, g = ub[i % 2], vb[i % 2], ut[i % 2], gsb[i % 2]
            k.dma("pool", u[:], U[i], W=[u])
            k.dma("pool", v[:], V[i], W=[v])
            k.dma("act", self.VB.t[i], v[:], R=[v], W=[self.VB.bs[i]])
            for c in range(16):
                ps = P[c // 8]
                psv = ps.t[:].bitcast(BF16)
                k.op("pe", lambda e, psv=psv, u=u, c=c: e.transpose(psv[:, (c % 8) * 128:(c % 8 + 1) * 128],
                                                                   u[:, c * 128:(c + 1) * 128], self.identb),
                     R=[u, self.cstb], W=[ps])
                if c % 8 == 7:
                    dst = t[:, c - 7:c + 1, :]
                    src = psv[:, 0:1024].rearrange("p (c j) -> p c j", j=128)
                    if c == 7:
                        k.op("act", lambda e, src=src, dst=dst: e.copy(out=dst, in_=src), R=[ps], W=[t])
                    else:
                        k.op("dve", lambda e, src=src, dst=dst: e.tensor_copy(out=dst, in_=src), R=[ps], W=[t])
            for half in range(2):
                pa, pb2 = P[2 + 2 * (nh % 3)], P[3 + 2 * (nh % 3)]
                nh += 1
                for q4, pbank in ((0, pa), (1, pb2)):
                    tg = half * 2 + q4
                    for c in range(16):
                        k.op("pe", lambda e, c=c, tg=tg, pbank=pbank, t=t: e.matmul(
                            pbank[:, :], lhsT=t[:, c, :], rhs=self.actT.t[:, c, tg * 512:(tg + 1) * 512],
                            start=(c == 0), stop=(c == 15)),
                            R=[t] + [self.actT.bs[tg * 4 + x] for x in range(4)], W=[pbank])
                    k.op("act", lambda e, tg=tg, pbank=pbank, g=g: e.activation(
                        out=g[:, tg * 512:(tg + 1) * 512], in_=pbank[:, :], func=AF.Gelu_apprx_tanh), R=[pbank], W=[g])
            k.dma("sp", self.GS.t[i], g[:], R=[g], W=[self.GS.bs[i]])
        k.barrier()


def _phase_peer_q(self, l):
    k = self.k
    with ExitStack() as es:
        st = [k.sb("pq%d" % i, (128, 512), F32, es=es) for i in range(2)]
        cnt = [0]

        def sink(m, tg, ps, c0):
            s = st[cnt[0] % 2]
            cnt[0] += 1
            mm = c0 // 128 + m
            eng = "act" if cnt[0] % 2 else "dve"
            if eng == "act":
                k.op("act", lambda e: e.copy(out=s[:], in_=ps[:]), R=[ps], W=[s])
            else:
                k.op("dve", lambda e: e.tensor_copy(out=s[:], in_=ps[:]), R=[ps], W=[s])
            k.dma("sp", self.qTs.t[mm, :, tg * 512:(tg + 1) * 512], s[:], R=[s], W=[self.qTs.bs[mm]])
        self.proj(self.I["peer_wq"].t[l], D, [(j * 512, 512, "feat", sink) for j in range(4)])


def _phase_peer_main(self, l, hold, hnew, tiles=None):
    k = self.k
    IB = 4
    NBLK = 128 // IB
    tiles = list(tiles if tiles is not None else range(NT))
    with ExitStack() as es:
        sets = []
        for si in range(2):
            S = {}
            S["qt"] = k.sb("eq%d" % si, (128, 16, 128), F32, es=es)
            S["s_all"] = k.sb("es%d" % si, (128, 16, 128), F32, es=es)
            S["e_all"] = k.sb("ee%d" % si, (128, 16, 128), F32, es=es)
            S["mx"] = k.sb("emx%d" % si, (128, 16), F32, es=es)
            S["ev"] = k.sb("eev%d" % si, (128, 16, 16), F32, es=es)
            S["ct"] = k.sb("ect%d" % si, (128, 8, 16), F32, es=es)
            S["rz"] = k.sb("erz%d" % si, (128, 8), F32, es=es)
            S["dg"] = k.sb("edg%d" % si, (128, 8, 128), BF16, es=es)
            sets.append(S)
        scr_sh = k.sb("escr", (128, 256), F32, es=es)
        cE_sh = k.sb("ecE", (128, 8, 256), F32, es=es)
        for S in sets:
            S["scr"] = scr_sh
            S["cE"] = cE_sh
        HA = 6
        Ea = [k.sb("eEa%d" % i, (128, HA, IB, 128), F32, es=es) for i in range(2)]
        Ed = [k.sb("eEd%d" % i, (128, 8 - HA, IB, 128), F32, es=es) for i in range(2)]
        Gb = [k.sb("eG%d" % i, (128, 8, IB, 128), BF16, es=es) for i in range(2)]
        vb = [k.sb("ev%d" % i, (128, IB, D), BF16, es=es) for i in range(3)]
        ge = [k.sb("ege%d" % i, (128, IB, 128), BF16, es=es) for i in range(3)]
        at = [k.sb("eat%d" % i, (128, IB, 128), BF16, es=es) for i in range(2)]
        sink = self._residual_sink(hold, hnew, 1, es)
        psO = self.psb[0:4]
        psG = self.psb[4:6]
        psS = self.psb[6:8]

        def preamble(T, S):
            tsl = slice(T * 128, (T + 1) * 128)
            qt, s_all, e_all, scr, mx, ev, cE, ct, rz, dg = (S[n] for n in ("qt", "s_all", "e_all", "scr", "mx", "ev", "cE", "ct", "rz", "dg"))
            k.dma("sp", qt[:], self.qTs.t[:, :, tsl].rearrange("m p t -> p m t"), R=self.qTs.bs, W=[qt])
            for half in range(2):
                for mm in range(8):
                    m = half * 8 + mm
                    ps = psS[mm // 4]
                    k.op("pe", lambda e, ps=ps, m=m, mm=mm: e.matmul(ps[:, (mm % 4) * 128:(mm % 4 + 1) * 128], lhsT=qt[:, m, :],
                                                              rhs=self.skT[:, m, :], start=True, stop=True),
                         R=[qt, self.skT], W=[ps])
                for g in range(2):
                    k.op("act", lambda e, g=g, half=half: e.copy(out=s_all[:, half * 8 + g * 4:half * 8 + (g + 1) * 4, :],
                                                                 in_=psS[g][:, :].rearrange("p (m k) -> p m k", k=128)),
                         R=[psS[g]], W=[s_all])
            k.op("dve", lambda e: e.tensor_reduce(out=mx[:], in_=s_all[:], axis=AX.X, op=ALU.max), R=[s_all], W=[mx])
            k.op("dve", lambda e: e.tensor_scalar(out=mx[:], in0=mx[:], scalar1=-1.0, scalar2=None, op0=ALU.mult),
                 R=[mx], W=[mx])
            for m in range(16):
                k.op("act", lambda e, m=m: e.activation(out=e_all[:, m, :], in_=s_all[:, m, :], func=AF.Exp,
                                                        bias=mx[:, m:m + 1]), R=[s_all, mx], W=[e_all])
            for m in range(16):
                k.op("dve", lambda e, m=m: e.max(out=ev[:, m, 0:8], in_=e_all[:, m, :]), R=[e_all], W=[ev])
                k.op("dve", lambda e, m=m: e.match_replace(out=scr[:, 0:128], in_to_replace=ev[:, m, 0:8],
                                                           in_values=e_all[:, m, :], imm_value=-1.0),
                     R=[e_all, ev], W=[scr])
                k.op("dve", lambda e, m=m: e.max(out=ev[:, m, 8:16], in_=scr[:, 0:128]), R=[scr], W=[ev])
                k.op("dve", lambda e, m=m: e.scalar_tensor_tensor(out=e_all[:, m, :], in0=e_all[:, m, :],
                                                                  scalar=ev[:, m, 15:16], in1=e_all[:, m, :],
                                                                  op0=ALU.is_ge, op1=ALU.mult),
                     R=[e_all, ev], W=[e_all])
            for h in range(HA):
                for a_ in range(16):
                    k.op("act", lambda e, h=h, a_=a_: e.activation(
                        out=cE[:, h, a_ * 16:(a_ + 1) * 16], in_=ev[:, 2 * h + 1, :], func=AF.Identity,
                        scale=ev[:, 2 * h, a_:a_ + 1]), R=[ev], W=[cE])
            for h in range(8):
                if h >= HA:
                    k.op("dve", lambda e, h=h: e.tensor_tensor(
                        out=cE[:, h, :].rearrange("p (a b) -> p a b", b=16),
                        in0=ev[:, 2 * h, :].unsqueeze(2).to_broadcast([128, 16, 16]),
                        in1=ev[:, 2 * h + 1, :].unsqueeze(1).to_broadcast([128, 16, 16]), op=ALU.mult),
                        R=[ev], W=[cE])
                k.op("dve", lambda e, h=h: e.max(out=ct[:, h, 0:8], in_=cE[:, h, :]), R=[cE], W=[ct])
                k.op("dve", lambda e, h=h: e.match_replace(out=scr[:], in_to_replace=ct[:, h, 0:8], in_values=cE[:, h, :],
                                                           imm_value=-1.0), R=[cE, ct], W=[scr])
                k.op("dve", lambda e, h=h: e.max(out=ct[:, h, 8:16], in_=scr[:]), R=[scr], W=[ct])
            k.op("dve", lambda e: e.tensor_reduce(out=rz[:], in_=ct[:], axis=AX.X, op=ALU.add), R=[ct], W=[rz])
            k.op("dve", lambda e: e.reciprocal(out=rz[:], in_=rz[:]), R=[rz], W=[rz])
            for h in range(8):
                k.op("dve", lambda e, h=h: e.tensor_scalar(out=dg[:, h, :], in0=self.ident, scalar1=rz[:, h:h + 1],
                                                           scalar2=None, op0=ALU.mult), R=[rz, self.cst], W=[dg])
            if "pe_s" in self.dbg and T == 0:
                k.dma("sp", self.dbg["pe_s"].t[:, :, :], s_all[:], R=[s_all], W=[self.dbg["pe_s"]])
                k.dma("sp", self.dbg["pe_e"].t[:, :, :], e_all[:], R=[e_all], W=[self.dbg["pe_e"]])
                k.dma("sp", self.dbg["pe_ct"].t[:, :, :], ct[:], R=[ct], W=[self.dbg["pe_ct"]])
                k.dma("sp", self.dbg["pe_rz"].t[:, :], rz[:], R=[rz], W=[self.dbg["pe_rz"]])

        nb = [0]

        def front(T, S, ib):
            tsl = slice(T * 128, (T + 1) * 128)
            e_all, ct, dg = S["e_all"], S["ct"], S["dg"]
            n = nb[0]
            nb[0] += 1
            v, gt = vb[n % 3], ge[n % 3]
            pg, a_t = psG[n % 2], at[n % 2]
            EA, ED, G = Ea[n % 2], Ed[n % 2], Gb[n % 2]
            i0 = ib * IB
            e4 = e_all[:].rearrange("p (h q) k -> p h q k", q=2)
            k.dma("sp", gt[:], self.GS.t[i0:i0 + IB, :, tsl].rearrange("i j t -> j i t"),
                  R=self.GS.bs[i0:i0 + IB], W=[gt])
            k.dma("act", v[:], self.VB.t[i0:i0 + IB].rearrange("i j d -> j i d"),
                  R=self.VB.bs[i0:i0 + IB], W=[v])
            for h in range(HA):
                for ii in range(IB):
                    k.op("act", lambda e, h=h, ii=ii: e.activation(
                        out=EA[:, h, ii, :], in_=e4[:, h, 1, :], func=AF.Identity,
                        scale=e4[:, h, 0, i0 + ii:i0 + ii + 1]), R=[e_all], W=[EA])
            k.op("dve", lambda e: e.tensor_tensor(
                out=ED[:], in0=e4[:, HA:8, 0, i0:i0 + IB].unsqueeze(3).to_broadcast([128, 8 - HA, IB, 128]),
                in1=e4[:, HA:8, 1, :].unsqueeze(2).to_broadcast([128, 8 - HA, IB, 128]), op=ALU.mult),
                R=[e_all], W=[ED])
            for h in range(8):
                Eh = EA[:, h] if h < HA else ED[:, h - HA]
                Et = EA if h < HA else ED
                k.op("dve", lambda e, h=h, Eh=Eh: e.scalar_tensor_tensor(
                    out=G[:, h], in0=Eh, scalar=ct[:, h, 15:16], in1=Eh, op0=ALU.is_ge, op1=ALU.mult),
                    R=[Et, ct], W=[G])
            for ii in range(IB):
                for h in range(8):
                    k.op("pe", lambda e, h=h, ii=ii: e.matmul(
                        pg[:, ii * 128:(ii + 1) * 128], lhsT=G[:, h, ii, :], rhs=dg[:, h, :],
                        start=(h == 0), stop=(h == 7)), R=[G, dg], W=[pg])
            return (i0, v, gt, pg, a_t)

        def back(st):
            i0, v, gt, pg, a_t = st
            k.op("dve", lambda e: e.tensor_tensor(
                out=a_t[:].rearrange("p i t -> p (i t)"), in0=gt[:].rearrange("p i t -> p (i t)"),
                in1=pg[:, 0:IB * 128], op=ALU.mult), R=[gt, pg], W=[a_t])
            for ii in range(IB):
                i = i0 + ii
                for dblk in range(4):
                    k.op("pe", lambda e, ii=ii, dblk=dblk, i=i: e.matmul(
                        psO[dblk][:, :], lhsT=a_t[:, ii, :], rhs=v[:, ii, dblk * 512:(dblk + 1) * 512],
                        start=(i == 0), stop=(i == 127), skip_group_check=True), R=[a_t, v], W=[psO[dblk]])

        preamble(tiles[0], sets[0])
        for ti, T in enumerate(tiles):
            S = sets[ti % 2]
            pending = None
            for ib in range(NBLK):
                st = front(T, S, ib)
                if pending is not None:
                    back(pending)
                pending = st
                if ib == NBLK // 2 and ti + 1 < len(tiles):
                    preamble(tiles[ti + 1], sets[(ti + 1) % 2])
            back(pending)
            for dblk in range(4):
                sink(T, psO[dblk], 512, dblk * 512)
        k.barrier()


Prog.phase_peer_prep = _phase_peer_prep
Prog.phase_peer_q = _phase_peer_q
Prog.phase_peer_main = _phase_peer_main


def _phase_even(self, hold, hnew):
    k = self.k
    qTs = k.dram("e_qT", (8, 128, TOK), BF16, nbuf=8)
    kTs = k.dram("e_kT", (8, 128, TOK), BF16, nbuf=8)
    ks = k.dram("e_k", (TOK, 1024), BF16, nbuf=16)
    vs = k.dram("e_v", (TOK, 1024), BF16, nbuf=16)
    os_ = k.dram("e_o", (TOK, 1024), F32, nbuf=16)
    pps = k.dram("e_p", (TOK, 1024), F32, nbuf=16)
    gs = k.dram("e_g", (TOK, 16), F32, nbuf=16)
    hF = k.dram("e_hF", (TOK, 1024), F32, nbuf=16)
    with ExitStack() as es:
        sb16 = [k.sb("ep_b%d" % i, (128, 512), BF16, es=es) for i in range(2)]
        sf32 = [k.sb("ep_f%d" % i, (128, 512), F32, es=es) for i in range(2)]
        cnt = [0]

        def sink_feat(m, tg, ps, c0):
            s = sb16[cnt[0] % 2]
            cnt[0] += 1
            isk = c0 >= 1024
            dst = kTs if isk else qTs
            mm = (c0 - (1024 if isk else 0)) // 128 + m
            k.op("act", lambda e: e.activation(out=s[:], in_=ps[:], func=AF.Identity, scale=(0.0625 if isk else 1.0)),
                 R=[ps], W=[s])
            k.dma("sp", dst.t[mm, :, tg * 512:(tg + 1) * 512], s[:], R=[s], W=[dst.bs[mm]])

        def sink_tok(T, ps, width, c0):
            i = cnt[0] % 2
            cnt[0] += 1
            tsl = slice(T * 128, (T + 1) * 128)
            if c0 < 3072:
                s = sb16[i]
                dst, col, sc = (ks, c0 - 1024, 0.0625) if c0 < 2048 else (vs, c0 - 2048, 1.0)
                k.op("act", lambda e: e.activation(out=s[:, 0:width], in_=ps[:, 0:width], func=AF.Identity, scale=sc),
                     R=[ps], W=[s])
            else:
                s = sf32[i]
                dst, col = (os_, c0 - 3072) if c0 < 4096 else ((pps, c0 - 4096) if c0 < 5120 else (gs, 0))
                k.op("dve", lambda e: e.tensor_copy(out=s[:, 0:width], in_=ps[:, 0:width]), R=[ps], W=[s])
            k.dma("sp", dst.t[tsl, col:col + width], s[:, 0:width], R=[s], W=[dst.bs[T]])
        plan = [(0, 512, "feat", sink_feat), (512, 512, "feat", sink_feat),
                (1024, 512, "feat", sink_feat), (1536, 512, "feat", sink_feat)]
        plan += [(c0, 512, "tok", sink_tok) for c0 in range(1024, 5120, 512)]
        plan += [(5120, 16, "tok", sink_tok)]
        self.proj(self.I["w_in_even"].t[0], 5136, plan)
    with ExitStack() as es:
        Cn = [[k.sb("Cn%d%d" % (d, h), (128, 2, 257), F32, es=es) for h in range(4)] for d in range(2)]
        Cb = [[k.sb("Cb%d%d" % (d, h), (128, 2, 257), BF16, es=es) for h in range(4)] for d in range(2)]
        mrep = k.sb("mrep", (128, 8), F32, es=es)
        keep = k.sb("keep", (128, 2, 16), F32, es=es)
        bg = k.sb("bg", (128, 16), F32, es=es)
        gcol = k.sb("gcol", (128, 8), F32, es=es)
        pscol = k.sb("pscol", (128, 8), F32, es=es)
        pw = k.sb("pw", (128, 4, 2, 256), BF16, es=es)
        k.dma("sp", mrep[:], self.I["m0"].t[:, :], W=[mrep])
        k.dma("sp", keep[:], self.I["keep"].t[:, :, :], W=[keep])
        k.dma("sp", bg[:], self.I["b_gate_even"].t[0].partition_broadcast(128), W=[bg])
        self._col_from_row(self.I["mlstm_gain"].t[0], 8, gcol, es)
        self._col_from_row(self.I["pool_scale"].t[0], 8, pscol, es)
        for g in range(4):
            for cc in range(2):
                k.dma("pool", pw[:, g, cc, :], self.I["pool_w"].t[0, g, cc * 128:(cc + 1) * 128, :], W=[pw])
        for d in range(2):
            for h in range(4):
                for cc in range(2):
                    k.dma("sp", Cn[d][h][:, cc, 0:256], self.I["C0"].t[d, h, cc * 128:(cc + 1) * 128, :], W=[Cn[d][h]])
                    k.dma("sp", Cn[d][h][:, cc, 256:257],
                          self.I["n0"].t[d, h, cc * 128:(cc + 1) * 128].rearrange("(p o) -> p o", o=1), W=[Cn[d][h]])
                k.op("dve", lambda e, d=d, h=h: e.tensor_copy(out=Cb[d][h][:], in_=Cn[d][h][:]), R=[Cn[d][h]], W=[Cb[d][h]])
        qTt = [k.sb("m_q%d" % i, (128, 8, 128), BF16, es=es) for i in range(2)]
        kTt = [k.sb("m_kT%d" % i, (128, 8, 128), BF16, es=es) for i in range(2)]
        kt = [k.sb("m_k%d" % i, (128, 1024), BF16, es=es) for i in range(2)]
        vx = [k.sb("m_v%d" % i, (128, 4, 257), BF16, es=es) for i in range(2)]
        gt = [k.sb("m_g%d" % i, (128, 16), F32, es=es) for i in range(2)]
        for i in range(2):
            k.op("pool", lambda e, i=i: e.memset(vx[i][:, :, 256:257], 1.0), W=[vx[i]])
        sm = {n: k.sb("m_" + n, (128, w), F32, es=es) for n, w in
              [("gg", 16), ("ab", 4), ("ex", 4), ("mn", 4), ("fl", 4), ("bsb", 8), ("r", 4), ("rmax", 1), ("dmax", 1),
               ("inter", 1), ("mt", 1), ("nmt", 1), ("a", 1), ("eneg", 1), ("den", 1), ("mm", 1), ("nmm", 1),
               ("dec", 1), ("ws", 1), ("mnew", 1), ("ss", 4)]}
        SH = []
        for hh in range(4):
            H = {n: k.sb("mh%d_%s" % (hh, n), (128, 1), F32, es=es) for n in
                 ("rmax", "dmax", "inter", "mt", "nmt", "a", "eneg", "den", "mm", "nmm", "dec", "ws", "mnew")}
            H["dg"] = k.sb("mh%d_dg" % hh, (128, 128), F32, es=es)
            H["dmat"] = k.sb("mh%d_dmat" % hh, (128, 128), F32, es=es)
            H["wt"] = k.sb("mh%d_w" % hh, (128, 128), F32, es=es)
            H["smat"] = k.sb("mh%d_smat" % hh, (128, 128), BF16, es=es)
            H["smT"] = k.sb("mh%d_smT" % hh, (128, 128), BF16, es=es)
            H["qca"] = k.sb("mh%d_qca" % hh, (128, 257), F32, es=es)
            H["num"] = k.sb("mh%d_num" % hh, (128, 257), F32, es=es)
            H["kws"] = k.sb("mh%d_kws" % hh, (128, 256), BF16, es=es)
            SH.append(H)
        hsum = k.sb("m_hsum", (128, 4, 256), F32, es=es)
        ot = k.sb("m_o", (128, 1024), F32, es=es)
        hn = k.sb("m_hn", (128, 1024), F32, es=es)
        hm = k.sb("m_hm", (128, 1024), BF16, es=es)
        junk = k.sb("m_junk", (128, 256), F32, es=es)
        pt = [k.sb("m_pt%d" % i, (128, 1024), F32, es=es) for i in range(3)]
        pm = k.sb("m_pm", (128, 3, 4, 128), F32, es=es)
        pld = k.sb("m_pld", (128, 2, 128), BF16, es=es)
        P = self.psb
        step = 0
        for d in range(2):
            tri = self.triF if d == 0 else self.triB
            msk = self.maskF if d == 0 else self.maskB
            for T in (range(NT) if d == 0 else range(NT - 1, -1, -1)):
                tsl = slice(T * 128, (T + 1) * 128)
                i = step % 2
                step += 1
                q_, kT_, k_, v_, g_ = qTt[i], kTt[i], kt[i], vx[i], gt[i]
                k.dma("sp", q_[:], qTs.t[:, :, tsl].rearrange("m p t -> p m t"), R=qTs.bs, W=[q_])
                k.dma("act", kT_[:], kTs.t[:, :, tsl].rearrange("m p t -> p m t"), R=kTs.bs, W=[kT_])
                k.dma("sp", k_[:], ks.t[tsl, :], R=[ks.bs[T]], W=[k_])
                k.dma("act", v_[:, :, 0:256], vs.t[tsl, :].rearrange("t (h d) -> t h d", d=256), R=[vs.bs[T]], W=[v_])
                k.dma("sp", g_[:], gs.t[tsl, :], R=[gs.bs[T]], W=[g_])
                S = sm
                k.op("dve", lambda e: e.tensor_tensor(out=S["gg"][:], in0=g_[:], in1=bg[:], op=ALU.add), R=[g_, bg], W=[S["gg"]])
                fg = S["gg"][:, 8 + d * 4:12 + d * 4]
                ig = S["gg"][:, d * 4:d * 4 + 4]
                k.op("dve", lambda e: e.tensor_scalar(out=S["ab"][:], in0=fg, scalar1=-1.0, scalar2=None, op0=ALU.mult), R=[S["gg"]], W=[S["ab"]])
                k.op("dve", lambda e: e.tensor_tensor(out=S["ab"][:], in0=S["ab"][:], in1=fg, op=ALU.max), R=[S["gg"], S["ab"]], W=[S["ab"]])
                k.op("act", lambda e: e.activation(out=S["ex"][:], in_=S["ab"][:], func=AF.Exp, scale=-1.0), R=[S["ab"]], W=[S["ex"]])
                k.op("act", lambda e: e.activation(out=S["ex"][:], in_=S["ex"][:], func=AF.Ln, bias=1.0), R=[S["ex"]], W=[S["ex"]])
                k.op("dve", lambda e: e.tensor_scalar(out=S["mn"][:], in0=fg, scalar1=0.0, scalar2=None, op0=ALU.min), R=[S["gg"]], W=[S["mn"]])
                k.op("dve", lambda e: e.tensor_tensor(out=S["fl"][:], in0=S["mn"][:], in1=S["ex"][:], op=ALU.subtract), R=[S["mn"], S["ex"]], W=[S["fl"]])
                k.op("pe", lambda e: e.matmul(P[0][:, 0:4], lhsT=tri, rhs=S["fl"][:], start=True, stop=True), R=[S["fl"], self.cst], W=[P[0]])
                k.op("pe", lambda e: e.matmul(P[0][:, 4:8], lhsT=self.ones, rhs=S["fl"][:], start=True, stop=True), R=[S["fl"], self.cst], W=[P[0]])
                k.op("dve", lambda e: e.tensor_copy(out=S["bsb"][:], in_=P[0][:, 0:8]), R=[P[0]], W=[S["bsb"]])
                k.op("dve", lambda e: e.tensor_tensor(out=S["r"][:], in0=ig, in1=S["bsb"][:, 0:4], op=ALU.subtract), R=[S["gg"], S["bsb"]], W=[S["r"]])
                if d == 1:
                    k.dma("sp", hsum[:], hF.t[tsl, :].rearrange("t (h d) -> t h d", d=256), R=[hF.bs[T]], W=[hsum])
                def head_body(h):
                    H = SH[h]
                    col = d * 4 + h
                    C_, Cb_ = Cn[d][h], Cb[d][h]
                    b_h = S["bsb"][:, h:h + 1]
                    be_h = S["bsb"][:, 4 + h:5 + h]
                    m_h = mrep[:, col:col + 1]
                    dg, dmat, wt, smat, smT, qca, num, kws = (H[n] for n in ("dg", "dmat", "wt", "smat", "smT", "qca", "num", "kws"))
                    k.op("dve", lambda e: e.tensor_scalar(out=dg[:], in0=self.ident, scalar1=S["r"][:, h:h + 1], scalar2=None, op0=ALU.mult), R=[S["r"], self.cst], W=[dg])
                    yield
                    k.op("pe", lambda e: e.matmul(P[1][:, 0:128], lhsT=self.ones, rhs=dg[:], start=True, stop=True), R=[dg, self.cst], W=[P[1]])
                    k.op("dve", lambda e: e.tensor_reduce(out=H["rmax"][:], in_=P[1][:, 0:128], axis=AX.X, op=ALU.max), R=[P[1]], W=[H["rmax"]])
                    k.op("dve", lambda e: e.scalar_tensor_tensor(out=dmat[:], in0=P[1][:, 0:128], scalar=b_h, in1=msk, op0=ALU.add, op1=ALU.add), R=[P[1], S["bsb"], self.cst], W=[dmat])
                    yield
                    k.op("dve", lambda e: e.tensor_reduce(out=H["dmax"][:], in_=dmat[:], axis=AX.X, op=ALU.max), R=[dmat], W=[H["dmax"]])
                    k.op("dve", lambda e: e.tensor_tensor(out=H["inter"][:], in0=b_h, in1=m_h, op=ALU.add), R=[S["bsb"], mrep], W=[H["inter"]])
                    yield
                    k.op("dve", lambda e: e.tensor_tensor(out=H["mt"][:], in0=H["inter"][:], in1=H["dmax"][:], op=ALU.max), R=[H["inter"], H["dmax"]], W=[H["mt"]])
                    k.op("dve", lambda e: e.tensor_scalar(out=H["nmt"][:], in0=H["mt"][:], scalar1=-1.0, scalar2=None, op0=ALU.mult), R=[H["mt"]], W=[H["nmt"]])
                    yield
                    k.op("act", lambda e: e.activation(out=wt[:], in_=dmat[:], func=AF.Exp, bias=H["nmt"][:, 0:1]), R=[dmat, H["nmt"]], W=[wt])
                    k.op("act", lambda e: e.activation(out=H["a"][:], in_=H["inter"][:], func=AF.Exp, bias=H["nmt"][:, 0:1]), R=[H["inter"], H["nmt"]], W=[H["a"]])
                    k.op("act", lambda e: e.activation(out=H["eneg"][:], in_=H["mt"][:], func=AF.Exp, scale=-1.0), R=[H["mt"]], W=[H["eneg"]])
                    yield
                    k.op("dve", lambda e: e.tensor_tensor(out=H["mm"][:], in0=m_h, in1=H["rmax"][:], op=ALU.max), R=[mrep, H["rmax"]], W=[H["mm"]])
                    k.op("dve", lambda e: e.tensor_scalar(out=H["nmm"][:], in0=H["mm"][:], scalar1=-1.0, scalar2=None, op0=ALU.mult), R=[H["mm"]], W=[H["nmm"]])
                    yield
                    k.op("act", lambda e: e.activation(out=H["dec"][:], in_=m_h, func=AF.Exp, bias=H["nmm"][:, 0:1]), R=[mrep, H["nmm"]], W=[H["dec"]])
                    k.op("act", lambda e: e.activation(out=H["ws"][:], in_=S["r"][:, h:h + 1], func=AF.Exp, bias=H["nmm"][:, 0:1]), R=[S["r"], H["nmm"]], W=[H["ws"]])
                    k.op("dve", lambda e: e.tensor_tensor(out=H["mnew"][:], in0=H["mm"][:], in1=be_h, op=ALU.add), R=[H["mm"], S["bsb"]], W=[H["mnew"]])
                    yield
                    for cc in range(2):
                        k.op("pe", lambda e, cc=cc: e.matmul(P[2][:, 0:128], lhsT=q_[:, h * 2 + cc, :], rhs=kT_[:, h * 2 + cc, :], start=(cc == 0), stop=(cc == 1)), R=[q_, kT_], W=[P[2]])
                    k.op("dve", lambda e: e.tensor_tensor(out=smat[:], in0=P[2][:, 0:128], in1=wt[:], op=ALU.mult), R=[P[2], wt], W=[smat])
                    yield
                    p3v = P[3].t[:].bitcast(BF16)
                    k.op("pe", lambda e: e.transpose(p3v[:, 0:128], smat[:], self.identb), R=[smat, self.cstb], W=[P[3]])
                    k.op("act", lambda e: e.copy(out=smT[:], in_=p3v[:, 0:128]), R=[P[3]], W=[smT])
                    yield
                    for cc in range(2):
                        k.op("pe", lambda e, cc=cc: e.matmul(P[4][:, 0:257], lhsT=q_[:, h * 2 + cc, :], rhs=Cb_[:, cc, :], start=(cc == 0), stop=(cc == 1)), R=[q_, Cb_], W=[P[4]])
                    k.op("act", lambda e: e.activation(out=qca[:], in_=P[4][:, 0:257], func=AF.Identity, scale=H["a"][:, 0:1]), R=[P[4], H["a"]], W=[qca])
                    yield
                    k.op("pe", lambda e: e.matmul(P[5][:, 0:257], lhsT=smT[:], rhs=v_[:, h, :], start=True, stop=True), R=[smT, v_], W=[P[5]])
                    k.op("dve", lambda e: e.tensor_tensor(out=num[:], in0=P[5][:, 0:257], in1=qca[:], op=ALU.add), R=[P[5], qca], W=[num])
                    yield
                    k.op("dve", lambda e: e.tensor_scalar(out=H["den"][:], in0=num[:, 256:257], scalar1=-1.0, scalar2=None, op0=ALU.mult), R=[num], W=[H["den"]])
                    k.op("dve", lambda e: e.tensor_tensor(out=H["den"][:], in0=H["den"][:], in1=num[:, 256:257], op=ALU.max), R=[num, H["den"]], W=[H["den"]])
                    k.op("dve", lambda e: e.tensor_tensor(out=H["den"][:], in0=H["den"][:], in1=H["eneg"][:], op=ALU.max), R=[H["eneg"], H["den"]], W=[H["den"]])
                    k.op("dve", lambda e: e.reciprocal(out=H["den"][:], in_=H["den"][:]), R=[H["den"]], W=[H["den"]])
                    yield
                    if d == 0:
                        k.op("dve", lambda e: e.tensor_scalar(out=hsum[:, h, :], in0=num[:, 0:256], scalar1=H["den"][:, 0:1], scalar2=None, op0=ALU.mult), R=[num, H["den"]], W=[hsum])
                    else:
                        k.op("dve", lambda e: e.scalar_tensor_tensor(out=hsum[:, h, :], in0=num[:, 0:256], scalar=H["den"][:, 0:1], in1=hsum[:, h, :], op0=ALU.mult, op1=ALU.add), R=[num, H["den"], hsum], W=[hsum])
                    k.op("dve", lambda e: e.tensor_scalar(out=kws[:], in0=k_[:, h * 256:(h + 1) * 256], scalar1=H["ws"][:, 0:1], scalar2=None, op0=ALU.mult), R=[k_, H["ws"]], W=[kws])
                    yield
                    for cc in range(2):
                        pu = P[6 + cc]
                        k.op("pe", lambda e, cc=cc, pu=pu: e.matmul(pu[:, 0:257], lhsT=kws[:, cc * 128:(cc + 1) * 128], rhs=v_[:, h, :], start=True, stop=True), R=[kws, v_], W=[pu])
                        k.op("dve", lambda e, cc=cc, pu=pu: e.scalar_tensor_tensor(out=C_[:, cc, :], in0=C_[:, cc, :], scalar=H["dec"][:, 0:1], in1=pu[:, 0:257], op0=ALU.mult, op1=ALU.add), R=[C_, H["dec"], pu], W=[C_])
                        yield
                    if (d == 0 and T % 2 == 1) or (d == 1 and T % 2 == 0):
                        slot = T // 2
                        for cc in range(2):
                            k.dma("sp", self.O["stC"].t[slot, d, h, cc * 128:(cc + 1) * 128, :], C_[:, cc, 0:256], R=[C_], W=[Buf()])
                            k.dma("sp", self.O["stn"].t[slot, d, h, cc * 128:(cc + 1) * 128].rearrange("(p o) -> p o", o=1), C_[:, cc, 256:257], R=[C_], W=[Buf()])
                        k.dma("sp", self.O["stm"].t[slot, d, h:h + 1].rearrange("(p o) -> p o", o=1), H["mnew"][0:1, 0:1], R=[H["mnew"]], W=[Buf()])
                    kf = keep[:, d, T:T + 1]
                    k.op("dve", lambda e: e.tensor_scalar(out=C_[:], in0=C_[:], scalar1=kf, scalar2=None, op0=ALU.mult), R=[C_, keep], W=[C_])
                    k.op("act", lambda e: e.copy(out=Cb_[:], in_=C_[:]), R=[C_], W=[Cb_])
                    k.op("dve", lambda e: e.tensor_tensor(out=m_h, in0=H["mnew"][:], in1=kf, op=ALU.mult), R=[H["mnew"], keep], W=[mrep])
                    yield

                gens = [head_body(h) for h in range(4)]
                while gens:
                    for g_ in list(gens):
                        try:
                            next(g_)
                        except StopIteration:
                            gens.remove(g_)
                if d == 0:
                    k.dma("sp", hF.t[tsl, :].rearrange("t (h d) -> t h d", d=256), hsum[:], R=[hsum], W=[hF.bs[T]])
                    rs = [r for r in range(3) if 0 <= T + r - 1 < NT]
                    for r in rs:
                        Tn = T + r - 1
                        k.dma("act", pt[r][:], pps.t[Tn * 128:(Tn + 1) * 128, :], R=[pps.bs[Tn]], W=[pt[r]])
                    k.dma("sp", pm[:], self.I["poolm"].t[T].rearrange("r g s t -> s r g t"), W=[pm])
                    for g in range(4):
                        for cc in range(2):
                            for r in rs:
                                k.op("pe", lambda e, g=g, cc=cc, r=r: e.matmul(P[1][:, 256 + cc * 128:256 + (cc + 1) * 128], lhsT=pt[r][:, g * 256 + cc * 128:g * 256 + (cc + 1) * 128], rhs=pm[:, r, g, :], start=(r == rs[0]), stop=(r == rs[-1])), R=[pt[r], pm], W=[P[1]])
                        k.op("act", lambda e: e.copy(out=pld[:].rearrange("p c t -> p (c t)"), in_=P[1][:, 256:512]), R=[P[1]], W=[pld])
                        for dd in range(2):
                            for cc in range(2):
                                k.op("pe", lambda e, g=g, cc=cc, dd=dd: e.matmul(P[2][:, 256:384], lhsT=pw[:, g, cc, dd * 128:(dd + 1) * 128], rhs=pld[:, cc, :], start=(cc == 0), stop=(cc == 1)), R=[pw, pld], W=[P[2]])
                            c = 8 + g * 2 + dd
                            k.op("dve", lambda e, c=c, g=g, dd=dd: e.tensor_scalar(out=self.actT.t[:, c, tsl], in0=P[2][:, 256:384], scalar1=pscol[:, g * 2 + dd:g * 2 + dd + 1], scalar2=None, op0=ALU.mult), R=[P[2], pscol], W=[self.actT.bs[T]])
                else:
                    for h in range(4):
                        k.op("act", lambda e, h=h: e.activation(out=junk[:], in_=hsum[:, h, :], func=AF.Square, accum_out=S["ss"][:, h:h + 1]), R=[hsum], W=[junk, S["ss"]])
                    self.rstd(S["ss"], 1.0 / 256)
                    k.dma("act", ot[:], os_.t[tsl, :], R=[os_.bs[T]], W=[ot])
                    k.op("act", lambda e: e.activation(out=ot[:], in_=ot[:], func=AF.Sigmoid), R=[ot], W=[ot])
                    for h in range(4):
                        k.op("dve", lambda e, h=h: e.tensor_scalar(out=hn[:, h * 256:(h + 1) * 256], in0=hsum[:, h, :], scalar1=S["ss"][:, h:h + 1], scalar2=None, op0=ALU.mult), R=[hsum, S["ss"]], W=[hn])
                    k.op("dve", lambda e: e.tensor_tensor(out=hm[:], in0=hn[:], in1=ot[:], op=ALU.mult), R=[hn, ot], W=[hm])
                    for c in range(8):
                        pb_ = P[3 + (c % 2) * 2]
                        pv = pb_.t[:].bitcast(BF16)
                        k.op("pe", lambda e, c=c, pv=pv: e.transpose(pv[:, 0:128], hm[:, c * 128:(c + 1) * 128], self.identb), R=[hm, self.cstb], W=[pb_])
                        k.op("act", lambda e, c=c, pv=pv: e.activation(out=self.actT.t[:, c, tsl], in_=pv[:, 0:128], func=AF.Identity, scale=gcol[:, c:c + 1]), R=[pb_, gcol], W=[self.actT.bs[T]])
        k.barrier()
    self.phase_outproj(self.I["w_out_even"].t[0], hold, hnew, 0)


Prog.phase_even = _phase_even


def _phase_odd(self, hold, hnew, l=1):
    k = self.k
    lam_init = 0.8 - 0.6 * math.exp(-0.3 * l)
    zq = k.dram("o_q", (TOK, 1024), F32, nbuf=16)
    zk = k.dram("o_k", (TOK, 1024), F32, nbuf=16)
    zu = k.dram("o_gu", (TOK, 1024), F32, nbuf=16)
    zv = k.dram("o_gv", (TOK, 1024), F32, nbuf=16)
    qTs = k.dram("o_qT", (8, 128, TOK), BF16, nbuf=8)
    kTs = k.dram("o_kT", (8, 128, 2560), BF16, nbuf=8)
    ck, cv = self.O["ck"], self.O["cv"]
    P = self.psb
    with ExitStack() as es:
        sf32 = [k.sb("op_f%d" % i, (128, 512), F32, es=es) for i in range(2)]
        cnt = [0]

        def sink_tok(T, ps, width, c0):
            s = sf32[cnt[0] % 2]
            cnt[0] += 1
            tsl = slice(T * 128, (T + 1) * 128)
            if cnt[0] % 2:
                k.op("act", lambda e: e.copy(out=s[:], in_=ps[:]), R=[ps], W=[s])
            else:
                k.op("dve", lambda e: e.tensor_copy(out=s[:], in_=ps[:]), R=[ps], W=[s])
            j = c0 // 1024
            col = c0 % 1024
            if j == 2:
                h0 = col // 128
                k.dma("sp", cv.t[T, h0:h0 + 4].rearrange("h t d -> t h d"), s[:].rearrange("t (h d) -> t h d", d=128),
                      R=[s], W=[cv.bs[T]])
            else:
                dst = [zq, zk, None, zu, zv][j]
                k.dma("sp", dst.t[tsl, col:col + 512], s[:], R=[s], W=[dst.bs[T]])
        self.proj(self.I["w_in_odd"].t[0], 5120, [(c0, 512, "tok", sink_tok) for c0 in range(0, 5120, 512)])
    with ExitStack() as es:
        gq = k.sb("o_gq", (128, 2, 64), F32, es=es)
        k.dma("sp", gq[:].rearrange("p a d -> p (a d)"), self.I["qk_gain"].t[0].rearrange("a d -> (a d)").partition_broadcast(128), W=[gq])
        xt = [k.sb("o_x%d" % i, (128, 16, 64), F32, es=es) for i in range(2)]
        t1 = k.sb("o_t1", (128, 16, 64), F32, es=es)
        sw = k.sb("o_sw", (128, 16, 64), F32, es=es)
        xb = k.sb("o_xb", (128, 16, 64), BF16, es=es)
        ss = k.sb("o_ss", (128, 16), F32, es=es)
        rc = k.sb("o_rc", (128, 64), F32, es=es)
        rs = k.sb("o_rs", (128, 64), F32, es=es)
        tb = [k.sb("o_tb%d" % i, (128, 128), BF16, es=es) for i in range(2)]
        ckf = k.sb("o_ckf", (128, 128), F32, es=es)
        n = 0
        t1s = [t1, k.sb("o_t1b", (128, 16, 64), F32, es=es)]
        sws = [sw, k.sb("o_swb", (128, 16, 64), F32, es=es)]
        xbs = [xb, k.sb("o_xbb", (128, 16, 64), BF16, es=es)]
        sss = [ss, k.sb("o_ssb", (128, 16), F32, es=es)]
        rcs = [rc, k.sb("o_rcb", (128, 64), F32, es=es)]
        rss = [rs, k.sb("o_rsb", (128, 64), F32, es=es)]
        tbs = [k.sb("o_tbx%d" % i, (128, 128), BF16, es=es) for i in range(4)]
        nn = [0]

        def chain(T, which, src, rc_, rs_):
            tsl = slice(T * 128, (T + 1) * 128)
            x, t1_, sw_, xb_, ss_ = xt[which], t1s[which], sws[which], xbs[which], sss[which]
            k.dma("sp", x[:].rearrange("p a d -> p (a d)"), src.t[tsl, :], R=[src.bs[T]], W=[x])
            k.op("dve", lambda e: e.tensor_tensor(out=t1_[:], in0=x[:], in1=x[:], op=ALU.mult), R=[x], W=[t1_])
            yield
            k.op("dve", lambda e: e.tensor_reduce(out=ss_[:], in_=t1_[:], axis=AX.X, op=ALU.add), R=[t1_], W=[ss_])
            yield
            k.op("dve", lambda e: e.tensor_scalar(out=ss_[:], in0=ss_[:], scalar1=1.0 / 64, scalar2=EPS, op0=ALU.mult, op1=ALU.add), R=[ss_], W=[ss_])
            yield
            k.op("act", lambda e: e.activation(out=ss_[:], in_=ss_[:], func=AF.Sqrt), R=[ss_], W=[ss_])
            yield
            k.op("dve", lambda e: e.reciprocal(out=ss_[:], in_=ss_[:]), R=[ss_], W=[ss_])
            yield
            k.op("dve", lambda e: e.tensor_tensor(out=x[:], in0=x[:], in1=ss_[:].unsqueeze(2).to_broadcast([128, 16, 64]), op=ALU.mult), R=[x, ss_], W=[x])
            yield
            k.op("dve", lambda e: e.tensor_tensor(out=x[:], in0=x[:], in1=gq[:, which, :].unsqueeze(1).to_broadcast([128, 16, 64]), op=ALU.mult), R=[x, gq], W=[x])
            yield
            xv = x[:].rearrange("p a (b c d) -> p a b c d", b=2, c=2)
            sv = sw_[:].rearrange("p a (b c d) -> p a b c d", b=2, c=2)
            for b_ in range(2):
                k.op("pool", lambda e, b_=b_: e.tensor_copy(out=sv[:, :, b_, 0, :], in_=xv[:, :, b_, 1, :]), R=[x], W=[sw_])
                k.op("pool", lambda e, b_=b_: e.tensor_copy(out=sv[:, :, b_, 1, :], in_=xv[:, :, b_, 0, :]), R=[x], W=[sw_])
            yield
            k.op("dve", lambda e: e.tensor_tensor(out=t1_[:], in0=x[:], in1=rc_[:].unsqueeze(1).to_broadcast([128, 16, 64]), op=ALU.mult), R=[x, rc_], W=[t1_])
            yield
            k.op("dve", lambda e: e.tensor_tensor(out=sw_[:], in0=sw_[:], in1=rs_[:].unsqueeze(1).to_broadcast([128, 16, 64]), op=ALU.mult), R=[sw_, rs_], W=[sw_])
            yield
            k.op("dve", lambda e: e.tensor_tensor(out=x[:], in0=t1_[:], in1=sw_[:], op=ALU.add), R=[t1_, sw_], W=[x])
            yield
            k.op("act", lambda e: e.copy(out=xb_[:], in_=x[:]), R=[x], W=[xb_])
            if which == 1:
                k.dma("sp", ck.t[T].rearrange("h t d -> t h d"), x[:].rearrange("p (h m) d -> p h (m d)", m=2), R=[x], W=[ck.bs[T]])
            yield
            dstT = kTs if which else qTs
            for h in range(8):
                pb_ = P[nn[0] % 4]
                pv = pb_.t[:].bitcast(BF16)
                t_ = tbs[nn[0] % 4]
                nn[0] += 1
                k.op("pe", lambda e, h=h, pv=pv: e.transpose(pv[:, 0:128], xb_[:, 2 * h:2 * h + 2, :].rearrange("p a d -> p (a d)"), self.identb), R=[xb_, self.cstb], W=[pb_])
                k.op("act", lambda e, pv=pv, t_=t_: e.copy(out=t_[:], in_=pv[:, 0:128]), R=[pb_], W=[t_])
                k.dma("sp", dstT.t[h, :, tsl], t_[:], R=[t_], W=[dstT.bs[h]])
                yield

        for T in range(NT):
            tsl = slice(T * 128, (T + 1) * 128)
            rc_, rs_ = rcs[T % 2], rss[T % 2]
            k.dma("sp", rc_[:], self.I["ropec"].t[tsl, :], W=[rc_])
            k.dma("sp", rs_[:], self.I["ropes"].t[tsl, :], W=[rs_])
            gens = [chain(T, 0, zq, rc_, rs_), chain(T, 1, zk, rc_, rs_)]
            while gens:
                for g_ in list(gens):
                    try:
                        next(g_)
                    except StopIteration:
                        gens.remove(g_)
        n = nn[0]
        for h in range(8):
            for j in range(4):
                pb_ = P[n % 4]
                pv = pb_.t[:].bitcast(BF16)
                t_ = tb[n % 2]
                n += 1
                k.dma("act", ckf[:], self.I["cache_k"].t[h, j * 128:(j + 1) * 128, :], W=[ckf])
                k.op("dve", lambda e: e.tensor_copy(out=xb[:, 0:2, :].rearrange("p a d -> p (a d)"), in_=ckf[:]), R=[ckf], W=[xb])
                k.op("pe", lambda e, pv=pv: e.transpose(pv[:, 0:128], xb[:, 0:2, :].rearrange("p a d -> p (a d)"), self.identb), R=[xb, self.cstb], W=[pb_])
                k.op("act", lambda e, pv=pv, t_=t_: e.copy(out=t_[:], in_=pv[:, 0:128]), R=[pb_], W=[t_])
                k.dma("sp", kTs.t[h, :, 2048 + j * 128:2048 + (j + 1) * 128], t_[:], R=[t_], W=[kTs.bs[h]])
        k.barrier()
    with ExitStack() as es:
        lamt = k.sb("a_lam", (128, 4, 64), F32, es=es)
        lp = k.sb("a_lp", (128, 2, 64), F32, es=es)
        ls = k.sb("a_ls", (128, 2), F32, es=es)
        nlam = k.sb("a_nlam", (128, 1), F32, es=es)
        k.dma("sp", lamt[:].rearrange("p a d -> p (a d)"), self.I["da_lambda"].t[0].rearrange("a d -> (a d)").partition_broadcast(128), W=[lamt])
        lv = lamt[:].rearrange("p (a b) d -> p a b d", b=2)
        k.op("dve", lambda e: e.tensor_tensor(out=lp[:], in0=lv[:, :, 0, :], in1=lv[:, :, 1, :], op=ALU.mult), R=[lamt], W=[lp])
        k.op("dve", lambda e: e.tensor_reduce(out=ls[:], in_=lp[:], axis=AX.X, op=ALU.add), R=[lp], W=[ls])
        k.op("act", lambda e: e.activation(out=ls[:], in_=ls[:], func=AF.Exp), R=[ls], W=[ls])
        k.op("dve", lambda e: e.tensor_tensor(out=nlam[:], in0=ls[:, 1:2], in1=ls[:, 0:1], op=ALU.subtract), R=[ls], W=[nlam])
        k.op("dve", lambda e: e.tensor_scalar(out=nlam[:], in0=nlam[:], scalar1=-lam_init, scalar2=None, op0=ALU.add), R=[nlam], W=[nlam])
        slcol = k.sb("a_sl", (128, 1), F32, es=es)
        self._col_from_row(self.I["da_subln"].t[0], 1, slcol, es)
        am = k.sb("a_am", (128, 16, 20), F32, es=es)
        k.dma("sp", am[:], self.I["amask"].t[:, :, :], W=[am])
        fm = k.sb("a_fm", (128, 16, 20), F32, es=es)
        k.op("dve", lambda e: e.tensor_scalar(out=fm[:], in0=am[:], scalar1=-1.0, scalar2=None, op0=ALU.is_ge), R=[am], W=[fm])
        slc = k.sb("a_slc", (128, 1), F32, es=es)
        k.op("dve", lambda e: e.tensor_scalar(out=slc[:], in0=slcol[:], scalar1=(1.0 - lam_init), scalar2=None, op0=ALU.mult), R=[slcol], W=[slc])
        kT = [k.sb("a_kT%d" % i, (128, 2560), BF16, es=es) for i in range(2)]
        qT = [k.sb("a_qT%d" % i, (128, TOK), BF16, es=es) for i in range(2)]
        vx = [k.sb("a_vx%d" % i, (128, 20, 128), BF16, es=es) for i in range(2)]
        pex = [k.sb("a_pe%d" % i, (128, 512), BF16, es=es) for i in range(3)]
        pT = [k.sb("a_pT%d" % i, (128, 4, 128), BF16, es=es) for i in range(3)]
        r0 = k.sb("a_r0", (128, 512), F32, es=es)
        r1 = k.sb("a_r1", (128, 512), F32, es=es)
        osb = k.sb("a_osb", (128, 512), F32, es=es)
        sq = k.sb("a_sq", (128, 512), F32, es=es)
        npt = 0
        for h in range(8):
            kT_, qT_, vx_ = kT[h % 2], qT[h % 2], vx[h % 2]
            k.dma("sp", kT_[:], kTs.t[h], R=[kTs.bs[h]], W=[kT_])
            k.dma("sp", qT_[:], qTs.t[h], R=[qTs.bs[h]], W=[qT_])
            k.dma("pool", vx_[:, 0:16, :], cv.t[:, h].rearrange("T t d -> t T d"), R=cv.bs, W=[vx_])
            k.dma("pool", vx_[:, 16:20, :], self.I["cache_v"].t[h].rearrange("(j t) d -> t j d", t=128), W=[vx_])
            for Tg in range(4):
                gsl = slice(Tg * 512, (Tg + 1) * 512)
                seq = [(m, kt) for m in range(2) for kt in range(20)]
                LA = 2
                for j in range(len(seq) + LA):
                    if j < len(seq):
                        m, kt = seq[j]
                        msl = slice(m * 64, (m + 1) * 64)
                        pbank = P[j % 3]
                        k.op("pe", lambda e, kt=kt, pbank=pbank, msl=msl: e.matmul(
                            pbank[:, :], lhsT=kT_[msl, kt * 128:(kt + 1) * 128], rhs=qT_[msl, gsl], start=True, stop=True),
                            R=[kT_, qT_], W=[pbank])
                    if j >= LA:
                        m, kt = seq[j - LA]
                        pbank = P[(j - LA) % 3]
                        px, p_ = pex[npt % 3], pT[npt % 3]
                        npt += 1
                        k.op("act", lambda e, pbank=pbank, px=px: e.activation(out=px[:], in_=pbank[:, :], func=AF.Exp, scale=0.125),
                             R=[pbank], W=[px])
                        k.op("dve", lambda e, kt=kt, px=px, p_=p_: e.tensor_tensor(
                            out=p_[:], in0=px[:].rearrange("p (a t) -> p a t", t=128),
                            in1=fm[:, Tg * 4:(Tg + 1) * 4, kt].unsqueeze(2).to_broadcast([128, 4, 128]), op=ALU.mult),
                            R=[px, fm], W=[p_])
                        pflat = p_[:].rearrange("p a t -> p (a t)")
                        k.op("pe", lambda e, kt=kt, pflat=pflat, m=m: e.matmul(
                            P[3 + m][:, :], lhsT=vx_[:, kt, :], rhs=pflat, start=(kt == 0), stop=(kt == 19)),
                            R=[p_, vx_], W=[P[3 + m]])
                        k.op("pe", lambda e, kt=kt, pflat=pflat, m=m: e.matmul(
                            P[5 + m][:, :], lhsT=self.onesb, rhs=pflat, start=(kt == 0), stop=(kt == 19)),
                            R=[p_, self.cstb], W=[P[5 + m]])
                k.op("dve", lambda e: e.reciprocal(out=r0[:], in_=P[5][:, :]), R=[P[5]], W=[r0])
                k.op("dve", lambda e: e.reciprocal(out=r1[:], in_=P[6][:, :]), R=[P[6]], W=[r1])
                k.op("dve", lambda e: e.tensor_scalar(out=r1[:], in0=r1[:], scalar1=nlam[:, 0:1], scalar2=None, op0=ALU.mult), R=[r1, nlam], W=[r1])
                k.op("dve", lambda e: e.tensor_tensor(out=osb[:], in0=P[3][:, :], in1=r0[:], op=ALU.mult), R=[P[3], r0], W=[osb])
                k.op("dve", lambda e: e.tensor_tensor(out=r1[:], in0=P[4][:, :], in1=r1[:], op=ALU.mult), R=[P[4], r1], W=[r1])
                k.op("dve", lambda e: e.tensor_tensor(out=osb[:], in0=osb[:], in1=r1[:], op=ALU.add), R=[osb, r1], W=[osb])
                k.op("act", lambda e: e.activation(out=sq[:], in_=osb[:], func=AF.Square), R=[osb], W=[sq])
                k.op("pe", lambda e: e.matmul(P[7][:, :], lhsT=self.ones, rhs=sq[:], start=True, stop=True), R=[sq, self.cst], W=[P[7]])
                k.op("dve", lambda e: e.tensor_scalar(out=r0[:], in0=P[7][:, :], scalar1=1.0 / 128, scalar2=EPS, op0=ALU.mult, op1=ALU.add), R=[P[7]], W=[r0])
                k.op("act", lambda e: e.activation(out=r0[:], in_=r0[:], func=AF.Sqrt), R=[r0], W=[r0])
                k.op("dve", lambda e: e.reciprocal(out=r0[:], in_=r0[:]), R=[r0], W=[r0])
                k.op("dve", lambda e: e.scalar_tensor_tensor(out=self.actT.t[:, h, gsl], in0=osb[:], scalar=slc[:, 0:1], in1=r0[:], op0=ALU.mult, op1=ALU.mult),
                     R=[osb, slc, r0], W=[self.actT.bs[Tg * 4 + i] for i in range(4)])
        k.barrier()
    with ExitStack() as es:
        wsT = k.sb("g_wsT", (128, 8, 128), BF16, es=es)
        wtmp = k.sb("g_wt", (128, 128), F32, es=es)
        bcol = k.sb("g_b", (128, 8), F32, es=es)
        k.dma("sp", bcol[:], self.I["gm_b"].t[0].rearrange("g t -> t g"), W=[bcol], allow_slow_non_contiguous=True)
        for g in range(8):
            k.dma("sp", wtmp[:], self.I["gm_ws"].t[0, g], W=[wtmp])
            k.op("pe", lambda e: e.transpose(P[0][:, 0:128], wtmp[:], self.ident), R=[wtmp, self.cst], W=[P[0]])
            k.op("act", lambda e, g=g: e.copy(out=wsT[:, g, :], in_=P[0][:, 0:128]), R=[P[0]], W=[wsT])
        gu = k.sb("g_u", (128, 1024), F32, es=es)
        gv = k.sb("g_v", (128, 8, 128), F32, es=es)
        t1 = k.sb("g_t1", (128, 8, 128), F32, es=es)
        ss = k.sb("g_ss", (128, 8), F32, es=es)
        vg = k.sb("g_vg", (128, 8, 128), BF16, es=es)
        ob = k.sb("g_ob", (128, 1024), BF16, es=es)
        for T in range(NT):
            tsl = slice(T * 128, (T + 1) * 128)
            k.dma("sp", gu[:], zu.t[tsl, :], R=[zu.bs[T]], W=[gu])
            k.dma("act", gv[:].rearrange("p g c -> p (g c)"), zv.t[tsl, :], R=[zv.bs[T]], W=[gv])
            k.op("act", lambda e: e.activation(out=gu[:], in_=gu[:], func=AF.Gelu_apprx_tanh), R=[gu], W=[gu])
            k.op("act", lambda e: e.activation(out=gv[:], in_=gv[:], func=AF.Gelu_apprx_tanh), R=[gv], W=[gv])
            k.op("dve", lambda e: e.tensor_tensor(out=t1[:], in0=gv[:], in1=gv[:], op=ALU.mult), R=[gv], W=[t1])
            k.op("dve", lambda e: e.tensor_reduce(out=ss[:], in_=t1[:], axis=AX.X, op=ALU.add), R=[t1], W=[ss])
            self.rstd(ss, 1.0 / 128)
            k.op("dve", lambda e: e.tensor_tensor(out=vg[:], in0=gv[:], in1=ss[:].unsqueeze(2).to_broadcast([128, 8, 128]), op=ALU.mult), R=[gv, ss], W=[vg])
            for g in range(8):
                pb_ = P[1 + g // 4]
                k.op("pe", lambda e, g=g, pb_=pb_: e.matmul(pb_[:, (g % 4) * 128:(g % 4 + 1) * 128], lhsT=wsT[:, g, :], rhs=vg[:, g, :], start=True, stop=True), R=[wsT, vg], W=[pb_])
            for g in range(8):
                pb_ = P[1 + g // 4]
                k.op("dve", lambda e, g=g, pb_=pb_: e.scalar_tensor_tensor(out=ob[:, g * 128:(g + 1) * 128], in0=pb_[:, (g % 4) * 128:(g % 4 + 1) * 128], scalar=bcol[:, g:g + 1], in1=gu[:, g * 128:(g + 1) * 128], op0=ALU.add, op1=ALU.mult), R=[pb_, bcol, gu], W=[ob])
            for g in range(8):
                pb_ = P[3 + g % 2]
                pv = pb_.t[:].bitcast(BF16)
                k.op("pe", lambda e, g=g, pv=pv: e.transpose(pv[:, 0:128], ob[:, g * 128:(g + 1) * 128], self.identb), R=[ob, self.cstb], W=[pb_])
                k.op("act", lambda e, g=g, pv=pv: e.copy(out=self.actT.t[:, 8 + g, tsl], in_=pv[:, 0:128]), R=[pb_], W=[self.actT.bs[T]])
        k.barrier()
    self.phase_outproj(self.I["w_out_odd"].t[0], hold, hnew, 0)


Prog.phase_odd = _phase_odd


def build_full():
    p = Prog()
    for l in range(2):
        hb = 2 * l
        p.phase_mod(l)
        p.alloc_act()
        p.phase_norm(p.h[hb], 0)
        if l == 0:
            p.phase_even(p.h[hb], p.h[hb + 1])
        else:
            p.phase_odd(p.h[hb], p.h[hb + 1])
        p.phase_norm(p.h[hb + 1], 1)
        p.phase_peer_prep(l)
        p.phase_peer_q(l)
        p.free_act()
        p.phase_peer_main(l, p.h[hb + 1], p.h[hb + 2])
    p.k.finish()
    return p
```

```python
import math
from contextlib import ExitStack
import numpy as np
import concourse.bass as bass
import concourse.mybir as mybir
from concourse.bass_utils import run_bass_kernel_spmd

F32 = mybir.dt.float32
BF16 = mybir.dt.bfloat16
AF = mybir.ActivationFunctionType
ALU = mybir.AluOpType
AX = mybir.AxisListType

D = 2048
NT = 16
TOK = 2048
EPS = 1e-6
NEG = -1.0e30
RD = 8


class Buf:
    __slots__ = ("w", "r", "name")

    def __init__(self, name=""):
        self.w = {}
        self.r = {}
        self.name = name


class Tile:
    def __init__(self, t, name, nbuf=1):
        self.t = t
        self.b = Buf(name)
        self.bs = [Buf(name + str(i)) for i in range(nbuf)] if nbuf > 1 else None

    def __getitem__(self, k):
        return self.t[k]


class KB:
    def __init__(self):
        self.nc = bass.Bass("TRN2", target_bir_lowering=False)
        nc = self.nc
        self.es = ExitStack()
        self.eng = {"pe": nc.tensor, "act": nc.scalar, "dve": nc.vector, "pool": nc.gpsimd, "sp": nc.sync}
        self.sems = {}
        self.cnt = {}
        for e in ["pe", "act", "dve", "pool"]:
            self.sems["c_" + e] = self.es.enter_context(nc.semaphore("c_" + e))
            self.cnt["c_" + e] = 0
        self.dq = ["sp", "pool", "act"]
        self.dcnt = {q: 0 for q in self.dq}
        for q in self.dq:
            for i in range(RD):
                k = "d_%s_%d" % (q, i)
                self.sems[k] = self.es.enter_context(nc.semaphore(k))
                self.cnt[k] = 0
        self.waited = {e: {} for e in self.eng}
        self.nins = 0

    def _nid(self):
        self._n = getattr(self, "_n", 0) + 1
        return self._n

    def sb(self, name, shape, dt=F32, es=None, nbuf=1):
        t = (es or self.es).enter_context(self.nc.sbuf_tensor("sb%d_%s" % (self._nid(), name), list(shape), dt))
        return Tile(t, name, nbuf)

    def ps(self, name, shape, dt=F32, es=None):
        t = (es or self.es).enter_context(self.nc.psum_tensor("pp%d_%s" % (self._nid(), name), list(shape), dt))
        return Tile(t, name)

    def dram(self, name, shape, dt=F32, kind="Internal", nbuf=1):
        t = self.nc.dram_tensor(name if kind != "Internal" else "dr%d_%s" % (self._nid(), name), list(shape), dt, kind=kind).ap()
        return Tile(t, name, nbuf)

    def _wait(self, e, key, val):
        if val <= 0:
            return
        if e == "pe" and key == "c_pe":
            return
        if self.waited[e].get(key, 0) >= val:
            return
        self.eng[e].wait_ge(self.sems[key], val)
        self.waited[e][key] = val

    def _deps(self, e, R, W):
        for b in R:
            for k, v in b.w.items():
                self._wait(e, k, v)
        own = "c_" + e
        for b in W:
            for k, v in b.w.items():
                if k != own:
                    self._wait(e, k, v)
            for k, v in b.r.items():
                self._wait(e, k, v)

    def _mark(self, key, val, R, W):
        for b in R:
            if b.r.get(key, 0) < val:
                b.r[key] = val
        for b in W:
            b.w = {key: val}
            b.r = {}

    @staticmethod
    def _bufs(xs):
        out = []
        for x in xs:
            if isinstance(x, Tile):
                out.append(x.b)
            elif x is not None:
                out.append(x)
        return out

    def op(self, e, fn, R=(), W=()):
        R = self._bufs(R)
        W = self._bufs(W)
        self._deps(e, R, W)
        key = "c_" + e
        self.cnt[key] += 1
        fn(self.eng[e]).then_inc(self.sems[key], 1)
        self._mark(key, self.cnt[key], R, W)
        self.nins += 1

    def dma(self, q, out, in_, R=(), W=(), **kw):
        if q == "act":
            q = "sp"
        R = self._bufs(R)
        W = self._bufs(W)
        self._deps(q, R, W)
        n = self.dcnt[q]
        slot = n % RD
        val = 16 * (n // RD + 1)
        key = "d_%s_%d" % (q, slot)
        self._wait(q, key, val - 16)
        self.dcnt[q] += 1
        self.cnt[key] = val
        self.eng[q].dma_start(out=out, in_=in_, **kw).then_inc(self.sems[key], 16)
        self._mark(key, val, R, W)
        self.nins += 1

    def barrier(self, engines=None):
        for e in (engines or list(self.eng)):
            for k, v in self.cnt.items():
                self._wait(e, k, v)

    def finish(self):
        self.barrier(["sp"])
        self.es.close()


W_SPEC = [
    ("norm_mix", (2, D)), ("norm_ffn", (2, D)), ("w_mod", (2, D, 6 * D)), ("b_mod", (2, 6 * D)),
    ("w_in_even", (1, D, 5136)), ("b_gate_even", (1, 16)), ("mlstm_gain", (1, 1024)),
    ("pool_w", (1, 4, 256, 256)), ("pool_scale", (1, 1024)), ("w_out_even", (1, D, D)),
    ("w_in_odd", (1, D, 5120)), ("qk_gain", (1, 2, 64)), ("da_lambda", (1, 4, 64)), ("da_subln", (1, 128)),
    ("gm_ws", (1, 8, 128, 128)), ("gm_b", (1, 8, 128)), ("w_out_odd", (1, D, D)),
    ("peer_wq", (2, D, D)), ("peer_subkeys", (2, 8, 2, 128, 128)), ("peer_u", (2, 16384, D)),
    ("peer_v", (2, 16384, D)),
]
C_SPEC = [
    ("x", (TOK, D)), ("cond_pc", (128, 16)), ("consts", (128, 8, 128)),
    ("keep", (128, 2, 16)), ("C0", (2, 4, 256, 256)), ("n0", (2, 4, 256)), ("m0", (128, 8)),
    ("amask", (128, 16, 20)), ("ropec", (TOK, 64)), ("ropes", (TOK, 64)),
    ("cache_k", (8, 512, 128)), ("cache_v", (8, 512, 128)), ("poolm", (16, 3, 4, 128, 128)),
]
O_SPEC = [
    ("y", (TOK, D)), ("stC", (8, 2, 4, 256, 256)), ("stn", (8, 2, 4, 256)), ("stm", (8, 2, 4)),
    ("ck", (16, 8, 128, 128)), ("cv", (16, 8, 128, 128)),
]


class LazyIn(dict):
    def __init__(self, k):
        super().__init__()
        self.k = k
        self.shapes = dict(W_SPEC + C_SPEC)

    def __missing__(self, name):
        t = self.k.dram(name, self.shapes[name], F32, kind="ExternalInput")
        self[name] = t
        return t


class Prog:
    def __init__(self, stages=("all",), dbg=()):
        self.k = KB()
        k = self.k
        self.stages = stages
        self.I = LazyIn(k)
        self.O = {}
        for name, shp in O_SPEC:
            self.O[name] = k.dram(name, shp, F32, kind="ExternalOutput", nbuf=16)
        self.dbg = {}
        for name, shp, dt in dbg:
            self.dbg[name] = k.dram("dbg_" + name, shp, dt, kind="ExternalOutput", nbuf=16)
        self.h = [self.I["x"]] + [k.dram("h%d" % i, (TOK, D), F32, nbuf=16) for i in range(1, 4)] + [self.O["y"]]
        self.h[0].bs = [Buf("x%d" % i) for i in range(16)]
        self.setup_consts()

    def setup_consts(self):
        k = self.k
        self.cst = k.sb("cst", (128, 8, 128), F32)
        k.dma("sp", self.cst[:], self.I["consts"][:, :, :], W=[self.cst])
        c = self.cst
        self.ident = c[:, 0, :]
        self.ones = c[:, 1, :]
        self.triF = c[:, 2, :]
        self.triB = c[:, 3, :]
        self.maskF = c[:, 4, :]
        self.maskB = c[:, 5, :]
        self.cstb = k.sb("cstb", (128, 2, 128), BF16)
        k.op("dve", lambda e: e.tensor_copy(out=self.cstb[:], in_=c[:, 0:2, :]), R=[c], W=[self.cstb])
        self.identb = self.cstb[:, 0, :]
        self.onesb = self.cstb[:, 1, :]
        self.actT = None
        self._act_es = None
        self.skT = k.sb("skT", (128, 16, 128), F32)
        self.psb = [k.ps("psb%d" % i, (128, 512), F32) for i in range(8)]
        self.condsb = k.sb("condsb", (128, 16), F32)
        k.dma("sp", self.condsb[:], self.I["cond_pc"][:, :], W=[self.condsb])
        self.sc = k.sb("sc", (128, 16), F32)
        k.op("act", lambda e: e.activation(out=self.sc[:], in_=self.condsb[:], func=AF.Silu),
             R=[self.condsb], W=[self.sc])
        self.gate = [k.sb("gateM", (128, D), F32), k.sb("gateF", (128, D), F32)]
        self.acol = [k.sb("acolM", (128, 16), F32), k.sb("acolF", (128, 16), F32)]
        self.bcol = [k.sb("bcolM", (128, 16), F32), k.sb("bcolF", (128, 16), F32)]

    def alloc_act(self):
        self._act_es = ExitStack()
        self.actT = self.k.sb("actT", (128, 16, TOK), BF16, es=self._act_es, nbuf=16)

    def free_act(self):
        self.k.barrier()
        self._act_es.close()
        self.actT = None

    def rstd(self, s, inv_n, eps=EPS):
        k = self.k
        k.op("dve", lambda e: e.tensor_scalar(out=s[:], in0=s[:], scalar1=inv_n, scalar2=eps,
                                              op0=ALU.mult, op1=ALU.add), R=[s], W=[s])
        k.op("act", lambda e: e.activation(out=s[:], in_=s[:], func=AF.Sqrt), R=[s], W=[s])
        k.op("dve", lambda e: e.reciprocal(out=s[:], in_=s[:]), R=[s], W=[s])

    def phase_mod(self, l):
        k = self.k
        with ExitStack() as es:
            wb = [k.sb("mw%d" % i, (128, 16, 512), F32, es=es) for i in range(2)]
            bb = [k.sb("mb%d" % i, (128, 512), F32, es=es) for i in range(2)]
            self.screp = k.sb("screp", (128, 16, 128), F32, es=es)
            for c_ in range(16):
                k.op("dve", lambda e, c_=c_: e.tensor_copy(out=self.screp[:, c_, :],
                                                          in_=self.sc[:, c_:c_ + 1].to_broadcast([128, 128])),
                     R=[self.sc], W=[self.screp])
            srow = k.sb("srow", (128, D), F32, es=es)
            arow = k.sb("arow", (128, D), F32, es=es)
            grow = k.sb("grow", (128, D), F32, es=es)
            wm = self.I["w_mod"].t[l].rearrange("(c p) n -> p c n", p=128)
            bm = self.I["b_mod"].t[l]
            blk = 0
            for sub in range(2):
                gsrc = self.I["norm_mix" if sub == 0 else "norm_ffn"].t[l]
                k.dma("sp", grow[:], gsrc.partition_broadcast(128), W=[grow])
                for i in range(3):
                    for j in range(4):
                        n0 = (sub * 3 + i) * D + j * 512
                        w = wb[blk % 2]
                        b = bb[blk % 2]
                        k.dma("sp", w[:, 0:8, :], wm[:, 0:8, n0:n0 + 512], W=[w])
                        k.dma("act", w[:, 8:16, :], wm[:, 8:16, n0:n0 + 512], W=[w])
                        k.dma("sp", b[:], bm[n0:n0 + 512].partition_broadcast(128), W=[b])
                        ps = self.psb[blk % 2]
                        for c in range(16):
                            k.op("pe", lambda e, c=c, ps=ps, w=w: e.matmul(ps[:], lhsT=self.screp[:, c, :], rhs=w[:, c, :],
                                                                        start=(c == 0), stop=(c == 15)),
                                 R=[self.screp, w], W=[ps])
                        dst = [srow, arow, self.gate[sub]][i]
                        k.op("dve", lambda e, ps=ps, b=b, dst=dst, j=j: e.tensor_tensor(
                            out=dst[:, j * 512:(j + 1) * 512], in0=ps[:], in1=b[:], op=ALU.add),
                            R=[ps, b], W=[dst])
                        blk += 1
                k.op("dve", lambda e: e.scalar_tensor_tensor(out=arow[:], in0=arow[:], scalar=1.0, in1=grow[:],
                                                             op0=ALU.add, op1=ALU.mult), R=[arow, grow], W=[arow])
                for src, dst in ((arow, self.acol[sub]), (srow, self.bcol[sub])):
                    for c in range(16):
                        ps = self.psb[2 + (c % 2)]
                        k.op("pe", lambda e, ps=ps, src=src, c=c: e.transpose(ps[:, 0:128], src[:, c * 128:(c + 1) * 128],
                                                                          self.ident), R=[src, self.cst], W=[ps])
                        k.op("dve", lambda e, ps=ps, dst=dst, c=c: e.tensor_copy(out=dst[:, c:c + 1], in_=ps[:, 0:1]),
                             R=[ps], W=[dst])
            k.barrier()

    def phase_norm(self, hsrc, sub):
        k = self.k
        with ExitStack() as es:
            ht = [k.sb("nh%d" % i, (128, D), F32, es=es) for i in range(2)]
            junk = k.sb("njunk", (128, D), F32, es=es)
            hs = [k.sb("nhs%d" % i, (128, D), BF16, es=es) for i in range(2)]
            ss = [k.sb("nss%d" % i, (128, 1), F32, es=es) for i in range(2)]
            pst = [k.ps("npst%d" % i, (128, 1024), BF16, es=es) for i in range(2)] if False else None
            pcnt = [0]

            def chain(T):
                h = ht[T % 2]
                s_ = ss[T % 2]
                hb = hs[T % 2]
                k.dma("sp", h[:, 0:1024], hsrc.t[T * 128:(T + 1) * 128, 0:1024], R=[hsrc.bs[T]], W=[h])
                k.dma("sp", h[:, 1024:2048], hsrc.t[T * 128:(T + 1) * 128, 1024:2048], R=[hsrc.bs[T]], W=[h])
                k.op("act", lambda e: e.activation(out=junk[:], in_=h[:], func=AF.Square, accum_out=s_[:]),
                     R=[h], W=[junk, s_])
                yield
                k.op("dve", lambda e: e.tensor_scalar(out=s_[:], in0=s_[:], scalar1=1.0 / D, scalar2=EPS,
                                                      op0=ALU.mult, op1=ALU.add), R=[s_], W=[s_])
                yield
                k.op("act", lambda e: e.activation(out=s_[:], in_=s_[:], func=AF.Sqrt), R=[s_], W=[s_])
                yield
                k.op("dve", lambda e: e.reciprocal(out=s_[:], in_=s_[:]), R=[s_], W=[s_])
                yield
                k.op("dve", lambda e: e.tensor_scalar(out=hb[:], in0=h[:], scalar1=s_[:, 0:1], scalar2=None,
                                                      op0=ALU.mult), R=[h, s_], W=[hb])
                yield
                for c in range(16):
                    ps = self.psb[pcnt[0] % 4]
                    pcnt[0] += 1
                    psv = ps.t[:].bitcast(BF16)
                    k.op("pe", lambda e, psv=psv, c=c: e.transpose(psv[:, 0:128], hb[:, c * 128:(c + 1) * 128],
                                                                  self.identb), R=[hb, self.cstb], W=[ps])
                    dst = self.actT.t[:, c, T * 128:(T + 1) * 128]
                    if c % 2 == 0:
                        k.op("act", lambda e, psv=psv, dst=dst, c=c: e.activation(
                            out=dst, in_=psv[:, 0:128], func=AF.Identity, scale=self.acol[sub][:, c:c + 1],
                            bias=self.bcol[sub][:, c:c + 1]), R=[ps, self.acol[sub], self.bcol[sub]], W=[self.actT.bs[T]])
                    else:
                        k.op("dve", lambda e, psv=psv, dst=dst, c=c: e.tensor_scalar(
                            out=dst, in0=psv[:, 0:128], scalar1=self.acol[sub][:, c:c + 1],
                            scalar2=self.bcol[sub][:, c:c + 1], op0=ALU.mult, op1=ALU.add),
                            R=[ps, self.acol[sub], self.bcol[sub]], W=[self.actT.bs[T]])
                    yield

            for T0 in range(0, NT, 2):
                gens = [chain(T0), chain(T0 + 1)]
                while gens:
                    for g_ in list(gens):
                        try:
                            next(g_)
                        except StopIteration:
                            gens.remove(g_)
            k.barrier()

    def proj(self, wsrc, ncols, col_plan, es_outer=None):
        k = self.k
        with ExitStack() as es:
            wb = [k.sb("pw%d" % i, (128, 16, 512), BF16, es=es) for i in range(2)]
            wv = wsrc.rearrange("(c p) n -> p c n", p=128)
            bi = 0
            pi = 0
            for (c0, width, mode, sink) in col_plan:
                w = wb[bi % 2]
                bi += 1
                for cc in range(0, 16, 4):
                    k.dma("pool", w[:, cc:cc + 4, 0:width], wv[:, cc:cc + 4, c0:c0 + width], W=[w])
                if mode == "tok":
                    for T in range(NT):
                        ps = self.psb[4 + pi % 4]
                        pi += 1
                        for c in range(16):
                            k.op("pe", lambda e, ps=ps, w=w, c=c, T=T: e.matmul(
                                ps[:, 0:width], lhsT=self.actT.t[:, c, T * 128:(T + 1) * 128], rhs=w[:, c, 0:width],
                                start=(c == 0), stop=(c == 15)), R=[self.actT.bs[T], w], W=[ps])
                        sink(T, ps, width, c0)
                else:
                    for m in range(width // 128):
                        for tg in range(4):
                            ps = self.psb[4 + pi % 4]
                            pi += 1
                            for c in range(16):
                                k.op("pe", lambda e, ps=ps, w=w, c=c, m=m, tg=tg: e.matmul(
                                    ps[:, :], lhsT=w[:, c, m * 128:(m + 1) * 128],
                                    rhs=self.actT.t[:, c, tg * 512:(tg + 1) * 512],
                                    start=(c == 0), stop=(c == 15)),
                                    R=[self.actT.bs[tg * 4 + i] for i in range(4)] + [w], W=[ps])
                            sink(m, tg, ps, c0)
            k.barrier()


def build_program(stages=("all",), dbg=()):
    p = Prog(stages, dbg)
    return p


def make_consts():
    c = np.zeros((128, 8, 128), np.float32)
    i = np.arange(128)
    c[:, 0, :] = np.eye(128)
    c[:, 1, :] = 1.0
    c[:, 2, :] = (i[:, None] <= i[None, :])
    c[:, 3, :] = (i[:, None] >= i[None, :])
    c[:, 4, :] = np.where(i[None, :] <= i[:, None], 0.0, NEG)
    c[:, 5, :] = np.where(i[None, :] >= i[:, None], 0.0, NEG)
    return c


def core_inputs(inp, core):
    prompt = core < 4
    m = {}
    f32 = np.float32
    if prompt:
        m["x"] = np.ascontiguousarray(inp["x_prompt"][8 * core:8 * core + 8].reshape(TOK, D))
        cond = inp["c_ctx"]
        L = 256
    else:
        b = core - 4
        m["x"] = np.ascontiguousarray(inp["x_sample"][b])
        cond = inp["c"][b]
        L = 2048
    m["cond_pc"] = np.ascontiguousarray(np.asarray(cond).reshape(16, 128).T)
    m["consts"] = make_consts()
    keep = np.ones((128, 2, 16), f32)
    T = np.arange(16)
    if prompt:
        keep[:, 0, :] = (T % 2 == 0)
        keep[:, 1, :] = (T % 2 == 1)
        m["C0"] = np.zeros((2, 4, 256, 256), f32)
        m["n0"] = np.zeros((2, 4, 256), f32)
        m["m0"] = np.zeros((128, 8), f32)
        m["cache_k"] = np.zeros((8, 512, 128), f32)
        m["cache_v"] = np.zeros((8, 512, 128), f32)
        am = np.full((16, 20), -30000.0, f32)
        for t in range(16):
            am[t, (t // 2) * 2:(t // 2) * 2 + 2] = 0.0
        m["ropec"] = np.ones((TOK, 64), f32)
        m["ropes"] = np.zeros((TOK, 64), f32)
    else:
        m["C0"] = np.ascontiguousarray(inp["state_mlstm_C"][b, 0])
        m["n0"] = np.ascontiguousarray(inp["state_mlstm_n"][b, 0])
        m["m0"] = np.ascontiguousarray(np.broadcast_to(np.asarray(inp["state_mlstm_m"][b, 0]).reshape(1, 8), (128, 8)))
        m["cache_k"] = np.ascontiguousarray(inp["cache_da_k"][b, 0])
        m["cache_v"] = np.ascontiguousarray(inp["cache_da_v"][b, 0])
        am = np.zeros((16, 20), f32)
        t = np.arange(TOK)
        row = (t // 64).astype(f32)
        col = (t % 64).astype(f32)
        inv = (np.float32(10000.0) ** (-np.arange(16, dtype=f32) / np.float32(16))).astype(f32)
        ar = (row[:, None] * inv).astype(f32)
        ac = (col[:, None] * inv).astype(f32)
        m["ropec"] = np.concatenate([np.cos(ar), np.cos(ar), np.cos(ac), np.cos(ac)], 1).astype(f32)
        m["ropes"] = np.concatenate([-np.sin(ar), np.sin(ar), -np.sin(ac), np.sin(ac)], 1).astype(f32)
    m["keep"] = keep
    m["amask"] = np.ascontiguousarray(np.broadcast_to(am[None], (128, 16, 20)))
    m["poolm"] = make_poolm(L)
    return m


_POOLM = {}


def make_poolm(L):
    if L in _POOLM:
        return _POOLM[L]
    pm = np.zeros((16, 3, 4, 128, 128), np.float32)
    pos = np.arange(TOK)
    seq0 = (pos // L) * L
    for g, w in enumerate((2, 4, 8, 16)):
        p = pos - seq0
        lo = np.clip(p - w // 2, 0, L) + seq0
        hi = np.clip(p - w // 2 + w, 0, L) + seq0
        cnt = (hi - lo).astype(np.float32)
        A = np.zeros((TOK, TOK), np.float32)
        for t in range(TOK):
            A[lo[t]:hi[t], t] = np.float32(1.0) / cnt[t]
            A[t, t] -= 1.0
        for T in range(16):
            for r in range(3):
                Tn = T + r - 1
                if 0 <= Tn < 16:
                    pm[T, r, g] = A[Tn * 128:(Tn + 1) * 128, T * 128:(T + 1) * 128]
    _POOLM[L] = pm
    return pm


_PROG = {}


def kernel(**inputs):
    inp = {k_: np.asarray(v) for k_, v in inputs.items()}
    if "p" not in _PROG:
        _PROG["p"] = build_full()
    p = _PROG["p"]
    wnames = [n for n, _ in W_SPEC]
    in_maps = []
    for core in range(8):
        m = core_inputs(inp, core)
        for n in wnames:
            m[n] = inp[n]
        in_maps.append({n: np.ascontiguousarray(m[n], dtype=np.float32) for n in p.I})
    res = run_bass_kernel_spmd(p.k.nc, in_maps, core_ids=list(range(8)))
    r = res.results
    y_prompt = np.concatenate([r[c]["y"].reshape(8, 256, D) for c in range(4)], 0)
    y_sample = np.stack([r[c]["y"] for c in range(4, 8)], 0)
    nC = np.concatenate([r[c]["stC"] for c in range(4)], 0)[:, None]
    nn = np.concatenate([r[c]["stn"] for c in range(4)], 0)[:, None]
    nm = np.concatenate([r[c]["stm"] for c in range(4)], 0)[:, None]

    def cache(name):
        out = []
        for c in range(4):
            a = r[c][name].reshape(8, 2, 8, 128, 128).transpose(0, 2, 1, 3, 4).reshape(8, 8, 256, 128)
            out.append(a)
        return np.concatenate(out, 0)[:, None]
    f = lambda a: np.ascontiguousarray(a, dtype=np.float32)
    return (f(y_prompt), f(y_sample), f(nC), f(nn), f(nm), f(cache("ck")), f(cache("cv")))


def _col_from_row(self, row_ap, n, dst, es):
    k = self.k
    tmp = k.sb("cfr_%d" % self._uid(), (128, n * 128), F32, es=es)
    k.dma("sp", tmp[:], row_ap.partition_broadcast(128), W=[tmp])
    for c in range(n):
        ps = self.psb[c % 2]
        k.op("pe", lambda e, ps=ps, c=c: e.transpose(ps[:, 0:128], tmp[:, c * 128:(c + 1) * 128], self.ident),
             R=[tmp, self.cst], W=[ps])
        k.op("dve", lambda e, ps=ps, c=c: e.tensor_copy(out=dst[:, c:c + 1], in_=ps[:, 0:1]), R=[ps], W=[dst])


def _uid(self):
    self._u = getattr(self, "_u", 0) + 1
    return self._u


def _residual_sink(self, hold, hnew, sub, es):
    k = self.k
    hb = [k.sb("rs_h%d_%d" % (i, self._uid()), (128, 512), F32, es=es) for i in range(2)]
    tb = [k.sb("rs_t%d_%d" % (i, self._uid()), (128, 512), F32, es=es) for i in range(2)]
    cnt = [0]

    def sink(T, ps, width, c0):
        i = cnt[0] % 2
        cnt[0] += 1
        h, t = hb[i], tb[i]
        k.dma("sp", h[:, 0:width], hold.t[T * 128:(T + 1) * 128, c0:c0 + width], R=[hold.bs[T]], W=[h])
        k.op("dve", lambda e: e.tensor_tensor(out=t[:, 0:width], in0=ps[:, 0:width],
                                              in1=self.gate[sub][:, c0:c0 + width], op=ALU.mult),
             R=[ps, self.gate[sub]], W=[t])
        k.op("pool", lambda e: e.tensor_tensor(out=t[:, 0:width], in0=t[:, 0:width], in1=h[:, 0:width], op=ALU.add),
             R=[t, h], W=[t])
        k.dma("sp", hnew.t[T * 128:(T + 1) * 128, c0:c0 + width], t[:, 0:width], R=[t], W=[hnew.bs[T]])
    return sink


def _phase_outproj(self, wsrc, hold, hnew, sub):
    with ExitStack() as es:
        sink = self._residual_sink(hold, hnew, sub, es)
        self.proj(wsrc, D, [(j * 512, 512, "tok", sink) for j in range(4)])


Prog._col_from_row = _col_from_row
Prog._uid = _uid
Prog._residual_sink = _residual_sink
Prog.phase_outproj = _phase_outproj


def _top16(self, src_ap, dst16, scr, R, es_bufs):
    k = self.k
    k.op("dve", lambda e: e.max(out=dst16[:, 0:8], in_=src_ap), R=R, W=[dst16])
    k.op("dve", lambda e: e.match_replace(out=scr[:], in_to_replace=dst16[:, 0:8], in_values=src_ap, imm_value=-1.0),
         R=R + [dst16], W=[scr])
    k.op("dve", lambda e: e.max(out=dst16[:, 8:16], in_=scr[:]), R=[scr], W=[dst16])


Prog._top16 = _top16


def _phase_peer_prep(self, l):
    k = self.k
    if not hasattr(self, "GS"):
        self.GS = k.dram("GS", (128, 128, TOK), BF16, nbuf=128)
        self.VB = k.dram("VB", (128, 128, D), BF16, nbuf=128)
        self.qTs = k.dram("qTs", (16, 128, TOK), F32, nbuf=16)
    U = self.I["peer_u"].t[l].rearrange("(i j) d -> i j d", j=128)
    V = self.I["peer_v"].t[l].rearrange("(i j) d -> i j d", j=128)
    P = self.psb
    with ExitStack() as es:
        ub = [k.sb("pu%d" % i, (128, D), BF16, es=es) for i in range(2)]
        vb = [k.sb("pv%d" % i, (128, D), BF16, es=es) for i in range(2)]
        ut = [k.sb("put%d" % i, (128, 16, 128), BF16, es=es) for i in range(2)]
        gsb = [k.sb("pgs%d" % i, (128, TOK), BF16, es=es) for i in range(2)]
        sk = k.sb("psk", (128, 128), F32, es=es)
        for m in range(16):
            k.dma("sp", sk[:], self.I["peer_subkeys"].t[l, m // 2, m % 2], W=[sk])
            ps = P[m % 2]
            k.op("pe", lambda e, ps=ps: e.transpose(ps[:, 0:128], sk[:], self.ident), R=[sk, self.cst], W=[ps])
            k.op("act", lambda e, ps=ps, m=m: e.copy(out=self.skT[:, m, :], in_=ps[:, 0:128]), R=[ps], W=[self.skT])
        nh = 0
        for i in range(128):
            u, v, t, g = ub[i % 2], vb[i % 2], ut[i % 2], gsb[i % 2]
            k.dma("pool", u[:], U[i], W=[u])
            k.dma("pool", v[:], V[i], W=[v])
            k.dma("act", self.VB.t[i], v[:], R=[v], W=[self.VB.bs[i]])
            for c in range(16):
                ps = P[c // 8]
                psv = ps.t[:].bitcast(BF16)
                k.op("pe", lambda e, psv=psv, u=u, c=c: e.transpose(psv[:, (c % 8) * 128:(c % 8 + 1) * 128],
                                                                   u[:, c * 128:(c + 1) * 128], self.identb),
                     R=[u, self.cstb], W=[ps])
                if c % 8 == 7:
                    dst = t[:, c - 7:c + 1, :]
                    src = psv[:, 0:1024].rearrange("p (c j) -> p c j", j=128)
                    if c == 7:
                        k.op("act", lambda e, src=src, dst=dst: e.copy(out=dst, in_=src), R=[ps], W=[t])
                    else:
                        k.op("dve", lambda e, src=src, dst=dst: e.tensor_copy(out=dst, in_=src), R=[ps], W=[t])
            for half in range(2):
                pa, pb2 = P[2 + 2 * (nh % 3)], P[3 + 2 * (nh % 3)]
                nh += 1
                for q4, pbank in ((0, pa), (1, pb2)):
                    tg = half * 2 + q4
                    for c in range(16):
                        k.op("pe", lambda e, c=c, tg=tg, pbank=pbank, t=t: e.matmul(
                            pbank[:, :], lhsT=t[:, c, :], rhs=self.actT.t[:, c, tg * 512:(tg + 1) * 512],
                            start=(c == 0), stop=(c == 15)),
                            R=[t] + [self.actT.bs[tg * 4 + x] for x in range(4)], W=[pbank])
                    k.op("act", lambda e, tg=tg, pbank=pbank, g=g: e.activation(
                        out=g[:, tg * 512:(tg + 1) * 512], in_=pbank[:, :], func=AF.Gelu_apprx_tanh), R=[pbank], W=[g])
            k.dma("sp", self.GS.t[i], g[:], R=[g], W=[self.GS.bs[i]])
        k.barrier()


def _phase_peer_q(self, l):
    k = self.k
    with ExitStack() as es:
        st = [k.sb("pq%d" % i, (128, 512), F32, es=es) for i in range(2)]
        cnt = [0]

        def sink(m, tg, ps, c0):
            s = st[cnt[0] % 2]
            cnt[0] += 1
            mm = c0 // 128 + m
            eng = "act" if cnt[0] % 2 else "dve"
            if eng == "act":
                k.op("act", lambda e: e.copy(out=s[:], in_=ps[:]), R=[ps], W=[s])
            else:
                k.op("dve", lambda e: e.tensor_copy(out=s[:], in_=ps[:]), R=[ps], W=[s])
            k.dma("sp", self.qTs.t[mm, :, tg * 512:(tg + 1) * 512], s[:], R=[s], W=[self.qTs.bs[mm]])
        self.proj(self.I["peer_wq"].t[l], D, [(j * 512, 512, "feat", sink) for j in range(4)])


def _phase_peer_main(self, l, hold, hnew, tiles=None):
    k = self.k
    IB = 4
    NBLK = 128 // IB
    tiles = list(tiles if tiles is not None else range(NT))
    with ExitStack() as es:
        sets = []
        for si in range(2):
            S = {}
            S["qt"] = k.sb("eq%d" % si, (128, 16, 128), F32, es=es)
            S["s_all"] = k.sb("es%d" % si, (128, 16, 128), F32, es=es)
            S["e_all"] = k.sb("ee%d" % si, (128, 16, 128), F32, es=es)
            S["mx"] = k.sb("emx%d" % si, (128, 16), F32, es=es)
            S["ev"] = k.sb("eev%d" % si, (128, 16, 16), F32, es=es)
            S["ct"] = k.sb("ect%d" % si, (128, 8, 16), F32, es=es)
            S["rz"] = k.sb("erz%d" % si, (128, 8), F32, es=es)
            S["dg"] = k.sb("edg%d" % si, (128, 8, 128), BF16, es=es)
            sets.append(S)
        scr_sh = k.sb("escr", (128, 256), F32, es=es)
        cE_sh = k.sb("ecE", (128, 8, 256), F32, es=es)
        for S in sets:
            S["scr"] = scr_sh
            S["cE"] = cE_sh
        HA = 6
        Ea = [k.sb("eEa%d" % i, (128, HA, IB, 128), F32, es=es) for i in range(2)]
        Ed = [k.sb("eEd%d" % i, (128, 8 - HA, IB, 128), F32, es=es) for i in range(2)]
        Gb = [k.sb("eG%d" % i, (128, 8, IB, 128), BF16, es=es) for i in range(2)]
        vb = [k.sb("ev%d" % i, (128, IB, D), BF16, es=es) for i in range(3)]
        ge = [k.sb("ege%d" % i, (128, IB, 128), BF16, es=es) for i in range(3)]
        at = [k.sb("eat%d" % i, (128, IB, 128), BF16, es=es) for i in range(2)]
        sink = self._residual_sink(hold, hnew, 1, es)
        psO = self.psb[0:4]
        psG = self.psb[4:6]
        psS = self.psb[6:8]

        def preamble(T, S):
            tsl = slice(T * 128, (T + 1) * 128)
            qt, s_all, e_all, scr, mx, ev, cE, ct, rz, dg = (S[n] for n in ("qt", "s_all", "e_all", "scr", "mx", "ev", "cE", "ct", "rz", "dg"))
            k.dma("sp", qt[:], self.qTs.t[:, :, tsl].rearrange("m p t -> p m t"), R=self.qTs.bs, W=[qt])
            for half in range(2):
                for mm in range(8):
                    m = half * 8 + mm
                    ps = psS[mm // 4]
                    k.op("pe", lambda e, ps=ps, m=m, mm=mm: e.matmul(ps[:, (mm % 4) * 128:(mm % 4 + 1) * 128], lhsT=qt[:, m, :],
                                                              rhs=self.skT[:, m, :], start=True, stop=True),
                         R=[qt, self.skT], W=[ps])
                for g in range(2):
                    k.op("act", lambda e, g=g, half=half: e.copy(out=s_all[:, half * 8 + g * 4:half * 8 + (g + 1) * 4, :],
                                                                 in_=psS[g][:, :].rearrange("p (m k) -> p m k", k=128)),
                         R=[psS[g]], W=[s_all])
            k.op("dve", lambda e: e.tensor_reduce(out=mx[:], in_=s_all[:], axis=AX.X, op=ALU.max), R=[s_all], W=[mx])
            k.op("dve", lambda e: e.tensor_scalar(out=mx[:], in0=mx[:], scalar1=-1.0, scalar2=None, op0=ALU.mult),
                 R=[mx], W=[mx])
            for m in range(16):
                k.op("act", lambda e, m=m: e.activation(out=e_all[:, m, :], in_=s_all[:, m, :], func=AF.Exp,
                                                        bias=mx[:, m:m + 1]), R=[s_all, mx], W=[e_all])
            for m in range(16):
                k.op("dve", lambda e, m=m: e.max(out=ev[:, m, 0:8], in_=e_all[:, m, :]), R=[e_all], W=[ev])
                k.op("dve", lambda e, m=m: e.match_replace(out=scr[:, 0:128], in_to_replace=ev[:, m, 0:8],
                                                           in_values=e_all[:, m, :], imm_value=-1.0),
                     R=[e_all, ev], W=[scr])
                k.op("dve", lambda e, m=m: e.max(out=ev[:, m, 8:16], in_=scr[:, 0:128]), R=[scr], W=[ev])
                k.op("dve", lambda e, m=m: e.scalar_tensor_tensor(out=e_all[:, m, :], in0=e_all[:, m, :],
                                                                  scalar=ev[:, m, 15:16], in1=e_all[:, m, :],
                                                                  op0=ALU.is_ge, op1=ALU.mult),
                     R=[e_all, ev], W=[e_all])
            for h in range(HA):
                for a_ in range(16):
                    k.op("act", lambda e, h=h, a_=a_: e.activation(
                        out=cE[:, h, a_ * 16:(a_ + 1) * 16], in_=ev[:, 2 * h + 1, :], func=AF.Identity,
                        scale=ev[:, 2 * h, a_:a_ + 1]), R=[ev], W=[cE])
            for h in range(8):
                if h >= HA:
                    k.op("dve", lambda e, h=h: e.tensor_tensor(
                        out=cE[:, h, :].rearrange("p (a b) -> p a b", b=16),
                        in0=ev[:, 2 * h, :].unsqueeze(2).to_broadcast([128, 16, 16]),
                        in1=ev[:, 2 * h + 1, :].unsqueeze(1).to_broadcast([128, 16, 16]), op=ALU.mult),
                        R=[ev], W=[cE])
                k.op("dve", lambda e, h=h: e.max(out=ct[:, h, 0:8], in_=cE[:, h, :]), R=[cE], W=[ct])
                k.op("dve", lambda e, h=h: e.match_replace(out=scr[:], in_to_replace=ct[:, h, 0:8], in_values=cE[:, h, :],
                                                           imm_value=-1.0), R=[cE, ct], W=[scr])
                k.op("dve", lambda e, h=h: e.max(out=ct[:, h, 8:16], in_=scr[:]), R=[scr], W=[ct])
            k.op("dve", lambda e: e.tensor_reduce(out=rz[:], in_=ct[:], axis=AX.X, op=ALU.add), R=[ct], W=[rz])
            k.op("dve", lambda e: e.reciprocal(out=rz[:], in_=rz[:]), R=[rz], W=[rz])
            for h in range(8):
                k.op("dve", lambda e, h=h: e.tensor_scalar(out=dg[:, h, :], in0=self.ident, scalar1=rz[:, h:h + 1],
                                                           scalar2=None, op0=ALU.mult), R=[rz, self.cst], W=[dg])
            if "pe_s" in self.dbg and T == 0:
                k.dma("sp", self.dbg["pe_s"].t[:, :, :], s_all[:], R=[s_all], W=[self.dbg["pe_s"]])
                k.dma("sp", self.dbg["pe_e"].t[:, :, :], e_all[:], R=[e_all], W=[self.dbg["pe_e"]])
                k.dma("sp", self.dbg["pe_ct"].t[:, :, :], ct[:], R=[ct], W=[self.dbg["pe_ct"]])
                k.dma("sp", self.dbg["pe_rz"].t[:, :], rz[:], R=[rz], W=[self.dbg["pe_rz"]])

        nb = [0]

        def front(T, S, ib):
            tsl = slice(T * 128, (T + 1) * 128)
            e_all, ct, dg = S["e_all"], S["ct"], S["dg"]
            n = nb[0]
            nb[0] += 1
            v, gt = vb[n % 3], ge[n % 3]
            pg, a_t = psG[n % 2], at[n % 2]
            EA, ED, G = Ea[n % 2], Ed[n % 2], Gb[n % 2]
            i0 = ib * IB
            e4 = e_all[:].rearrange("p (h q) k -> p h q k", q=2)
            k.dma("sp", gt[:], self.GS.t[i0:i0 + IB, :, tsl].rearrange("i j t -> j i t"),
                  R=self.GS.bs[i0:i0 + IB], W=[gt])
            k.dma("act", v[:], self.VB.t[i0:i0 + IB].rearrange("i j d -> j i d"),
                  R=self.VB.bs[i0:i0 + IB], W=[v])
            for h in range(HA):
                for ii in range(IB):
                    k.op("act", lambda e, h=h, ii=ii: e.activation(
                        out=EA[:, h, ii, :], in_=e4[:, h, 1, :], func=AF.Identity,
                        scale=e4[:, h, 0, i0 + ii:i0 + ii + 1]), R=[e_all], W=[EA])
            k.op("dve", lambda e: e.tensor_tensor(
                out=ED[:], in0=e4[:, HA:8, 0, i0:i0 + IB].unsqueeze(3).to_broadcast([128, 8 - HA, IB, 128]),
                in1=e4[:, HA:8, 1, :].unsqueeze(2).to_broadcast([128, 8 - HA, IB, 128]), op=ALU.mult),
                R=[e_all], W=[ED])
            for h in range(8):
                Eh = EA[:, h] if h < HA else ED[:, h - HA]
                Et = EA if h < HA else ED
                k.op("dve", lambda e, h=h, Eh=Eh: e.scalar_tensor_tensor(
                    out=G[:, h], in0=Eh, scalar=ct[:, h, 15:16], in1=Eh, op0=ALU.is_ge, op1=ALU.mult),
                    R=[Et, ct], W=[G])
            for ii in range(IB):
                for h in range(8):
                    k.op("pe", lambda e, h=h, ii=ii: e.matmul(
                        pg[:, ii * 128:(ii + 1) * 128], lhsT=G[:, h, ii, :], rhs=dg[:, h, :],
                        start=(h == 0), stop=(h == 7)), R=[G, dg], W=[pg])
            return (i0, v, gt, pg, a_t)

        def back(st):
            i0, v, gt, pg, a_t = st
            k.op("dve", lambda e: e.tensor_tensor(
                out=a_t[:].rearrange("p i t -> p (i t)"), in0=gt[:].rearrange("p i t -> p (i t)"),
                in1=pg[:, 0:IB * 128], op=ALU.mult), R=[gt, pg], W=[a_t])
            for ii in range(IB):
                i = i0 + ii
                for dblk in range(4):
                    k.op("pe", lambda e, ii=ii, dblk=dblk, i=i: e.matmul(
                        psO[dblk][:, :], lhsT=a_t[:, ii, :], rhs=v[:, ii, dblk * 512:(dblk + 1) * 512],
                        start=(i == 0), stop=(i == 127), skip_group_check=True), R=[a_t, v], W=[psO[dblk]])

        preamble(tiles[0], sets[0])
        for ti, T in enumerate(tiles):
            S = sets[ti % 2]
            pending = None
            for ib in range(NBLK):
                st = front(T, S, ib)
                if pending is not None:
                    back(pending)
                pending = st
                if ib == NBLK // 2 and ti + 1 < len(tiles):
                    preamble(tiles[ti + 1], sets[(ti + 1) % 2])
            back(pending)
            for dblk in range(4):
                sink(T, psO[dblk], 512, dblk * 512)
        k.barrier()


Prog.phase_peer_prep = _phase_peer_prep
Prog.phase_peer_q = _phase_peer_q
Prog.phase_peer_main = _phase_peer_main


def _phase_even(self, hold, hnew):
    k = self.k
    qTs = k.dram("e_qT", (8, 128, TOK), BF16, nbuf=8)
    kTs = k.dram("e_kT", (8, 128, TOK), BF16, nbuf=8)
    ks = k.dram("e_k", (TOK, 1024), BF16, nbuf=16)
    vs = k.dram("e_v", (TOK, 1024), BF16, nbuf=16)
    os_ = k.dram("e_o", (TOK, 1024), F32, nbuf=16)
    pps = k.dram("e_p", (TOK, 1024), F32, nbuf=16)
    gs = k.dram("e_g", (TOK, 16), F32, nbuf=16)
    hF = k.dram("e_hF", (TOK, 1024), F32, nbuf=16)
    with ExitStack() as es:
        sb16 = [k.sb("ep_b%d" % i, (128, 512), BF16, es=es) for i in range(2)]
        sf32 = [k.sb("ep_f%d" % i, (128, 512), F32, es=es) for i in range(2)]
        cnt = [0]

        def sink_feat(m, tg, ps, c0):
            s = sb16[cnt[0] % 2]
            cnt[0] += 1
            isk = c0 >= 1024
            dst = kTs if isk else qTs
            mm = (c0 - (1024 if isk else 0)) // 128 + m
            k.op("act", lambda e: e.activation(out=s[:], in_=ps[:], func=AF.Identity, scale=(0.0625 if isk else 1.0)),
                 R=[ps], W=[s])
            k.dma("sp", dst.t[mm, :, tg * 512:(tg + 1) * 512], s[:], R=[s], W=[dst.bs[mm]])

        def sink_tok(T, ps, width, c0):
            i = cnt[0] % 2
            cnt[0] += 1
            tsl = slice(T * 128, (T + 1) * 128)
            if c0 < 3072:
                s = sb16[i]
                dst, col, sc = (ks, c0 - 1024, 0.0625) if c0 < 2048 else (vs, c0 - 2048, 1.0)
                k.op("act", lambda e: e.activation(out=s[:, 0:width], in_=ps[:, 0:width], func=AF.Identity, scale=sc),
                     R=[ps], W=[s])
            else:
                s = sf32[i]
                dst, col = (os_, c0 - 3072) if c0 < 4096 else ((pps, c0 - 4096) if c0 < 5120 else (gs, 0))
                k.op("dve", lambda e: e.tensor_copy(out=s[:, 0:width], in_=ps[:, 0:width]), R=[ps], W=[s])
            k.dma("sp", dst.t[tsl, col:col + width], s[:, 0:width], R=[s], W=[dst.bs[T]])
        plan = [(0, 512, "feat", sink_feat), (512, 512, "feat", sink_feat),
                (1024, 512, "feat", sink_feat), (1536, 512, "feat", sink_feat)]
        plan += [(c0, 512, "tok", sink_tok) for c0 in range(1024, 5120, 512)]
        plan += [(5120, 16, "tok", sink_tok)]
        self.proj(self.I["w_in_even"].t[0], 5136, plan)
    with ExitStack() as es:
        Cn = [[k.sb("Cn%d%d" % (d, h), (128, 2, 257), F32, es=es) for h in range(4)] for d in range(2)]
        Cb = [[k.sb("Cb%d%d" % (d, h), (128, 2, 257), BF16, es=es) for h in range(4)] for d in range(2)]
        mrep = k.sb("mrep", (128, 8), F32, es=es)
        keep = k.sb("keep", (128, 2, 16), F32, es=es)
        bg = k.sb("bg", (128, 16), F32, es=es)
        gcol = k.sb("gcol", (128, 8), F32, es=es)
        pscol = k.sb("pscol", (128, 8), F32, es=es)
        pw = k.sb("pw", (128, 4, 2, 256), BF16, es=es)
        k.dma("sp", mrep[:], self.I["m0"].t[:, :], W=[mrep])
        k.dma("sp", keep[:], self.I["keep"].t[:, :, :], W=[keep])
        k.dma("sp", bg[:], self.I["b_gate_even"].t[0].partition_broadcast(128), W=[bg])
        self._col_from_row(self.I["mlstm_gain"].t[0], 8, gcol, es)
        self._col_from_row(self.I["pool_scale"].t[0], 8, pscol, es)
        for g in range(4):
            for cc in range(2):
                k.dma("pool", pw[:, g, cc, :], self.I["pool_w"].t[0, g, cc * 128:(cc + 1) * 128, :], W=[pw])
        for d in range(2):
            for h in range(4):
                for cc in range(2):
                    k.dma("sp", Cn[d][h][:, cc, 0:256], self.I["C0"].t[d, h, cc * 128:(cc + 1) * 128, :], W=[Cn[d][h]])
                    k.dma("sp", Cn[d][h][:, cc, 256:257],
                          self.I["n0"].t[d, h, cc * 128:(cc + 1) * 128].rearrange("(p o) -> p o", o=1), W=[Cn[d][h]])
                k.op("dve", lambda e, d=d, h=h: e.tensor_copy(out=Cb[d][h][:], in_=Cn[d][h][:]), R=[Cn[d][h]], W=[Cb[d][h]])
        qTt = [k.sb("m_q%d" % i, (128, 8, 128), BF16, es=es) for i in range(2)]
        kTt = [k.sb("m_kT%d" % i, (128, 8, 128), BF16, es=es) for i in range(2)]
        kt = [k.sb("m_k%d" % i, (128, 1024), BF16, es=es) for i in range(2)]
        vx = [k.sb("m_v%d" % i, (128, 4, 257), BF16, es=es) for i in range(2)]
        gt = [k.sb("m_g%d" % i, (128, 16), F32, es=es) for i in range(2)]
        for i in range(2):
            k.op("pool", lambda e, i=i: e.memset(vx[i][:, :, 256:257], 1.0), W=[vx[i]])
        sm = {n: k.sb("m_" + n, (128, w), F32, es=es) for n, w in
              [("gg", 16), ("ab", 4), ("ex", 4), ("mn", 4), ("fl", 4), ("bsb", 8), ("r", 4), ("rmax", 1), ("dmax", 1),
               ("inter", 1), ("mt", 1), ("nmt", 1), ("a", 1), ("eneg", 1), ("den", 1), ("mm", 1), ("nmm", 1),
               ("dec", 1), ("ws", 1), ("mnew", 1), ("ss", 4)]}
        SH = []
        for hh in range(4):
            H = {n: k.sb("mh%d_%s" % (hh, n), (128, 1), F32, es=es) for n in
                 ("rmax", "dmax", "inter", "mt", "nmt", "a", "eneg", "den", "mm", "nmm", "dec", "ws", "mnew")}
            H["dg"] = k.sb("mh%d_dg" % hh, (128, 128), F32, es=es)
            H["dmat"] = k.sb("mh%d_dmat" % hh, (128, 128), F32, es=es)
            H["wt"] = k.sb("mh%d_w" % hh, (128, 128), F32, es=es)
            H["smat"] = k.sb("mh%d_smat" % hh, (128, 128), BF16, es=es)
            H["smT"] = k.sb("mh%d_smT" % hh, (128, 128), BF16, es=es)
            H["qca"] = k.sb("mh%d_qca" % hh, (128, 257), F32, es=es)
            H["num"] = k.sb("mh%d_num" % hh, (128, 257), F32, es=es)
            H["kws"] = k.sb("mh%d_kws" % hh, (128, 256), BF16, es=es)
            SH.append(H)
        hsum = k.sb("m_hsum", (128, 4, 256), F32, es=es)
        ot = k.sb("m_o", (128, 1024), F32, es=es)
        hn = k.sb("m_hn", (128, 1024), F32, es=es)
        hm = k.sb("m_hm", (128, 1024), BF16, es=es)
        junk = k.sb("m_junk", (128, 256), F32, es=es)
        pt = [k.sb("m_pt%d" % i, (128, 1024), F32, es=es) for i in range(3)]
        pm = k.sb("m_pm", (128, 3, 4, 128), F32, es=es)
        pld = k.sb("m_pld", (128, 2, 128), BF16, es=es)
        P = self.psb
        step = 0
        for d in range(2):
            tri = self.triF if d == 0 else self.triB
            msk = self.maskF if d == 0 else self.maskB
            for T in (range(NT) if d == 0 else range(NT - 1, -1, -1)):
                tsl = slice(T * 128, (T + 1) * 128)
                i = step % 2
                step += 1
                q_, kT_, k_, v_, g_ = qTt[i], kTt[i], kt[i], vx[i], gt[i]
                k.dma("sp", q_[:], qTs.t[:, :, tsl].rearrange("m p t -> p m t"), R=qTs.bs, W=[q_])
                k.dma("act", kT_[:], kTs.t[:, :, tsl].rearrange("m p t -> p m t"), R=kTs.bs, W=[kT_])
                k.dma("sp", k_[:], ks.t[tsl, :], R=[ks.bs[T]], W=[k_])
                k.dma("act", v_[:, :, 0:256], vs.t[tsl, :].rearrange("t (h d) -> t h d", d=256), R=[vs.bs[T]], W=[v_])
                k.dma("sp", g_[:], gs.t[tsl, :], R=[gs.bs[T]], W=[g_])
                S = sm
                k.op("dve", lambda e: e.tensor_tensor(out=S["gg"][:], in0=g_[:], in1=bg[:], op=ALU.add), R=[g_, bg], W=[S["gg"]])
                fg = S["gg"][:, 8 + d * 4:12 + d * 4]
                ig = S["gg"][:, d * 4:d * 4 + 4]
                k.op("dve", lambda e: e.tensor_scalar(out=S["ab"][:], in0=fg, scalar1=-1.0, scalar2=None, op0=ALU.mult), R=[S["gg"]], W=[S["ab"]])
                k.op("dve", lambda e: e.tensor_tensor(out=S["ab"][:], in0=S["ab"][:], in1=fg, op=ALU.max), R=[S["gg"], S["ab"]], W=[S["ab"]])
                k.op("act", lambda e: e.activation(out=S["ex"][:], in_=S["ab"][:], func=AF.Exp, scale=-1.0), R=[S["ab"]], W=[S["ex"]])
                k.op("act", lambda e: e.activation(out=S["ex"][:], in_=S["ex"][:], func=AF.Ln, bias=1.0), R=[S["ex"]], W=[S["ex"]])
                k.op("dve", lambda e: e.tensor_scalar(out=S["mn"][:], in0=fg, scalar1=0.0, scalar2=None, op0=ALU.min), R=[S["gg"]], W=[S["mn"]])
                k.op("dve", lambda e: e.tensor_tensor(out=S["fl"][:], in0=S["mn"][:], in1=S["ex"][:], op=ALU.subtract), R=[S["mn"], S["ex"]], W=[S["fl"]])
                k.op("pe", lambda e: e.matmul(P[0][:, 0:4], lhsT=tri, rhs=S["fl"][:], start=True, stop=True), R=[S["fl"], self.cst], W=[P[0]])
                k.op("pe", lambda e: e.matmul(P[0][:, 4:8], lhsT=self.ones, rhs=S["fl"][:], start=True, stop=True), R=[S["fl"], self.cst], W=[P[0]])
                k.op("dve", lambda e: e.tensor_copy(out=S["bsb"][:], in_=P[0][:, 0:8]), R=[P[0]], W=[S["bsb"]])
                k.op("dve", lambda e: e.tensor_tensor(out=S["r"][:], in0=ig, in1=S["bsb"][:, 0:4], op=ALU.subtract), R=[S["gg"], S["bsb"]], W=[S["r"]])
                if d == 1:
                    k.dma("sp", hsum[:], hF.t[tsl, :].rearrange("t (h d) -> t h d", d=256), R=[hF.bs[T]], W=[hsum])
                def head_body(h):
                    H = SH[h]
                    col = d * 4 + h
                    C_, Cb_ = Cn[d][h], Cb[d][h]
                    b_h = S["bsb"][:, h:h + 1]
                    be_h = S["bsb"][:, 4 + h:5 + h]
                    m_h = mrep[:, col:col + 1]
                    dg, dmat, wt, smat, smT, qca, num, kws = (H[n] for n in ("dg", "dmat", "wt", "smat", "smT", "qca", "num", "kws"))
                    k.op("dve", lambda e: e.tensor_scalar(out=dg[:], in0=self.ident, scalar1=S["r"][:, h:h + 1], scalar2=None, op0=ALU.mult), R=[S["r"], self.cst], W=[dg])
                    yield
                    k.op("pe", lambda e: e.matmul(P[1][:, 0:128], lhsT=self.ones, rhs=dg[:], start=True, stop=True), R=[dg, self.cst], W=[P[1]])
                    k.op("dve", lambda e: e.tensor_reduce(out=H["rmax"][:], in_=P[1][:, 0:128], axis=AX.X, op=ALU.max), R=[P[1]], W=[H["rmax"]])
                    k.op("dve", lambda e: e.scalar_tensor_tensor(out=dmat[:], in0=P[1][:, 0:128], scalar=b_h, in1=msk, op0=ALU.add, op1=ALU.add), R=[P[1], S["bsb"], self.cst], W=[dmat])
                    yield
                    k.op("dve", lambda e: e.tensor_reduce(out=H["dmax"][:], in_=dmat[:], axis=AX.X, op=ALU.max), R=[dmat], W=[H["dmax"]])
                    k.op("dve", lambda e: e.tensor_tensor(out=H["inter"][:], in0=b_h, in1=m_h, op=ALU.add), R=[S["bsb"], mrep], W=[H["inter"]])
                    yield
                    k.op("dve", lambda e: e.tensor_tensor(out=H["mt"][:], in0=H["inter"][:], in1=H["dmax"][:], op=ALU.max), R=[H["inter"], H["dmax"]], W=[H["mt"]])
                    k.op("dve", lambda e: e.tensor_scalar(out=H["nmt"][:], in0=H["mt"][:], scalar1=-1.0, scalar2=None, op0=ALU.mult), R=[H["mt"]], W=[H["nmt"]])
                    yield
                    k.op("act", lambda e: e.activation(out=wt[:], in_=dmat[:], func=AF.Exp, bias=H["nmt"][:, 0:1]), R=[dmat, H["nmt"]], W=[wt])
                    k.op("act", lambda e: e.activation(out=H["a"][:], in_=H["inter"][:], func=AF.Exp, bias=H["nmt"][:, 0:1]), R=[H["inter"], H["nmt"]], W=[H["a"]])
                    k.op("act", lambda e: e.activation(out=H["eneg"][:], in_=H["mt"][:], func=AF.Exp, scale=-1.0), R=[H["mt"]], W=[H["eneg"]])
                    yield
                    k.op("dve", lambda e: e.tensor_tensor(out=H["mm"][:], in0=m_h, in1=H["rmax"][:], op=ALU.max), R=[mrep, H["rmax"]], W=[H["mm"]])
                    k.op("dve", lambda e: e.tensor_scalar(out=H["nmm"][:], in0=H["mm"][:], scalar1=-1.0, scalar2=None, op0=ALU.mult), R=[H["mm"]], W=[H["nmm"]])
                    yield
                    k.op("act", lambda e: e.activation(out=H["dec"][:], in_=m_h, func=AF.Exp, bias=H["nmm"][:, 0:1]), R=[mrep, H["nmm"]], W=[H["dec"]])
                    k.op("act", lambda e: e.activation(out=H["ws"][:], in_=S["r"][:, h:h + 1], func=AF.Exp, bias=H["nmm"][:, 0:1]), R=[S["r"], H["nmm"]], W=[H["ws"]])
                    k.op("dve", lambda e: e.tensor_tensor(out=H["mnew"][:], in0=H["mm"][:], in1=be_h, op=ALU.add), R=[H["mm"], S["bsb"]], W=[H["mnew"]])
                    yield
                    for cc in range(2):
                        k.op("pe", lambda e, cc=cc: e.matmul(P[2][:, 0:128], lhsT=q_[:, h * 2 + cc, :], rhs=kT_[:, h * 2 + cc, :], start=(cc == 0), stop=(cc == 1)), R=[q_, kT_], W=[P[2]])
                    k.op("dve", lambda e: e.tensor_tensor(out=smat[:], in0=P[2][:, 0:128], in1=wt[:], op=ALU.mult), R=[P[2], wt], W=[smat])
                    yield
                    p3v = P[3].t[:].bitcast(BF16)
                    k.op("pe", lambda e: e.transpose(p3v[:, 0:128], smat[:], self.identb), R=[smat, self.cstb], W=[P[3]])
                    k.op("act", lambda e: e.copy(out=smT[:], in_=p3v[:, 0:128]), R=[P[3]], W=[smT])
                    yield
                    for cc in range(2):
                        k.op("pe", lambda e, cc=cc: e.matmul(P[4][:, 0:257], lhsT=q_[:, h * 2 + cc, :], rhs=Cb_[:, cc, :], start=(cc == 0), stop=(cc == 1)), R=[q_, Cb_], W=[P[4]])
                    k.op("act", lambda e: e.activation(out=qca[:], in_=P[4][:, 0:257], func=AF.Identity, scale=H["a"][:, 0:1]), R=[P[4], H["a"]], W=[qca])
                    yield
                    k.op("pe", lambda e: e.matmul(P[5][:, 0:257], lhsT=smT[:], rhs=v_[:, h, :], start=True, stop=True), R=[smT, v_], W=[P[5]])
                    k.op("dve", lambda e: e.tensor_tensor(out=num[:], in0=P[5][:, 0:257], in1=qca[:], op=ALU.add), R=[P[5], qca], W=[num])
                    yield
                    k.op("dve", lambda e: e.tensor_scalar(out=H["den"][:], in0=num[:, 256:257], scalar1=-1.0, scalar2=None, op0=ALU.mult), R=[num], W=[H["den"]])
                    k.op("dve", lambda e: e.tensor_tensor(out=H["den"][:], in0=H["den"][:], in1=num[:, 256:257], op=ALU.max), R=[num, H["den"]], W=[H["den"]])
                    k.op("dve", lambda e: e.tensor_tensor(out=H["den"][:], in0=H["den"][:], in1=H["eneg"][:], op=ALU.max), R=[H["eneg"], H["den"]], W=[H["den"]])
                    k.op("dve", lambda e: e.reciprocal(out=H["den"][:], in_=H["den"][:]), R=[H["den"]], W=[H["den"]])
                    yield
                    if d == 0:
                        k.op("dve", lambda e: e.tensor_scalar(out=hsum[:, h, :], in0=num[:, 0:256], scalar1=H["den"][:, 0:1], scalar2=None, op0=ALU.mult), R=[num, H["den"]], W=[hsum])
                    else:
                        k.op("dve", lambda e: e.scalar_tensor_tensor(out=hsum[:, h, :], in0=num[:, 0:256], scalar=H["den"][:, 0:1], in1=hsum[:, h, :], op0=ALU.mult, op1=ALU.add), R=[num, H["den"], hsum], W=[hsum])
                    k.op("dve", lambda e: e.tensor_scalar(out=kws[:], in0=k_[:, h * 256:(h + 1) * 256], scalar1=H["ws"][:, 0:1], scalar2=None, op0=ALU.mult), R=[k_, H["ws"]], W=[kws])
                    yield
                    for cc in range(2):
                        pu = P[6 + cc]
                        k.op("pe", lambda e, cc=cc, pu=pu: e.matmul(pu[:, 0:257], lhsT=kws[:, cc * 128:(cc + 1) * 128], rhs=v_[:, h, :], start=True, stop=True), R=[kws, v_], W=[pu])
                        k.op("dve", lambda e, cc=cc, pu=pu: e.scalar_tensor_tensor(out=C_[:, cc, :], in0=C_[:, cc, :], scalar=H["dec"][:, 0:1], in1=pu[:, 0:257], op0=ALU.mult, op1=ALU.add), R=[C_, H["dec"], pu], W=[C_])
                        yield
                    if (d == 0 and T % 2 == 1) or (d == 1 and T % 2 == 0):
                        slot = T // 2
                        for cc in range(2):
                            k.dma("sp", self.O["stC"].t[slot, d, h, cc * 128:(cc + 1) * 128, :], C_[:, cc, 0:256], R=[C_], W=[Buf()])
                            k.dma("sp", self.O["stn"].t[slot, d, h, cc * 128:(cc + 1) * 128].rearrange("(p o) -> p o", o=1), C_[:, cc, 256:257], R=[C_], W=[Buf()])
                        k.dma("sp", self.O["stm"].t[slot, d, h:h + 1].rearrange("(p o) -> p o", o=1), H["mnew"][0:1, 0:1], R=[H["mnew"]], W=[Buf()])
                    kf = keep[:, d, T:T + 1]
                    k.op("dve", lambda e: e.tensor_scalar(out=C_[:], in0=C_[:], scalar1=kf, scalar2=None, op0=ALU.mult), R=[C_, keep], W=[C_])
                    k.op("act", lambda e: e.copy(out=Cb_[:], in_=C_[:]), R=[C_], W=[Cb_])
                    k.op("dve", lambda e: e.tensor_tensor(out=m_h, in0=H["mnew"][:], in1=kf, op=ALU.mult), R=[H["mnew"], keep], W=[mrep])
                    yield

                gens = [head_body(h) for h in range(4)]
                while gens:
                    for g_ in list(gens):
                        try:
                            next(g_)
                        except StopIteration:
                            gens.remove(g_)
                if d == 0:
                    k.dma("sp", hF.t[tsl, :].rearrange("t (h d) -> t h d", d=256), hsum[:], R=[hsum], W=[hF.bs[T]])
                    rs = [r for r in range(3) if 0 <= T + r - 1 < NT]
                    for r in rs:
                        Tn = T + r - 1
                        k.dma("act", pt[r][:], pps.t[Tn * 128:(Tn + 1) * 128, :], R=[pps.bs[Tn]], W=[pt[r]])
                    k.dma("sp", pm[:], self.I["poolm"].t[T].rearrange("r g s t -> s r g t"), W=[pm])
                    for g in range(4):
                        for cc in range(2):
                            for r in rs:
                                k.op("pe", lambda e, g=g, cc=cc, r=r: e.matmul(P[1][:, 256 + cc * 128:256 + (cc + 1) * 128], lhsT=pt[r][:, g * 256 + cc * 128:g * 256 + (cc + 1) * 128], rhs=pm[:, r, g, :], start=(r == rs[0]), stop=(r == rs[-1])), R=[pt[r], pm], W=[P[1]])
                        k.op("act", lambda e: e.copy(out=pld[:].rearrange("p c t -> p (c t)"), in_=P[1][:, 256:512]), R=[P[1]], W=[pld])
                        for dd in range(2):
                            for cc in range(2):
                                k.op("pe", lambda e, g=g, cc=cc, dd=dd: e.matmul(P[2][:, 256:384], lhsT=pw[:, g, cc, dd * 128:(dd + 1) * 128], rhs=pld[:, cc, :], start=(cc == 0), stop=(cc == 1)), R=[pw, pld], W=[P[2]])
                            c = 8 + g * 2 + dd
                            k.op("dve", lambda e, c=c, g=g, dd=dd: e.tensor_scalar(out=self.actT.t[:, c, tsl], in0=P[2][:, 256:384], scalar1=pscol[:, g * 2 + dd:g * 2 + dd + 1], scalar2=None, op0=ALU.mult), R=[P[2], pscol], W=[self.actT.bs[T]])
                else:
                    for h in range(4):
                        k.op("act", lambda e, h=h: e.activation(out=junk[:], in_=hsum[:, h, :], func=AF.Square, accum_out=S["ss"][:, h:h + 1]), R=[hsum], W=[junk, S["ss"]])
                    self.rstd(S["ss"], 1.0 / 256)
                    k.dma("act", ot[:], os_.t[tsl, :], R=[os_.bs[T]], W=[ot])
                    k.op("act", lambda e: e.activation(out=ot[:], in_=ot[:], func=AF.Sigmoid), R=[ot], W=[ot])
                    for h in range(4):
                        k.op("dve", lambda e, h=h: e.tensor_scalar(out=hn[:, h * 256:(h + 1) * 256], in0=hsum[:, h, :], scalar1=S["ss"][:, h:h + 1], scalar2=None, op0=ALU.mult), R=[hsum, S["ss"]], W=[hn])
                    k.op("dve", lambda e: e.tensor_tensor(out=hm[:], in0=hn[:], in1=ot[:], op=ALU.mult), R=[hn, ot], W=[hm])
                    for c in range(8):
                        pb_ = P[3 + (c % 2) * 2]
                        pv = pb_.t[:].bitcast(BF16)
                        k.op("pe", lambda e, c=c, pv=pv: e.transpose(pv[:, 0:128], hm[:, c * 128:(c + 1) * 128], self.identb), R=[hm, self.cstb], W=[pb_])
                        k.op("act", lambda e, c=c, pv=pv: e.activation(out=self.actT.t[:, c, tsl], in_=pv[:, 0:128], func=AF.Identity, scale=gcol[:, c:c + 1]), R=[pb_, gcol], W=[self.actT.bs[T]])
        k.barrier()
    self.phase_outproj(self.I["w_out_even"].t[0], hold, hnew, 0)


Prog.phase_even = _phase_even


def _phase_odd(self, hold, hnew, l=1):
    k = self.k
    lam_init = 0.8 - 0.6 * math.exp(-0.3 * l)
    zq = k.dram("o_q", (TOK, 1024), F32, nbuf=16)
    zk = k.dram("o_k", (TOK, 1024), F32, nbuf=16)
    zu = k.dram("o_gu", (TOK, 1024), F32, nbuf=16)
    zv = k.dram("o_gv", (TOK, 1024), F32, nbuf=16)
    qTs = k.dram("o_qT", (8, 128, TOK), BF16, nbuf=8)
    kTs = k.dram("o_kT", (8, 128, 2560), BF16, nbuf=8)
    ck, cv = self.O["ck"], self.O["cv"]
    P = self.psb
    with ExitStack() as es:
        sf32 = [k.sb("op_f%d" % i, (128, 512), F32, es=es) for i in range(2)]
        cnt = [0]

        def sink_tok(T, ps, width, c0):
            s = sf32[cnt[0] % 2]
            cnt[0] += 1
            tsl = slice(T * 128, (T + 1) * 128)
            if cnt[0] % 2:
                k.op("act", lambda e: e.copy(out=s[:], in_=ps[:]), R=[ps], W=[s])
            else:
                k.op("dve", lambda e: e.tensor_copy(out=s[:], in_=ps[:]), R=[ps], W=[s])
            j = c0 // 1024
            col = c0 % 1024
            if j == 2:
                h0 = col // 128
                k.dma("sp", cv.t[T, h0:h0 + 4].rearrange("h t d -> t h d"), s[:].rearrange("t (h d) -> t h d", d=128),
                      R=[s], W=[cv.bs[T]])
            else:
                dst = [zq, zk, None, zu, zv][j]
                k.dma("sp", dst.t[tsl, col:col + 512], s[:], R=[s], W=[dst.bs[T]])
        self.proj(self.I["w_in_odd"].t[0], 5120, [(c0, 512, "tok", sink_tok) for c0 in range(0, 5120, 512)])
    with ExitStack() as es:
        gq = k.sb("o_gq", (128, 2, 64), F32, es=es)
        k.dma("sp", gq[:].rearrange("p a d -> p (a d)"), self.I["qk_gain"].t[0].rearrange("a d -> (a d)").partition_broadcast(128), W=[gq])
        xt = [k.sb("o_x%d" % i, (128, 16, 64), F32, es=es) for i in range(2)]
        t1 = k.sb("o_t1", (128, 16, 64), F32, es=es)
        sw = k.sb("o_sw", (128, 16, 64), F32, es=es)
        xb = k.sb("o_xb", (128, 16, 64), BF16, es=es)
        ss = k.sb("o_ss", (128, 16), F32, es=es)
        rc = k.sb("o_rc", (128, 64), F32, es=es)
        rs = k.sb("o_rs", (128, 64), F32, es=es)
        tb = [k.sb("o_tb%d" % i, (128, 128), BF16, es=es) for i in range(2)]
        ckf = k.sb("o_ckf", (128, 128), F32, es=es)
        n = 0
        t1s = [t1, k.sb("o_t1b", (128, 16, 64), F32, es=es)]
        sws = [sw, k.sb("o_swb", (128, 16, 64), F32, es=es)]
        xbs = [xb, k.sb("o_xbb", (128, 16, 64), BF16, es=es)]
        sss = [ss, k.sb("o_ssb", (128, 16), F32, es=es)]
        rcs = [rc, k.sb("o_rcb", (128, 64), F32, es=es)]
        rss = [rs, k.sb("o_rsb", (128, 64), F32, es=es)]
        tbs = [k.sb("o_tbx%d" % i, (128, 128), BF16, es=es) for i in range(4)]
        nn = [0]

        def chain(T, which, src, rc_, rs_):
            tsl = slice(T * 128, (T + 1) * 128)
            x, t1_, sw_, xb_, ss_ = xt[which], t1s[which], sws[which], xbs[which], sss[which]
            k.dma("sp", x[:].rearrange("p a d -> p (a d)"), src.t[tsl, :], R=[src.bs[T]], W=[x])
            k.op("dve", lambda e: e.tensor_tensor(out=t1_[:], in0=x[:], in1=x[:], op=ALU.mult), R=[x], W=[t1_])
            yield
            k.op("dve", lambda e: e.tensor_reduce(out=ss_[:], in_=t1_[:], axis=AX.X, op=ALU.add), R=[t1_], W=[ss_])
            yield
            k.op("dve", lambda e: e.tensor_scalar(out=ss_[:], in0=ss_[:], scalar1=1.0 / 64, scalar2=EPS, op0=ALU.mult, op1=ALU.add), R=[ss_], W=[ss_])
            yield
            k.op("act", lambda e: e.activation(out=ss_[:], in_=ss_[:], func=AF.Sqrt), R=[ss_], W=[ss_])
            yield
            k.op("dve", lambda e: e.reciprocal(out=ss_[:], in_=ss_[:]), R=[ss_], W=[ss_])
            yield
            k.op("dve", lambda e: e.tensor_tensor(out=x[:], in0=x[:], in1=ss_[:].unsqueeze(2).to_broadcast([128, 16, 64]), op=ALU.mult), R=[x, ss_], W=[x])
            yield
            k.op("dve", lambda e: e.tensor_tensor(out=x[:], in0=x[:], in1=gq[:, which, :].unsqueeze(1).to_broadcast([128, 16, 64]), op=ALU.mult), R=[x, gq], W=[x])
            yield
            xv = x[:].rearrange("p a (b c d) -> p a b c d", b=2, c=2)
            sv = sw_[:].rearrange("p a (b c d) -> p a b c d", b=2, c=2)
            for b_ in range(2):
                k.op("pool", lambda e, b_=b_: e.tensor_copy(out=sv[:, :, b_, 0, :], in_=xv[:, :, b_, 1, :]), R=[x], W=[sw_])
                k.op("pool", lambda e, b_=b_: e.tensor_copy(out=sv[:, :, b_, 1, :], in_=xv[:, :, b_, 0, :]), R=[x], W=[sw_])
            yield
            k.op("dve", lambda e: e.tensor_tensor(out=t1_[:], in0=x[:], in1=rc_[:].unsqueeze(1).to_broadcast([128, 16, 64]), op=ALU.mult), R=[x, rc_], W=[t1_])
            yield
            k.op("dve", lambda e: e.tensor_tensor(out=sw_[:], in0=sw_[:], in1=rs_[:].unsqueeze(1).to_broadcast([128, 16, 64]), op=ALU.mult), R=[sw_, rs_], W=[sw_])
            yield
            k.op("dve", lambda e: e.tensor_tensor(out=x[:], in0=t1_[:], in1=sw_[:], op=ALU.add), R=[t1_, sw_], W=[x])
            yield
            k.op("act", lambda e: e.copy(out=xb_[:], in_=x[:]), R=[x], W=[xb_])
            if which == 1:
                k.dma("sp", ck.t[T].rearrange("h t d -> t h d"), x[:].rearrange("p (h m) d -> p h (m d)", m=2), R=[x], W=[ck.bs[T]])
            yield
            dstT = kTs if which else qTs
            for h in range(8):
                pb_ = P[nn[0] % 4]
                pv = pb_.t[:].bitcast(BF16)
                t_ = tbs[nn[0] % 4]
                nn[0] += 1
                k.op("pe", lambda e, h=h, pv=pv: e.transpose(pv[:, 0:128], xb_[:, 2 * h:2 * h + 2, :].rearrange("p a d -> p (a d)"), self.identb), R=[xb_, self.cstb], W=[pb_])
                k.op("act", lambda e, pv=pv, t_=t_: e.copy(out=t_[:], in_=pv[:, 0:128]), R=[pb_], W=[t_])
                k.dma("sp", dstT.t[h, :, tsl], t_[:], R=[t_], W=[dstT.bs[h]])
                yield

        for T in range(NT):
            tsl = slice(T * 128, (T + 1) * 128)
            rc_, rs_ = rcs[T % 2], rss[T % 2]
            k.dma("sp", rc_[:], self.I["ropec"].t[tsl, :], W=[rc_])
            k.dma("sp", rs_[:], self.I["ropes"].t[tsl, :], W=[rs_])
            gens = [chain(T, 0, zq, rc_, rs_), chain(T, 1, zk, rc_, rs_)]
            while gens:
                for g_ in list(gens):
                    try:
                        next(g_)
                    except StopIteration:
                        gens.remove(g_)
        n = nn[0]
        for h in range(8):
            for j in range(4):
                pb_ = P[n % 4]
                pv = pb_.t[:].bitcast(BF16)
                t_ = tb[n % 2]
                n += 1
                k.dma("act", ckf[:], self.I["cache_k"].t[h, j * 128:(j + 1) * 128, :], W=[ckf])
                k.op("dve", lambda e: e.tensor_copy(out=xb[:, 0:2, :].rearrange("p a d -> p (a d)"), in_=ckf[:]), R=[ckf], W=[xb])
                k.op("pe", lambda e, pv=pv: e.transpose(pv[:, 0:128], xb[:, 0:2, :].rearrange("p a d -> p (a d)"), self.identb), R=[xb, self.cstb], W=[pb_])
                k.op("act", lambda e, pv=pv, t_=t_: e.copy(out=t_[:], in_=pv[:, 0:128]), R=[pb_], W=[t_])
                k.dma("sp", kTs.t[h, :, 2048 + j * 128:2048 + (j + 1) * 128], t_[:], R=[t_], W=[kTs.bs[h]])
        k.barrier()
    with ExitStack() as es:
        lamt = k.sb("a_lam", (128, 4, 64), F32, es=es)
        lp = k.sb("a_lp", (128, 2, 64), F32, es=es)
        ls = k.sb("a_ls", (128, 2), F32, es=es)
        nlam = k.sb("a_nlam", (128, 1), F32, es=es)
        k.dma("sp", lamt[:].rearrange("p a d -> p (a d)"), self.I["da_lambda"].t[0].rearrange("a d -> (a d)").partition_broadcast(128), W=[lamt])
        lv = lamt[:].rearrange("p (a b) d -> p a b d", b=2)
        k.op("dve", lambda e: e.tensor_tensor(out=lp[:], in0=lv[:, :, 0, :], in1=lv[:, :, 1, :], op=ALU.mult), R=[lamt], W=[lp])
        k.op("dve", lambda e: e.tensor_reduce(out=ls[:], in_=lp[:], axis=AX.X, op=ALU.add), R=[lp], W=[ls])
        k.op("act", lambda e: e.activation(out=ls[:], in_=ls[:], func=AF.Exp), R=[ls], W=[ls])
        k.op("dve", lambda e: e.tensor_tensor(out=nlam[:], in0=ls[:, 1:2], in1=ls[:, 0:1], op=ALU.subtract), R=[ls], W=[nlam])
        k.op("dve", lambda e: e.tensor_scalar(out=nlam[:], in0=nlam[:], scalar1=-lam_init, scalar2=None, op0=ALU.add), R=[nlam], W=[nlam])
        slcol = k.sb("a_sl", (128, 1), F32, es=es)
        self._col_from_row(self.I["da_subln"].t[0], 1, slcol, es)
        am = k.sb("a_am", (128, 16, 20), F32, es=es)
        k.dma("sp", am[:], self.I["amask"].t[:, :, :], W=[am])
        fm = k.sb("a_fm", (128, 16, 20), F32, es=es)
        k.op("dve", lambda e: e.tensor_scalar(out=fm[:], in0=am[:], scalar1=-1.0, scalar2=None, op0=ALU.is_ge), R=[am], W=[fm])
        slc = k.sb("a_slc", (128, 1), F32, es=es)
        k.op("dve", lambda e: e.tensor_scalar(out=slc[:], in0=slcol[:], scalar1=(1.0 - lam_init), scalar2=None, op0=ALU.mult), R=[slcol], W=[slc])
        kT = [k.sb("a_kT%d" % i, (128, 2560), BF16, es=es) for i in range(2)]
        qT = [k.sb("a_qT%d" % i, (128, TOK), BF16, es=es) for i in range(2)]
        vx = [k.sb("a_vx%d" % i, (128, 20, 128), BF16, es=es) for i in range(2)]
        pex = [k.sb("a_pe%d" % i, (128, 512), BF16, es=es) for i in range(3)]
        pT = [k.sb("a_pT%d" % i, (128, 4, 128), BF16, es=es) for i in range(3)]
        r0 = k.sb("a_r0", (128, 512), F32, es=es)
        r1 = k.sb("a_r1", (128, 512), F32, es=es)
        osb = k.sb("a_osb", (128, 512), F32, es=es)
        sq = k.sb("a_sq", (128, 512), F32, es=es)
        npt = 0
        for h in range(8):
            kT_, qT_, vx_ = kT[h % 2], qT[h % 2], vx[h % 2]
            k.dma("sp", kT_[:], kTs.t[h], R=[kTs.bs[h]], W=[kT_])
            k.dma("sp", qT_[:], qTs.t[h], R=[qTs.bs[h]], W=[qT_])
            k.dma("pool", vx_[:, 0:16, :], cv.t[:, h].rearrange("T t d -> t T d"), R=cv.bs, W=[vx_])
            k.dma("pool", vx_[:, 16:20, :], self.I["cache_v"].t[h].rearrange("(j t) d -> t j d", t=128), W=[vx_])
            for Tg in range(4):
                gsl = slice(Tg * 512, (Tg + 1) * 512)
                seq = [(m, kt) for m in range(2) for kt in range(20)]
                LA = 2
                for j in range(len(seq) + LA):
                    if j < len(seq):
                        m, kt = seq[j]
                        msl = slice(m * 64, (m + 1) * 64)
                        pbank = P[j % 3]
                        k.op("pe", lambda e, kt=kt, pbank=pbank, msl=msl: e.matmul(
                            pbank[:, :], lhsT=kT_[msl, kt * 128:(kt + 1) * 128], rhs=qT_[msl, gsl], start=True, stop=True),
                            R=[kT_, qT_], W=[pbank])
                    if j >= LA:
                        m, kt = seq[j - LA]
                        pbank = P[(j - LA) % 3]
                        px, p_ = pex[npt % 3], pT[npt % 3]
                        npt += 1
                        k.op("act", lambda e, pbank=pbank, px=px: e.activation(out=px[:], in_=pbank[:, :], func=AF.Exp, scale=0.125),
                             R=[pbank], W=[px])
                        k.op("dve", lambda e, kt=kt, px=px, p_=p_: e.tensor_tensor(
                            out=p_[:], in0=px[:].rearrange("p (a t) -> p a t", t=128),
                            in1=fm[:, Tg * 4:(Tg + 1) * 4, kt].unsqueeze(2).to_broadcast([128, 4, 128]), op=ALU.mult),
                            R=[px, fm], W=[p_])
                        pflat = p_[:].rearrange("p a t -> p (a t)")
                        k.op("pe", lambda e, kt=kt, pflat=pflat, m=m: e.matmul(
                            P[3 + m][:, :], lhsT=vx_[:, kt, :], rhs=pflat, start=(kt == 0), stop=(kt == 19)),
                            R=[p_, vx_], W=[P[3 + m]])
                        k.op("pe", lambda e, kt=kt, pflat=pflat, m=m: e.matmul(
                            P[5 + m][:, :], lhsT=self.onesb, rhs=pflat, start=(kt == 0), stop=(kt == 19)),
                            R=[p_, self.cstb], W=[P[5 + m]])
                k.op("dve", lambda e: e.reciprocal(out=r0[:], in_=P[5][:, :]), R=[P[5]], W=[r0])
                k.op("dve", lambda e: e.reciprocal(out=r1[:], in_=P[6][:, :]), R=[P[6]], W=[r1])
                k.op("dve", lambda e: e.tensor_scalar(out=r1[:], in0=r1[:], scalar1=nlam[:, 0:1], scalar2=None, op0=ALU.mult), R=[r1, nlam], W=[r1])
                k.op("dve", lambda e: e.tensor_tensor(out=osb[:], in0=P[3][:, :], in1=r0[:], op=ALU.mult), R=[P[3], r0], W=[osb])
                k.op("dve", lambda e: e.tensor_tensor(out=r1[:], in0=P[4][:, :], in1=r1[:], op=ALU.mult), R=[P[4], r1], W=[r1])
                k.op("dve", lambda e: e.tensor_tensor(out=osb[:], in0=osb[:], in1=r1[:], op=ALU.add), R=[osb, r1], W=[osb])
                k.op("act", lambda e: e.activation(out=sq[:], in_=osb[:], func=AF.Square), R=[osb], W=[sq])
                k.op("pe", lambda e: e.matmul(P[7][:, :], lhsT=self.ones, rhs=sq[:], start=True, stop=True), R=[sq, self.cst], W=[P[7]])
                k.op("dve", lambda e: e.tensor_scalar(out=r0[:], in0=P[7][:, :], scalar1=1.0 / 128, scalar2=EPS, op0=ALU.mult, op1=ALU.add), R=[P[7]], W=[r0])
                k.op("act", lambda e: e.activation(out=r0[:], in_=r0[:], func=AF.Sqrt), R=[r0], W=[r0])
                k.op("dve", lambda e: e.reciprocal(out=r0[:], in_=r0[:]), R=[r0], W=[r0])
                k.op("dve", lambda e: e.scalar_tensor_tensor(out=self.actT.t[:, h, gsl], in0=osb[:], scalar=slc[:, 0:1], in1=r0[:], op0=ALU.mult, op1=ALU.mult),
                     R=[osb, slc, r0], W=[self.actT.bs[Tg * 4 + i] for i in range(4)])
        k.barrier()
    with ExitStack() as es:
        wsT = k.sb("g_wsT", (128, 8, 128), BF16, es=es)
        wtmp = k.sb("g_wt", (128, 128), F32, es=es)
        bcol = k.sb("g_b", (128, 8), F32, es=es)
        k.dma("sp", bcol[:], self.I["gm_b"].t[0].rearrange("g t -> t g"), W=[bcol], allow_slow_non_contiguous=True)
        for g in range(8):
            k.dma("sp", wtmp[:], self.I["gm_ws"].t[0, g], W=[wtmp])
            k.op("pe", lambda e: e.transpose(P[0][:, 0:128], wtmp[:], self.ident), R=[wtmp, self.cst], W=[P[0]])
            k.op("act", lambda e, g=g: e.copy(out=wsT[:, g, :], in_=P[0][:, 0:128]), R=[P[0]], W=[wsT])
        gsets = []
        for si in range(2):
            gsets.append(dict(
                gu=k.sb("g_u%d" % si, (128, 1024), F32, es=es), gv=k.sb("g_v%d" % si, (128, 8, 128), F32, es=es),
                t1=k.sb("g_t1%d" % si, (128, 8, 128), F32, es=es), ss=k.sb("g_ss%d" % si, (128, 8), F32, es=es),
                vg=k.sb("g_vg%d" % si, (128, 8, 128), BF16, es=es), ob=k.sb("g_ob%d" % si, (128, 1024), BF16, es=es)))
        gcnt = [0]

        def gchain(T):
            G_ = gsets[T % 2]
            gu, gv, t1, ss, vg, ob = (G_[n_] for n_ in ("gu", "gv", "t1", "ss", "vg", "ob"))
            tsl = slice(T * 128, (T + 1) * 128)
            k.dma("sp", gu[:], zu.t[tsl, :], R=[zu.bs[T]], W=[gu])
            k.dma("sp", gv[:].rearrange("p g c -> p (g c)"), zv.t[tsl, :], R=[zv.bs[T]], W=[gv])
            k.op("act", lambda e: e.activation(out=gu[:], in_=gu[:], func=AF.Gelu_apprx_tanh), R=[gu], W=[gu])
            yield
            k.op("act", lambda e: e.activation(out=gv[:], in_=gv[:], func=AF.Gelu_apprx_tanh), R=[gv], W=[gv])
            yield
            k.op("dve", lambda e: e.tensor_tensor(out=t1[:], in0=gv[:], in1=gv[:], op=ALU.mult), R=[gv], W=[t1])
            yield
            k.op("dve", lambda e: e.tensor_reduce(out=ss[:], in_=t1[:], axis=AX.X, op=ALU.add), R=[t1], W=[ss])
            yield
            k.op("dve", lambda e: e.tensor_scalar(out=ss[:], in0=ss[:], scalar1=1.0 / 128, scalar2=EPS, op0=ALU.mult, op1=ALU.add), R=[ss], W=[ss])
            yield
            k.op("act", lambda e: e.activation(out=ss[:], in_=ss[:], func=AF.Sqrt), R=[ss], W=[ss])
            yield
            k.op("dve", lambda e: e.reciprocal(out=ss[:], in_=ss[:]), R=[ss], W=[ss])
            yield
            k.op("dve", lambda e: e.tensor_tensor(out=vg[:], in0=gv[:], in1=ss[:].unsqueeze(2).to_broadcast([128, 8, 128]), op=ALU.mult), R=[gv, ss], W=[vg])
            yield
            for g in range(8):
                pb_ = P[1 + gcnt[0] % 2]
                gcnt[0] += 1
                k.op("pe", lambda e, g=g, pb_=pb_: e.matmul(pb_[:, 0:128], lhsT=wsT[:, g, :], rhs=vg[:, g, :], start=True, stop=True), R=[wsT, vg], W=[pb_])
                k.op("dve", lambda e, g=g, pb_=pb_: e.scalar_tensor_tensor(out=ob[:, g * 128:(g + 1) * 128], in0=pb_[:, 0:128], scalar=bcol[:, g:g + 1], in1=gu[:, g * 128:(g + 1) * 128], op0=ALU.add, op1=ALU.mult), R=[pb_, bcol, gu], W=[ob])
                yield
            for g in range(8):
                pb_ = P[3 + gcnt[0] % 2]
                gcnt[0] += 1
                pv = pb_.t[:].bitcast(BF16)
                k.op("pe", lambda e, g=g, pv=pv: e.transpose(pv[:, 0:128], ob[:, g * 128:(g + 1) * 128], self.identb), R=[ob, self.cstb], W=[pb_])
                k.op("act", lambda e, g=g, pv=pv: e.copy(out=self.actT.t[:, 8 + g, tsl], in_=pv[:, 0:128]), R=[pb_], W=[self.actT.bs[T]])
                yield

        for T0 in range(0, NT, 2):
            gens = [gchain(T0), gchain(T0 + 1)]
            while gens:
                for g_ in list(gens):
                    try:
                        next(g_)
                    except StopIteration:
                        gens.remove(g_)
        k.barrier()
    self.phase_outproj(self.I["w_out_odd"].t[0], hold, hnew, 0)


Prog.phase_odd = _phase_odd


def build_full():
    p = Prog()
    for l in range(2):
        hb = 2 * l
        p.phase_mod(l)
        p.alloc_act()
        p.phase_norm(p.h[hb], 0)
        if l == 0:
            p.phase_even(p.h[hb], p.h[hb + 1])
        else:
            p.phase_odd(p.h[hb], p.h[hb + 1])
        p.phase_norm(p.h[hb + 1], 1)
        p.phase_peer_prep(l)
        p.phase_peer_q(l)
        p.free_act()
        p.phase_peer_main(l, p.h[hb + 1], p.h[hb + 2])
    p.k.finish()
    return p
```

```python
import math
from contextlib import ExitStack
import numpy as np
import concourse.bass as bass
import concourse.mybir as mybir
from concourse.bass_utils import run_bass_kernel_spmd

F32 = mybir.dt.float32
BF16 = mybir.dt.bfloat16
AF = mybir.ActivationFunctionType
ALU = mybir.AluOpType
AX = mybir.AxisListType

D = 2048
NT = 16
TOK = 2048
EPS = 1e-6
NEG = -1.0e30
RD = 8


class Buf:
    __slots__ = ("w", "r", "name")

    def __init__(self, name=""):
        self.w = {}
        self.r = {}
        self.name = name


class Tile:
    def __init__(self, t, name, nbuf=1):
        self.t = t
        self.b = Buf(name)
        self.bs = [Buf(name + str(i)) for i in range(nbuf)] if nbuf > 1 else None

    def __getitem__(self, k):
        return self.t[k]


class KB:
    def __init__(self):
        self.nc = bass.Bass("TRN2", target_bir_lowering=False)
        nc = self.nc
        self.es = ExitStack()
        self.eng = {"pe": nc.tensor, "act": nc.scalar, "dve": nc.vector, "pool": nc.gpsimd, "sp": nc.sync}
        self.sems = {}
        self.cnt = {}
        for e in ["pe", "act", "dve", "pool"]:
            self.sems["c_" + e] = self.es.enter_context(nc.semaphore("c_" + e))
            self.cnt["c_" + e] = 0
        self.dq = ["sp", "pool", "act"]
        self.dcnt = {q: 0 for q in self.dq}
        for q in self.dq:
            for i in range(RD):
                k = "d_%s_%d" % (q, i)
                self.sems[k] = self.es.enter_context(nc.semaphore(k))
                self.cnt[k] = 0
        self.waited = {e: {} for e in self.eng}
        self.nins = 0

    def _nid(self):
        self._n = getattr(self, "_n", 0) + 1
        return self._n

    def sb(self, name, shape, dt=F32, es=None, nbuf=1):
        t = (es or self.es).enter_context(self.nc.sbuf_tensor("sb%d_%s" % (self._nid(), name), list(shape), dt))
        return Tile(t, name, nbuf)

    def ps(self, name, shape, dt=F32, es=None):
        t = (es or self.es).enter_context(self.nc.psum_tensor("pp%d_%s" % (self._nid(), name), list(shape), dt))
        return Tile(t, name)

    def dram(self, name, shape, dt=F32, kind="Internal", nbuf=1):
        t = self.nc.dram_tensor(name if kind != "Internal" else "dr%d_%s" % (self._nid(), name), list(shape), dt, kind=kind).ap()
        return Tile(t, name, nbuf)

    def _wait(self, e, key, val):
        if val <= 0:
            return
        if e == "pe" and key == "c_pe":
            return
        if self.waited[e].get(key, 0) >= val:
            return
        self.eng[e].wait_ge(self.sems[key], val)
        self.waited[e][key] = val

    def _deps(self, e, R, W):
        for b in R:
            for k, v in b.w.items():
                self._wait(e, k, v)
        own = "c_" + e
        for b in W:
            for k, v in b.w.items():
                if k != own:
                    self._wait(e, k, v)
            for k, v in b.r.items():
                self._wait(e, k, v)

    def _mark(self, key, val, R, W):
        for b in R:
            if b.r.get(key, 0) < val:
                b.r[key] = val
        for b in W:
            b.w = {key: val}
            b.r = {}

    @staticmethod
    def _bufs(xs):
        out = []
        for x in xs:
            if isinstance(x, Tile):
                out.append(x.b)
            elif x is not None:
                out.append(x)
        return out

    def op(self, e, fn, R=(), W=()):
        R = self._bufs(R)
        W = self._bufs(W)
        self._deps(e, R, W)
        key = "c_" + e
        self.cnt[key] += 1
        fn(self.eng[e]).then_inc(self.sems[key], 1)
        self._mark(key, self.cnt[key], R, W)
        self.nins += 1

    def dma(self, q, out, in_, R=(), W=(), **kw):
        if q == "act":
            q = "sp"
        R = self._bufs(R)
        W = self._bufs(W)
        self._deps(q, R, W)
        n = self.dcnt[q]
        slot = n % RD
        val = 16 * (n // RD + 1)
        key = "d_%s_%d" % (q, slot)
        self._wait(q, key, val - 16)
        self.dcnt[q] += 1
        self.cnt[key] = val
        self.eng[q].dma_start(out=out, in_=in_, **kw).then_inc(self.sems[key], 16)
        self._mark(key, val, R, W)
        self.nins += 1

    def barrier(self, engines=None):
        for e in (engines or list(self.eng)):
            for k, v in self.cnt.items():
                self._wait(e, k, v)

    def finish(self):
        self.barrier(["sp"])
        self.es.close()


W_SPEC = [
    ("norm_mix", (2, D)), ("norm_ffn", (2, D)), ("w_mod", (2, D, 6 * D)), ("b_mod", (2, 6 * D)),
    ("w_in_even", (1, D, 5136)), ("b_gate_even", (1, 16)), ("mlstm_gain", (1, 1024)),
    ("pool_w", (1, 4, 256, 256)), ("pool_scale", (1, 1024)), ("w_out_even", (1, D, D)),
    ("w_in_odd", (1, D, 5120)), ("qk_gain", (1, 2, 64)), ("da_lambda", (1, 4, 64)), ("da_subln", (1, 128)),
    ("gm_ws", (1, 8, 128, 128)), ("gm_b", (1, 8, 128)), ("w_out_odd", (1, D, D)),
    ("peer_wq", (2, D, D)), ("peer_subkeys", (2, 8, 2, 128, 128)), ("peer_u", (2, 16384, D)),
    ("peer_v", (2, 16384, D)),
]
C_SPEC = [
    ("x", (TOK, D)), ("cond_pc", (128, 16)), ("consts", (128, 8, 128)),
    ("keep", (128, 2, 16)), ("C0", (2, 4, 256, 256)), ("n0", (2, 4, 256)), ("m0", (128, 8)),
    ("amask", (128, 16, 20)), ("ropec", (TOK, 64)), ("ropes", (TOK, 64)),
    ("cache_k", (8, 512, 128)), ("cache_v", (8, 512, 128)), ("poolm", (16, 3, 4, 128, 128)),
]
O_SPEC = [
    ("y", (TOK, D)), ("stC", (8, 2, 4, 256, 256)), ("stn", (8, 2, 4, 256)), ("stm", (8, 2, 4)),
    ("ck", (16, 8, 128, 128)), ("cv", (16, 8, 128, 128)),
]


class LazyIn(dict):
    def __init__(self, k):
        super().__init__()
        self.k = k
        self.shapes = dict(W_SPEC + C_SPEC)

    def __missing__(self, name):
        t = self.k.dram(name, self.shapes[name], F32, kind="ExternalInput")
        self[name] = t
        return t


class Prog:
    def __init__(self, stages=("all",), dbg=()):
        self.k = KB()
        k = self.k
        self.stages = stages
        self.I = LazyIn(k)
        self.O = {}
        for name, shp in O_SPEC:
            self.O[name] = k.dram(name, shp, F32, kind="ExternalOutput", nbuf=16)
        self.dbg = {}
        for name, shp, dt in dbg:
            self.dbg[name] = k.dram("dbg_" + name, shp, dt, kind="ExternalOutput", nbuf=16)
        self.h = [self.I["x"]] + [k.dram("h%d" % i, (TOK, D), F32, nbuf=16) for i in range(1, 4)] + [self.O["y"]]
        self.h[0].bs = [Buf("x%d" % i) for i in range(16)]
        self.setup_consts()

    def setup_consts(self):
        k = self.k
        self.cst = k.sb("cst", (128, 8, 128), F32)
        k.dma("sp", self.cst[:], self.I["consts"][:, :, :], W=[self.cst])
        c = self.cst
        self.ident = c[:, 0, :]
        self.ones = c[:, 1, :]
        self.triF = c[:, 2, :]
        self.triB = c[:, 3, :]
        self.maskF = c[:, 4, :]
        self.maskB = c[:, 5, :]
        self.cstb = k.sb("cstb", (128, 2, 128), BF16)
        k.op("dve", lambda e: e.tensor_copy(out=self.cstb[:], in_=c[:, 0:2, :]), R=[c], W=[self.cstb])
        self.identb = self.cstb[:, 0, :]
        self.onesb = self.cstb[:, 1, :]
        self.actT = None
        self._act_es = None
        self.skT = k.sb("skT", (128, 16, 128), F32)
        self.psb = [k.ps("psb%d" % i, (128, 512), F32) for i in range(8)]
        self.condsb = k.sb("condsb", (128, 16), F32)
        k.dma("sp", self.condsb[:], self.I["cond_pc"][:, :], W=[self.condsb])
        self.sc = k.sb("sc", (128, 16), F32)
        k.op("act", lambda e: e.activation(out=self.sc[:], in_=self.condsb[:], func=AF.Silu),
             R=[self.condsb], W=[self.sc])
        self.gate = [k.sb("gateM", (128, D), F32), k.sb("gateF", (128, D), F32)]
        self.acol = [k.sb("acolM", (128, 16), F32), k.sb("acolF", (128, 16), F32)]
        self.bcol = [k.sb("bcolM", (128, 16), F32), k.sb("bcolF", (128, 16), F32)]

    def alloc_act(self):
        self._act_es = ExitStack()
        self.actT = self.k.sb("actT", (128, 16, TOK), BF16, es=self._act_es, nbuf=16)

    def free_act(self):
        self.k.barrier()
        self._act_es.close()
        self.actT = None

    def rstd(self, s, inv_n, eps=EPS):
        k = self.k
        k.op("dve", lambda e: e.tensor_scalar(out=s[:], in0=s[:], scalar1=inv_n, scalar2=eps,
                                              op0=ALU.mult, op1=ALU.add), R=[s], W=[s])
        k.op("act", lambda e: e.activation(out=s[:], in_=s[:], func=AF.Sqrt), R=[s], W=[s])
        k.op("dve", lambda e: e.reciprocal(out=s[:], in_=s[:]), R=[s], W=[s])

    def phase_mod(self, l):
        k = self.k
        with ExitStack() as es:
            wb = [k.sb("mw%d" % i, (128, 16, 512), F32, es=es) for i in range(2)]
            bb = [k.sb("mb%d" % i, (128, 512), F32, es=es) for i in range(2)]
            self.screp = k.sb("screp", (128, 16, 128), F32, es=es)
            for c_ in range(16):
                k.op("dve", lambda e, c_=c_: e.tensor_copy(out=self.screp[:, c_, :],
                                                          in_=self.sc[:, c_:c_ + 1].to_broadcast([128, 128])),
                     R=[self.sc], W=[self.screp])
            srow = k.sb("srow", (128, D), F32, es=es)
            arow = k.sb("arow", (128, D), F32, es=es)
            grow = k.sb("grow", (128, D), F32, es=es)
            wm = self.I["w_mod"].t[l].rearrange("(c p) n -> p c n", p=128)
            bm = self.I["b_mod"].t[l]
            blk = 0
            for sub in range(2):
                gsrc = self.I["norm_mix" if sub == 0 else "norm_ffn"].t[l]
                k.dma("sp", grow[:], gsrc.partition_broadcast(128), W=[grow])
                for i in range(3):
                    for j in range(4):
                        n0 = (sub * 3 + i) * D + j * 512
                        w = wb[blk % 2]
                        b = bb[blk % 2]
                        k.dma("sp", w[:, 0:8, :], wm[:, 0:8, n0:n0 + 512], W=[w])
                        k.dma("act", w[:, 8:16, :], wm[:, 8:16, n0:n0 + 512], W=[w])
                        k.dma("sp", b[:], bm[n0:n0 + 512].partition_broadcast(128), W=[b])
                        ps = self.psb[blk % 2]
                        for c in range(16):
                            k.op("pe", lambda e, c=c, ps=ps, w=w: e.matmul(ps[:], lhsT=self.screp[:, c, :], rhs=w[:, c, :],
                                                                        start=(c == 0), stop=(c == 15)),
                                 R=[self.screp, w], W=[ps])
                        dst = [srow, arow, self.gate[sub]][i]
                        k.op("dve", lambda e, ps=ps, b=b, dst=dst, j=j: e.tensor_tensor(
                            out=dst[:, j * 512:(j + 1) * 512], in0=ps[:], in1=b[:], op=ALU.add),
                            R=[ps, b], W=[dst])
                        blk += 1
                k.op("dve", lambda e: e.scalar_tensor_tensor(out=arow[:], in0=arow[:], scalar=1.0, in1=grow[:],
                                                             op0=ALU.add, op1=ALU.mult), R=[arow, grow], W=[arow])
                for src, dst in ((arow, self.acol[sub]), (srow, self.bcol[sub])):
                    for c in range(16):
                        ps = self.psb[2 + (c % 2)]
                        k.op("pe", lambda e, ps=ps, src=src, c=c: e.transpose(ps[:, 0:128], src[:, c * 128:(c + 1) * 128],
                                                                          self.ident), R=[src, self.cst], W=[ps])
                        k.op("dve", lambda e, ps=ps, dst=dst, c=c: e.tensor_copy(out=dst[:, c:c + 1], in_=ps[:, 0:1]),
                             R=[ps], W=[dst])
            k.barrier()

    def phase_norm(self, hsrc, sub):
        k = self.k
        with ExitStack() as es:
            ht = [k.sb("nh%d" % i, (128, D), F32, es=es) for i in range(2)]
            junk = k.sb("njunk", (128, D), F32, es=es)
            hs = [k.sb("nhs%d" % i, (128, D), BF16, es=es) for i in range(2)]
            ss = [k.sb("nss%d" % i, (128, 1), F32, es=es) for i in range(2)]
            pst = [k.ps("npst%d" % i, (128, 1024), BF16, es=es) for i in range(2)] if False else None
            pcnt = [0]

            def chain(T):
                h = ht[T % 2]
                s_ = ss[T % 2]
                hb = hs[T % 2]
                k.dma("sp", h[:, 0:1024], hsrc.t[T * 128:(T + 1) * 128, 0:1024], R=[hsrc.bs[T]], W=[h])
                k.dma("sp", h[:, 1024:2048], hsrc.t[T * 128:(T + 1) * 128, 1024:2048], R=[hsrc.bs[T]], W=[h])
                k.op("act", lambda e: e.activation(out=junk[:], in_=h[:], func=AF.Square, accum_out=s_[:]),
                     R=[h], W=[junk, s_])
                yield
                k.op("dve", lambda e: e.tensor_scalar(out=s_[:], in0=s_[:], scalar1=1.0 / D, scalar2=EPS,
                                                      op0=ALU.mult, op1=ALU.add), R=[s_], W=[s_])
                yield
                k.op("act", lambda e: e.activation(out=s_[:], in_=s_[:], func=AF.Sqrt), R=[s_], W=[s_])
                yield
                k.op("dve", lambda e: e.reciprocal(out=s_[:], in_=s_[:]), R=[s_], W=[s_])
                yield
                k.op("dve", lambda e: e.tensor_scalar(out=hb[:], in0=h[:], scalar1=s_[:, 0:1], scalar2=None,
                                                      op0=ALU.mult), R=[h, s_], W=[hb])
                yield
                for c in range(16):
                    ps = self.psb[pcnt[0] % 4]
                    pcnt[0] += 1
                    psv = ps.t[:].bitcast(BF16)
                    k.op("pe", lambda e, psv=psv, c=c: e.transpose(psv[:, 0:128], hb[:, c * 128:(c + 1) * 128],
                                                                  self.identb), R=[hb, self.cstb], W=[ps])
                    dst = self.actT.t[:, c, T * 128:(T + 1) * 128]
                    if c % 2 == 0:
                        k.op("act", lambda e, psv=psv, dst=dst, c=c: e.activation(
                            out=dst, in_=psv[:, 0:128], func=AF.Identity, scale=self.acol[sub][:, c:c + 1],
                            bias=self.bcol[sub][:, c:c + 1]), R=[ps, self.acol[sub], self.bcol[sub]], W=[self.actT.bs[T]])
                    else:
                        k.op("dve", lambda e, psv=psv, dst=dst, c=c: e.tensor_scalar(
                            out=dst, in0=psv[:, 0:128], scalar1=self.acol[sub][:, c:c + 1],
                            scalar2=self.bcol[sub][:, c:c + 1], op0=ALU.mult, op1=ALU.add),
                            R=[ps, self.acol[sub], self.bcol[sub]], W=[self.actT.bs[T]])
                    yield

            for T0 in range(0, NT, 2):
                gens = [chain(T0), chain(T0 + 1)]
                while gens:
                    for g_ in list(gens):
                        try:
                            next(g_)
                        except StopIteration:
                            gens.remove(g_)
            k.barrier()

    def proj(self, wsrc, ncols, col_plan, es_outer=None):
        k = self.k
        with ExitStack() as es:
            wb = [k.sb("pw%d" % i, (128, 16, 512), BF16, es=es) for i in range(2)]
            wv = wsrc.rearrange("(c p) n -> p c n", p=128)
            bi = 0
            pi = 0
            for (c0, width, mode, sink) in col_plan:
                w = wb[bi % 2]
                bi += 1
                for cc in range(0, 16, 4):
                    k.dma("pool", w[:, cc:cc + 4, 0:width], wv[:, cc:cc + 4, c0:c0 + width], W=[w])
                if mode == "tok":
                    for T in range(NT):
                        ps = self.psb[4 + pi % 4]
                        pi += 1
                        for c in range(16):
                            k.op("pe", lambda e, ps=ps, w=w, c=c, T=T: e.matmul(
                                ps[:, 0:width], lhsT=self.actT.t[:, c, T * 128:(T + 1) * 128], rhs=w[:, c, 0:width],
                                start=(c == 0), stop=(c == 15)), R=[self.actT.bs[T], w], W=[ps])
                        sink(T, ps, width, c0)
                else:
                    for m in range(width // 128):
                        for tg in range(4):
                            ps = self.psb[4 + pi % 4]
                            pi += 1
                            for c in range(16):
                                k.op("pe", lambda e, ps=ps, w=w, c=c, m=m, tg=tg: e.matmul(
                                    ps[:, :], lhsT=w[:, c, m * 128:(m + 1) * 128],
                                    rhs=self.actT.t[:, c, tg * 512:(tg + 1) * 512],
                                    start=(c == 0), stop=(c == 15)),
                                    R=[self.actT.bs[tg * 4 + i] for i in range(4)] + [w], W=[ps])
                            sink(m, tg, ps, c0)
            k.barrier()


def build_program(stages=("all",), dbg=()):
    p = Prog(stages, dbg)
    return p


def make_consts():
    c = np.zeros((128, 8, 128), np.float32)
    i = np.arange(128)
    c[:, 0, :] = np.eye(128)
    c[:, 1, :] = 1.0
    c[:, 2, :] = (i[:, None] <= i[None, :])
    c[:, 3, :] = (i[:, None] >= i[None, :])
    c[:, 4, :] = np.where(i[None, :] <= i[:, None], 0.0, NEG)
    c[:, 5, :] = np.where(i[None, :] >= i[:, None], 0.0, NEG)
    return c


def core_inputs(inp, core):
    prompt = core < 4
    m = {}
    f32 = np.float32
    if prompt:
        m["x"] = np.ascontiguousarray(inp["x_prompt"][8 * core:8 * core + 8].reshape(TOK, D))
        cond = inp["c_ctx"]
        L = 256
    else:
        b = core - 4
        m["x"] = np.ascontiguousarray(inp["x_sample"][b])
        cond = inp["c"][b]
        L = 2048
    m["cond_pc"] = np.ascontiguousarray(np.asarray(cond).reshape(16, 128).T)
    m["consts"] = make_consts()
    keep = np.ones((128, 2, 16), f32)
    T = np.arange(16)
    if prompt:
        keep[:, 0, :] = (T % 2 == 0)
        keep[:, 1, :] = (T % 2 == 1)
        m["C0"] = np.zeros((2, 4, 256, 256), f32)
        m["n0"] = np.zeros((2, 4, 256), f32)
        m["m0"] = np.zeros((128, 8), f32)
        m["cache_k"] = np.zeros((8, 512, 128), f32)
        m["cache_v"] = np.zeros((8, 512, 128), f32)
        am = np.full((16, 20), -30000.0, f32)
        for t in range(16):
            am[t, (t // 2) * 2:(t // 2) * 2 + 2] = 0.0
        m["ropec"] = np.ones((TOK, 64), f32)
        m["ropes"] = np.zeros((TOK, 64), f32)
    else:
        m["C0"] = np.ascontiguousarray(inp["state_mlstm_C"][b, 0])
        m["n0"] = np.ascontiguousarray(inp["state_mlstm_n"][b, 0])
        m["m0"] = np.ascontiguousarray(np.broadcast_to(np.asarray(inp["state_mlstm_m"][b, 0]).reshape(1, 8), (128, 8)))
        m["cache_k"] = np.ascontiguousarray(inp["cache_da_k"][b, 0])
        m["cache_v"] = np.ascontiguousarray(inp["cache_da_v"][b, 0])
        am = np.zeros((16, 20), f32)
        t = np.arange(TOK)
        row = (t // 64).astype(f32)
        col = (t % 64).astype(f32)
        inv = (np.float32(10000.0) ** (-np.arange(16, dtype=f32) / np.float32(16))).astype(f32)
        ar = (row[:, None] * inv).astype(f32)
        ac = (col[:, None] * inv).astype(f32)
        m["ropec"] = np.concatenate([np.cos(ar), np.cos(ar), np.cos(ac), np.cos(ac)], 1).astype(f32)
        m["ropes"] = np.concatenate([-np.sin(ar), np.sin(ar), -np.sin(ac), np.sin(ac)], 1).astype(f32)
    m["keep"] = keep
    m["amask"] = np.ascontiguousarray(np.broadcast_to(am[None], (128, 16, 20)))
    m["poolm"] = make_poolm(L)
    return m


_POOLM = {}


def make_poolm(L):
    if L in _POOLM:
        return _POOLM[L]
    pm = np.zeros((16, 3, 4, 128, 128), np.float32)
    pos = np.arange(TOK)
    seq0 = (pos // L) * L
    for g, w in enumerate((2, 4, 8, 16)):
        p = pos - seq0
        lo = np.clip(p - w // 2, 0, L) + seq0
        hi = np.clip(p - w // 2 + w, 0, L) + seq0
        cnt = (hi - lo).astype(np.float32)
        A = np.zeros((TOK, TOK), np.float32)
        for t in range(TOK):
            A[lo[t]:hi[t], t] = np.float32(1.0) / cnt[t]
            A[t, t] -= 1.0
        for T in range(16):
            for r in range(3):
                Tn = T + r - 1
                if 0 <= Tn < 16:
                    pm[T, r, g] = A[Tn * 128:(Tn + 1) * 128, T * 128:(T + 1) * 128]
    _POOLM[L] = pm
    return pm


_PROG = {}


def kernel(**inputs):
    inp = {k_: np.asarray(v) for k_, v in inputs.items()}
    if "p" not in _PROG:
        _PROG["p"] = build_full()
    p = _PROG["p"]
    wnames = [n for n, _ in W_SPEC]
    in_maps = []
    for core in range(8):
        m = core_inputs(inp, core)
        for n in wnames:
            m[n] = inp[n]
        in_maps.append({n: np.ascontiguousarray(m[n], dtype=np.float32) for n in p.I})
    res = run_bass_kernel_spmd(p.k.nc, in_maps, core_ids=list(range(8)))
    r = res.results
    y_prompt = np.concatenate([r[c]["y"].reshape(8, 256, D) for c in range(4)], 0)
    y_sample = np.stack([r[c]["y"] for c in range(4, 8)], 0)
    nC = np.concatenate([r[c]["stC"] for c in range(4)], 0)[:, None]
    nn = np.concatenate([r[c]["stn"] for c in range(4)], 0)[:, None]
    nm = np.concatenate([r[c]["stm"] for c in range(4)], 0)[:, None]

    def cache(name):
        out = []
        for c in range(4):
            a = r[c][name].reshape(8, 2, 8, 128, 128).transpose(0, 2, 1, 3, 4).reshape(8, 8, 256, 128)
            out.append(a)
        return np.concatenate(out, 0)[:, None]
    f = lambda a: np.ascontiguousarray(a, dtype=np.float32)
    return (f(y_prompt), f(y_sample), f(nC), f(nn), f(nm), f(cache("ck")), f(cache("cv")))


def _col_from_row(self, row_ap, n, dst, es):
    k = self.k
    tmp = k.sb("cfr_%d" % self._uid(), (128, n * 128), F32, es=es)
    k.dma("sp", tmp[:], row_ap.partition_broadcast(128), W=[tmp])
    for c in range(n):
        ps = self.psb[c % 2]
        k.op("pe", lambda e, ps=ps, c=c: e.transpose(ps[:, 0:128], tmp[:, c * 128:(c + 1) * 128], self.ident),
             R=[tmp, self.cst], W=[ps])
        k.op("dve", lambda e, ps=ps, c=c: e.tensor_copy(out=dst[:, c:c + 1], in_=ps[:, 0:1]), R=[ps], W=[dst])


def _uid(self):
    self._u = getattr(self, "_u", 0) + 1
    return self._u


def _residual_sink(self, hold, hnew, sub, es):
    k = self.k
    hb = [k.sb("rs_h%d_%d" % (i, self._uid()), (128, 512), F32, es=es) for i in range(2)]
    tb = [k.sb("rs_t%d_%d" % (i, self._uid()), (128, 512), F32, es=es) for i in range(2)]
    cnt = [0]

    def sink(T, ps, width, c0):
        i = cnt[0] % 2
        cnt[0] += 1
        h, t = hb[i], tb[i]
        k.dma("sp", h[:, 0:width], hold.t[T * 128:(T + 1) * 128, c0:c0 + width], R=[hold.bs[T]], W=[h])
        k.op("dve", lambda e: e.tensor_tensor(out=t[:, 0:width], in0=ps[:, 0:width],
                                              in1=self.gate[sub][:, c0:c0 + width], op=ALU.mult),
             R=[ps, self.gate[sub]], W=[t])
        k.op("pool", lambda e: e.tensor_tensor(out=t[:, 0:width], in0=t[:, 0:width], in1=h[:, 0:width], op=ALU.add),
             R=[t, h], W=[t])
        k.dma("sp", hnew.t[T * 128:(T + 1) * 128, c0:c0 + width], t[:, 0:width], R=[t], W=[hnew.bs[T]])
    return sink


def _phase_outproj(self, wsrc, hold, hnew, sub):
    with ExitStack() as es:
        sink = self._residual_sink(hold, hnew, sub, es)
        self.proj(wsrc, D, [(j * 512, 512, "tok", sink) for j in range(4)])


Prog._col_from_row = _col_from_row
Prog._uid = _uid
Prog._residual_sink = _residual_sink
Prog.phase_outproj = _phase_outproj


def _top16(self, src_ap, dst16, scr, R, es_bufs):
    k = self.k
    k.op("dve", lambda e: e.max(out=dst16[:, 0:8], in_=src_ap), R=R, W=[dst16])
    k.op("dve", lambda e: e.match_replace(out=scr[:], in_to_replace=dst16[:, 0:8], in_values=src_ap, imm_value=-1.0),
         R=R + [dst16], W=[scr])
    k.op("dve", lambda e: e.max(out=dst16[:, 8:16], in_=scr[:]), R=[scr], W=[dst16])


Prog._top16 = _top16


def _phase_peer_prep(self, l):
    k = self.k
    if not hasattr(self, "GS"):
        self.GS = k.dram("GS", (128, 128, TOK), BF16, nbuf=128)
        self.VB = k.dram("VB", (128, 128, D), BF16, nbuf=128)
        self.qTs = k.dram("qTs", (16, 128, TOK), F32, nbuf=16)
    U = self.I["peer_u"].t[l].rearrange("(i j) d -> i j d", j=128)
    V = self.I["peer_v"].t[l].rearrange("(i j) d -> i j d", j=128)
    P = self.psb
    with ExitStack() as es:
        ub = [k.sb("pu%d" % i, (128, D), BF16, es=es) for i in range(2)]
        vb = [k.sb("pv%d" % i, (128, D), BF16, es=es) for i in range(2)]
        ut = [k.sb("put%d" % i, (128, 16, 128), BF16, es=es) for i in range(2)]
        gsb = [k.sb("pgs%d" % i, (128, TOK), BF16, es=es) for i in range(2)]
        sk = k.sb("psk", (128, 128), F32, es=es)
        for m in range(16):
            k.dma("sp", sk[:], self.I["peer_subkeys"].t[l, m // 2, m % 2], W=[sk])
            ps = P[m % 2]
            k.op("pe", lambda e, ps=ps: e.transpose(ps[:, 0:128], sk[:], self.ident), R=[sk, self.cst], W=[ps])
            k.op("act", lambda e, ps=ps, m=m: e.copy(out=self.skT[:, m, :], in_=ps[:, 0:128]), R=[ps], W=[self.skT])
        nh = 0
        for i in range(128):
            u, v, t, g = ub[i % 2], vb[i % 2], ut[i % 2], gsb[i % 2]
            k.dma("pool", u[:], U[i], W=[u])
            k.dma("pool", v[:], V[i], W=[v])
            k.dma("act", self.VB.t[i], v[:], R=[v], W=[self.VB.bs[i]])
            for c in range(16):
                ps = P[c // 8]
                psv = ps.t[:].bitcast(BF16)
                k.op("pe", lambda e, psv=psv, u=u, c=c: e.transpose(psv[:, (c % 8) * 128:(c % 8 + 1) * 128],
                                                                   u[:, c * 128:(c + 1) * 128], self.identb),
                     R=[u, self.cstb], W=[ps])
                if c % 8 == 7:
                    dst = t[:, c - 7:c + 1, :]
                    src = psv[:, 0:1024].rearrange("p (c j) -> p c j", j=128)
                    if c == 7:
                        k.op("act", lambda e, src=src, dst=dst: e.copy(out=dst, in_=src), R=[ps], W=[t])
                    else:
                        k.op("dve", lambda e, src=src, dst=dst: e.tensor_copy(out=dst, in_=src), R=[ps], W=[t])
            for half in range(2):
                pa, pb2 = P[2 + 2 * (nh % 3)], P[3 + 2 * (nh % 3)]
                nh += 1
                for q4, pbank in ((0, pa), (1, pb2)):
                    tg = half * 2 + q4
                    for c in range(16):
                        k.op("pe", lambda e, c=c, tg=tg, pbank=pbank, t=t: e.matmul(
                            pbank[:, :], lhsT=t[:, c, :], rhs=self.actT.t[:, c, tg * 512:(tg + 1) * 512],
                            start=(c == 0), stop=(c == 15)),
                            R=[t] + [self.actT.bs[tg * 4 + x] for x in range(4)], W=[pbank])
                    k.op("act", lambda e, tg=tg, pbank=pbank, g=g: e.activation(
                        out=g[:, tg * 512:(tg + 1) * 512], in_=pbank[:, :], func=AF.Gelu_apprx_tanh), R=[pbank], W=[g])
            k.dma("sp", self.GS.t[i], g[:], R=[g], W=[self.GS.bs[i]])
        k.barrier()


def _phase_peer_q(self, l):
    k = self.k
    with ExitStack() as es:
        st = [k.sb("pq%d" % i, (128, 512), F32, es=es) for i in range(2)]
        cnt = [0]

        def sink(m, tg, ps, c0):
            s = st[cnt[0] % 2]
            cnt[0] += 1
            mm = c0 // 128 + m
            eng = "act" if cnt[0] % 2 else "dve"
            if eng == "act":
                k.op("act", lambda e: e.copy(out=s[:], in_=ps[:]), R=[ps], W=[s])
            else:
                k.op("dve", lambda e: e.tensor_copy(out=s[:], in_=ps[:]), R=[ps], W=[s])
            k.dma("sp", self.qTs.t[mm, :, tg * 512:(tg + 1) * 512], s[:], R=[s], W=[self.qTs.bs[mm]])
        self.proj(self.I["peer_wq"].t[l], D, [(j * 512, 512, "feat", sink) for j in range(4)])


def _phase_peer_main(self, l, hold, hnew, tiles=None):
    k = self.k
    IB = 4
    NBLK = 128 // IB
    tiles = list(tiles if tiles is not None else range(NT))
    with ExitStack() as es:
        sets = []
        for si in range(2):
            S = {}
            S["qt"] = k.sb("eq%d" % si, (128, 16, 128), F32, es=es)
            S["s_all"] = k.sb("es%d" % si, (128, 16, 128), F32, es=es)
            S["e_all"] = k.sb("ee%d" % si, (128, 16, 128), F32, es=es)
            S["mx"] = k.sb("emx%d" % si, (128, 16), F32, es=es)
            S["ev"] = k.sb("eev%d" % si, (128, 16, 16), F32, es=es)
            S["ct"] = k.sb("ect%d" % si, (128, 8, 16), F32, es=es)
            S["rz"] = k.sb("erz%d" % si, (128, 8), F32, es=es)
            S["dg"] = k.sb("edg%d" % si, (128, 8, 128), BF16, es=es)
            sets.append(S)
        scr_sh = k.sb("escr", (128, 256), F32, es=es)
        cE_sh = k.sb("ecE", (128, 8, 256), F32, es=es)
        for S in sets:
            S["scr"] = scr_sh
            S["cE"] = cE_sh
        HA = 6
        Ea = [k.sb("eEa%d" % i, (128, HA, IB, 128), F32, es=es) for i in range(2)]
        Ed = [k.sb("eEd%d" % i, (128, 8 - HA, IB, 128), F32, es=es) for i in range(2)]
        Gb = [k.sb("eG%d" % i, (128, 8, IB, 128), BF16, es=es) for i in range(2)]
        vb = [k.sb("ev%d" % i, (128, IB, D), BF16, es=es) for i in range(3)]
        ge = [k.sb("ege%d" % i, (128, IB, 128), BF16, es=es) for i in range(3)]
        at = [k.sb("eat%d" % i, (128, IB, 128), BF16, es=es) for i in range(2)]
        sink = self._residual_sink(hold, hnew, 1, es)
        psO = self.psb[0:4]
        psG = self.psb[4:6]
        psS = self.psb[6:8]

        def preamble(T, S):
            tsl = slice(T * 128, (T + 1) * 128)
            qt, s_all, e_all, scr, mx, ev, cE, ct, rz, dg = (S[n] for n in ("qt", "s_all", "e_all", "scr", "mx", "ev", "cE", "ct", "rz", "dg"))
            k.dma("sp", qt[:], self.qTs.t[:, :, tsl].rearrange("m p t -> p m t"), R=self.qTs.bs, W=[qt])
            for half in range(2):
                for mm in range(8):
                    m = half * 8 + mm
                    ps = psS[mm // 4]
                    k.op("pe", lambda e, ps=ps, m=m, mm=mm: e.matmul(ps[:, (mm % 4) * 128:(mm % 4 + 1) * 128], lhsT=qt[:, m, :],
                                                              rhs=self.skT[:, m, :], start=True, stop=True),
                         R=[qt, self.skT], W=[ps])
                for g in range(2):
                    k.op("act", lambda e, g=g, half=half: e.copy(out=s_all[:, half * 8 + g * 4:half * 8 + (g + 1) * 4, :],
                                                                 in_=psS[g][:, :].rearrange("p (m k) -> p m k", k=128)),
                         R=[psS[g]], W=[s_all])
            k.op("dve", lambda e: e.tensor_reduce(out=mx[:], in_=s_all[:], axis=AX.X, op=ALU.max), R=[s_all], W=[mx])
            k.op("dve", lambda e: e.tensor_scalar(out=mx[:], in0=mx[:], scalar1=-1.0, scalar2=None, op0=ALU.mult),
                 R=[mx], W=[mx])
            for m in range(16):
                k.op("act", lambda e, m=m: e.activation(out=e_all[:, m, :], in_=s_all[:, m, :], func=AF.Exp,
                                                        bias=mx[:, m:m + 1]), R=[s_all, mx], W=[e_all])
            for m in range(16):
                k.op("dve", lambda e, m=m: e.max(out=ev[:, m, 0:8], in_=e_all[:, m, :]), R=[e_all], W=[ev])
                k.op("dve", lambda e, m=m: e.match_replace(out=scr[:, 0:128], in_to_replace=ev[:, m, 0:8],
                                                           in_values=e_all[:, m, :], imm_value=-1.0),
                     R=[e_all, ev], W=[scr])
                k.op("dve", lambda e, m=m: e.max(out=ev[:, m, 8:16], in_=scr[:, 0:128]), R=[scr], W=[ev])
                k.op("dve", lambda e, m=m: e.scalar_tensor_tensor(out=e_all[:, m, :], in0=e_all[:, m, :],
                                                                  scalar=ev[:, m, 15:16], in1=e_all[:, m, :],
                                                                  op0=ALU.is_ge, op1=ALU.mult),
                     R=[e_all, ev], W=[e_all])
            for h in range(HA):
                for a_ in range(16):
                    k.op("act", lambda e, h=h, a_=a_: e.activation(
                        out=cE[:, h, a_ * 16:(a_ + 1) * 16], in_=ev[:, 2 * h + 1, :], func=AF.Identity,
                        scale=ev[:, 2 * h, a_:a_ + 1]), R=[ev], W=[cE])
            for h in range(8):
                if h >= HA:
                    k.op("dve", lambda e, h=h: e.tensor_tensor(
                        out=cE[:, h, :].rearrange("p (a b) -> p a b", b=16),
                        in0=ev[:, 2 * h, :].unsqueeze(2).to_broadcast([128, 16, 16]),
                        in1=ev[:, 2 * h + 1, :].unsqueeze(1).to_broadcast([128, 16, 16]), op=ALU.mult),
                        R=[ev], W=[cE])
                k.op("dve", lambda e, h=h: e.max(out=ct[:, h, 0:8], in_=cE[:, h, :]), R=[cE], W=[ct])
                k.op("dve", lambda e, h=h: e.match_replace(out=scr[:], in_to_replace=ct[:, h, 0:8], in_values=cE[:, h, :],
                                                           imm_value=-1.0), R=[cE, ct], W=[scr])
                k.op("dve", lambda e, h=h: e.max(out=ct[:, h, 8:16], in_=scr[:]), R=[scr], W=[ct])
            k.op("dve", lambda e: e.tensor_reduce(out=rz[:], in_=ct[:], axis=AX.X, op=ALU.add), R=[ct], W=[rz])
            k.op("dve", lambda e: e.reciprocal(out=rz[:], in_=rz[:]), R=[rz], W=[rz])
            for h in range(8):
                k.op("dve", lambda e, h=h: e.tensor_scalar(out=dg[:, h, :], in0=self.ident, scalar1=rz[:, h:h + 1],
                                                           scalar2=None, op0=ALU.mult), R=[rz, self.cst], W=[dg])
            if "pe_s" in self.dbg and T == 0:
                k.dma("sp", self.dbg["pe_s"].t[:, :, :], s_all[:], R=[s_all], W=[self.dbg["pe_s"]])
                k.dma("sp", self.dbg["pe_e"].t[:, :, :], e_all[:], R=[e_all], W=[self.dbg["pe_e"]])
                k.dma("sp", self.dbg["pe_ct"].t[:, :, :], ct[:], R=[ct], W=[self.dbg["pe_ct"]])
                k.dma("sp", self.dbg["pe_rz"].t[:, :], rz[:], R=[rz], W=[self.dbg["pe_rz"]])

        nb = [0]

        def front(T, S, ib):
            tsl = slice(T * 128, (T + 1) * 128)
            e_all, ct, dg = S["e_all"], S["ct"], S["dg"]
            n = nb[0]
            nb[0] += 1
            v, gt = vb[n % 3], ge[n % 3]
            pg, a_t = psG[n % 2], at[n % 2]
            EA, ED, G = Ea[n % 2], Ed[n % 2], Gb[n % 2]
            i0 = ib * IB
            e4 = e_all[:].rearrange("p (h q) k -> p h q k", q=2)
            k.dma("sp", gt[:], self.GS.t[i0:i0 + IB, :, tsl].rearrange("i j t -> j i t"),
                  R=self.GS.bs[i0:i0 + IB], W=[gt])
            k.dma("act", v[:], self.VB.t[i0:i0 + IB].rearrange("i j d -> j i d"),
                  R=self.VB.bs[i0:i0 + IB], W=[v])
            for h in range(HA):
                for ii in range(IB):
                    k.op("act", lambda e, h=h, ii=ii: e.activation(
                        out=EA[:, h, ii, :], in_=e4[:, h, 1, :], func=AF.Identity,
                        scale=e4[:, h, 0, i0 + ii:i0 + ii + 1]), R=[e_all], W=[EA])
            k.op("dve", lambda e: e.tensor_tensor(
                out=ED[:], in0=e4[:, HA:8, 0, i0:i0 + IB].unsqueeze(3).to_broadcast([128, 8 - HA, IB, 128]),
                in1=e4[:, HA:8, 1, :].unsqueeze(2).to_broadcast([128, 8 - HA, IB, 128]), op=ALU.mult),
                R=[e_all], W=[ED])
            for h in range(8):
                Eh = EA[:, h] if h < HA else ED[:, h - HA]
                Et = EA if h < HA else ED
                k.op("dve", lambda e, h=h, Eh=Eh: e.scalar_tensor_tensor(
                    out=G[:, h], in0=Eh, scalar=ct[:, h, 15:16], in1=Eh, op0=ALU.is_ge, op1=ALU.mult),
                    R=[Et, ct], W=[G])
            for ii in range(IB):
                for h in range(8):
                    k.op("pe", lambda e, h=h, ii=ii: e.matmul(
                        pg[:, ii * 128:(ii + 1) * 128], lhsT=G[:, h, ii, :], rhs=dg[:, h, :],
                        start=(h == 0), stop=(h == 7)), R=[G, dg], W=[pg])
            return (i0, v, gt, pg, a_t)

        def back(st):
            i0, v, gt, pg, a_t = st
            k.op("dve", lambda e: e.tensor_tensor(
                out=a_t[:].rearrange("p i t -> p (i t)"), in0=gt[:].rearrange("p i t -> p (i t)"),
                in1=pg[:, 0:IB * 128], op=ALU.mult), R=[gt, pg], W=[a_t])
            for ii in range(IB):
                i = i0 + ii
                for dblk in range(4):
                    k.op("pe", lambda e, ii=ii, dblk=dblk, i=i: e.matmul(
                        psO[dblk][:, :], lhsT=a_t[:, ii, :], rhs=v[:, ii, dblk * 512:(dblk + 1) * 512],
                        start=(i == 0), stop=(i == 127), skip_group_check=True), R=[a_t, v], W=[psO[dblk]])

        preamble(tiles[0], sets[0])
        for ti, T in enumerate(tiles):
            S = sets[ti % 2]
            pending = None
            for ib in range(NBLK):
                st = front(T, S, ib)
                if pending is not None:
                    back(pending)
                pending = st
                if ib == NBLK // 2 and ti + 1 < len(tiles):
                    preamble(tiles[ti + 1], sets[(ti + 1) % 2])
            back(pending)
            for dblk in range(4):
                sink(T, psO[dblk], 512, dblk * 512)
        k.barrier()


Prog.phase_peer_prep = _phase_peer_prep
Prog.phase_peer_q = _phase_peer_q
Prog.phase_peer_main = _phase_peer_main


def _phase_even(self, hold, hnew):
    k = self.k
    qTs = k.dram("e_qT", (8, 128, TOK), BF16, nbuf=8)
    kTs = k.dram("e_kT", (8, 128, TOK), BF16, nbuf=8)
    ks = k.dram("e_k", (TOK, 1024), BF16, nbuf=16)
    vs = k.dram("e_v", (TOK, 1024), BF16, nbuf=16)
    os_ = k.dram("e_o", (TOK, 1024), F32, nbuf=16)
    pps = k.dram("e_p", (TOK, 1024), F32, nbuf=16)
    gs = k.dram("e_g", (TOK, 16), F32, nbuf=16)
    hF = k.dram("e_hF", (TOK, 1024), F32, nbuf=16)
    with ExitStack() as es:
        sb16 = [k.sb("ep_b%d" % i, (128, 512), BF16, es=es) for i in range(2)]
        sf32 = [k.sb("ep_f%d" % i, (128, 512), F32, es=es) for i in range(2)]
        cnt = [0]

        def sink_feat(m, tg, ps, c0):
            s = sb16[cnt[0] % 2]
            cnt[0] += 1
            isk = c0 >= 1024
            dst = kTs if isk else qTs
            mm = (c0 - (1024 if isk else 0)) // 128 + m
            k.op("act", lambda e: e.activation(out=s[:], in_=ps[:], func=AF.Identity, scale=(0.0625 if isk else 1.0)),
                 R=[ps], W=[s])
            k.dma("sp", dst.t[mm, :, tg * 512:(tg + 1) * 512], s[:], R=[s], W=[dst.bs[mm]])

        def sink_tok(T, ps, width, c0):
            i = cnt[0] % 2
            cnt[0] += 1
            tsl = slice(T * 128, (T + 1) * 128)
            if c0 < 3072:
                s = sb16[i]
                dst, col, sc = (ks, c0 - 1024, 0.0625) if c0 < 2048 else (vs, c0 - 2048, 1.0)
                k.op("act", lambda e: e.activation(out=s[:, 0:width], in_=ps[:, 0:width], func=AF.Identity, scale=sc),
                     R=[ps], W=[s])
            else:
                s = sf32[i]
                dst, col = (os_, c0 - 3072) if c0 < 4096 else ((pps, c0 - 4096) if c0 < 5120 else (gs, 0))
                k.op("dve", lambda e: e.tensor_copy(out=s[:, 0:width], in_=ps[:, 0:width]), R=[ps], W=[s])
            k.dma("sp", dst.t[tsl, col:col + width], s[:, 0:width], R=[s], W=[dst.bs[T]])
        plan = [(0, 512, "feat", sink_feat), (512, 512, "feat", sink_feat),
                (1024, 512, "feat", sink_feat), (1536, 512, "feat", sink_feat)]
        plan += [(c0, 512, "tok", sink_tok) for c0 in range(1024, 5120, 512)]
        plan += [(5120, 16, "tok", sink_tok)]
        self.proj(self.I["w_in_even"].t[0], 5136, plan)
    with ExitStack() as es:
        Cn = [[k.sb("Cn%d%d" % (d, h), (128, 2, 257), F32, es=es) for h in range(4)] for d in range(2)]
        Cb = [[k.sb("Cb%d%d" % (d, h), (128, 2, 257), BF16, es=es) for h in range(4)] for d in range(2)]
        mrep = k.sb("mrep", (128, 8), F32, es=es)
        keep = k.sb("keep", (128, 2, 16), F32, es=es)
        bg = k.sb("bg", (128, 16), F32, es=es)
        gcol = k.sb("gcol", (128, 8), F32, es=es)
        pscol = k.sb("pscol", (128, 8), F32, es=es)
        pw = k.sb("pw", (128, 4, 2, 256), BF16, es=es)
        k.dma("sp", mrep[:], self.I["m0"].t[:, :], W=[mrep])
        k.dma("sp", keep[:], self.I["keep"].t[:, :, :], W=[keep])
        k.dma("sp", bg[:], self.I["b_gate_even"].t[0].partition_broadcast(128), W=[bg])
        self._col_from_row(self.I["mlstm_gain"].t[0], 8, gcol, es)
        self._col_from_row(self.I["pool_scale"].t[0], 8, pscol, es)
        for g in range(4):
            for cc in range(2):
                k.dma("pool", pw[:, g, cc, :], self.I["pool_w"].t[0, g, cc * 128:(cc + 1) * 128, :], W=[pw])
        for d in range(2):
            for h in range(4):
                for cc in range(2):
                    k.dma("sp", Cn[d][h][:, cc, 0:256], self.I["C0"].t[d, h, cc * 128:(cc + 1) * 128, :], W=[Cn[d][h]])
                    k.dma("sp", Cn[d][h][:, cc, 256:257],
                          self.I["n0"].t[d, h, cc * 128:(cc + 1) * 128].rearrange("(p o) -> p o", o=1), W=[Cn[d][h]])
                k.op("dve", lambda e, d=d, h=h: e.tensor_copy(out=Cb[d][h][:], in_=Cn[d][h][:]), R=[Cn[d][h]], W=[Cb[d][h]])
        qTt = [k.sb("m_q%d" % i, (128, 8, 128), BF16, es=es) for i in range(2)]
        kTt = [k.sb("m_kT%d" % i, (128, 8, 128), BF16, es=es) for i in range(2)]
        kt = [k.sb("m_k%d" % i, (128, 1024), BF16, es=es) for i in range(2)]
        vx = [k.sb("m_v%d" % i, (128, 4, 257), BF16, es=es) for i in range(2)]
        gt = [k.sb("m_g%d" % i, (128, 16), F32, es=es) for i in range(2)]
        for i in range(2):
            k.op("pool", lambda e, i=i: e.memset(vx[i][:, :, 256:257], 1.0), W=[vx[i]])
        sm = {n: k.sb("m_" + n, (128, w), F32, es=es) for n, w in
              [("gg", 16), ("ab", 4), ("ex", 4), ("mn", 4), ("fl", 4), ("bsb", 8), ("r", 4), ("rmax", 1), ("dmax", 1),
               ("inter", 1), ("mt", 1), ("nmt", 1), ("a", 1), ("eneg", 1), ("den", 1), ("mm", 1), ("nmm", 1),
               ("dec", 1), ("ws", 1), ("mnew", 1), ("ss", 4)]}
        SH = []
        for hh in range(4):
            H = {n: k.sb("mh%d_%s" % (hh, n), (128, 1), F32, es=es) for n in
                 ("rmax", "dmax", "inter", "mt", "nmt", "a", "eneg", "den", "mm", "nmm", "dec", "ws", "mnew")}
            H["dg"] = k.sb("mh%d_dg" % hh, (128, 128), F32, es=es)
            H["dmat"] = k.sb("mh%d_dmat" % hh, (128, 128), F32, es=es)
            H["wt"] = k.sb("mh%d_w" % hh, (128, 128), F32, es=es)
            H["smat"] = k.sb("mh%d_smat" % hh, (128, 128), BF16, es=es)
            H["smT"] = k.sb("mh%d_smT" % hh, (128, 128), BF16, es=es)
            H["qca"] = k.sb("mh%d_qca" % hh, (128, 257), F32, es=es)
            H["num"] = k.sb("mh%d_num" % hh, (128, 257), F32, es=es)
            H["kws"] = k.sb("mh%d_kws" % hh, (128, 256), BF16, es=es)
            SH.append(H)
        hsum = k.sb("m_hsum", (128, 4, 256), F32, es=es)
        ot = k.sb("m_o", (128, 1024), F32, es=es)
        hn = k.sb("m_hn", (128, 1024), F32, es=es)
        hm = k.sb("m_hm", (128, 1024), BF16, es=es)
        junk = k.sb("m_junk", (128, 256), F32, es=es)
        pt = [k.sb("m_pt%d" % i, (128, 1024), F32, es=es) for i in range(3)]
        pm = k.sb("m_pm", (128, 3, 4, 128), F32, es=es)
        pld = k.sb("m_pld", (128, 2, 128), BF16, es=es)
        P = self.psb
        SS = [sm, dict(sm)]
        for n_ in ("gg", "ab", "ex", "mn", "fl", "bsb", "r"):
            SS[1][n_] = k.sb("m2_" + n_, tuple(sm[n_].t.shape), F32, es=es)
        steps = [(0, T) for T in range(NT)] + [(1, T) for T in range(NT - 1, -1, -1)]

        def prologue(si):
            d, T = steps[si]
            tsl = slice(T * 128, (T + 1) * 128)
            i = si % 2
            S = SS[i]
            tri = self.triF if d == 0 else self.triB
            q_, kT_, k_, v_, g_ = qTt[i], kTt[i], kt[i], vx[i], gt[i]
            k.dma("sp", q_[:], qTs.t[:, :, tsl].rearrange("m p t -> p m t"), R=qTs.bs, W=[q_])
            k.dma("sp", kT_[:], kTs.t[:, :, tsl].rearrange("m p t -> p m t"), R=kTs.bs, W=[kT_])
            k.dma("sp", k_[:], ks.t[tsl, :], R=[ks.bs[T]], W=[k_])
            k.dma("sp", v_[:, :, 0:256], vs.t[tsl, :].rearrange("t (h d) -> t h d", d=256), R=[vs.bs[T]], W=[v_])
            k.dma("sp", g_[:], gs.t[tsl, :], R=[gs.bs[T]], W=[g_])
            yield
            k.op("dve", lambda e: e.tensor_tensor(out=S["gg"][:], in0=g_[:], in1=bg[:], op=ALU.add), R=[g_, bg], W=[S["gg"]])
            fg = S["gg"][:, 8 + d * 4:12 + d * 4]
            ig = S["gg"][:, d * 4:d * 4 + 4]
            yield
            k.op("dve", lambda e: e.tensor_scalar(out=S["ab"][:], in0=fg, scalar1=-1.0, scalar2=None, op0=ALU.mult), R=[S["gg"]], W=[S["ab"]])
            k.op("dve", lambda e: e.tensor_tensor(out=S["ab"][:], in0=S["ab"][:], in1=fg, op=ALU.max), R=[S["gg"], S["ab"]], W=[S["ab"]])
            yield
            k.op("act", lambda e: e.activation(out=S["ex"][:], in_=S["ab"][:], func=AF.Exp, scale=-1.0), R=[S["ab"]], W=[S["ex"]])
            yield
            k.op("act", lambda e: e.activation(out=S["ex"][:], in_=S["ex"][:], func=AF.Ln, bias=1.0), R=[S["ex"]], W=[S["ex"]])
            k.op("dve", lambda e: e.tensor_scalar(out=S["mn"][:], in0=fg, scalar1=0.0, scalar2=None, op0=ALU.min), R=[S["gg"]], W=[S["mn"]])
            yield
            k.op("dve", lambda e: e.tensor_tensor(out=S["fl"][:], in0=S["mn"][:], in1=S["ex"][:], op=ALU.subtract), R=[S["mn"], S["ex"]], W=[S["fl"]])
            yield
            k.op("pe", lambda e: e.matmul(P[0][:, 0:4], lhsT=tri, rhs=S["fl"][:], start=True, stop=True), R=[S["fl"], self.cst], W=[P[0]])
            k.op("pe", lambda e: e.matmul(P[0][:, 4:8], lhsT=self.ones, rhs=S["fl"][:], start=True, stop=True), R=[S["fl"], self.cst], W=[P[0]])
            k.op("dve", lambda e: e.tensor_copy(out=S["bsb"][:], in_=P[0][:, 0:8]), R=[P[0]], W=[S["bsb"]])
            yield
            k.op("dve", lambda e: e.tensor_tensor(out=S["r"][:], in0=ig, in1=S["bsb"][:, 0:4], op=ALU.subtract), R=[S["gg"], S["bsb"]], W=[S["r"]])
            yield

        def pool_body(T):
            tsl = slice(T * 128, (T + 1) * 128)
            rs = [r for r in range(3) if 0 <= T + r - 1 < NT]
            for r in rs:
                Tn = T + r - 1
                k.dma("sp", pt[r][:], pps.t[Tn * 128:(Tn + 1) * 128, :], R=[pps.bs[Tn]], W=[pt[r]])
            k.dma("sp", pm[:], self.I["poolm"].t[T].rearrange("r g s t -> s r g t"), W=[pm])
            yield
            for g in range(4):
                for cc in range(2):
                    for r in rs:
                        k.op("pe", lambda e, g=g, cc=cc, r=r: e.matmul(P[1][:, 256 + cc * 128:256 + (cc + 1) * 128], lhsT=pt[r][:, g * 256 + cc * 128:g * 256 + (cc + 1) * 128], rhs=pm[:, r, g, :], start=(r == rs[0]), stop=(r == rs[-1])), R=[pt[r], pm], W=[P[1]])
                k.op("act", lambda e: e.copy(out=pld[:].rearrange("p c t -> p (c t)"), in_=P[1][:, 256:512]), R=[P[1]], W=[pld])
                yield
                for dd in range(2):
                    for cc in range(2):
                        k.op("pe", lambda e, g=g, cc=cc, dd=dd: e.matmul(P[2][:, 256:384], lhsT=pw[:, g, cc, dd * 128:(dd + 1) * 128], rhs=pld[:, cc, :], start=(cc == 0), stop=(cc == 1)), R=[pw, pld], W=[P[2]])
                    c = 8 + g * 2 + dd
                    k.op("dve", lambda e, c=c, g=g, dd=dd: e.tensor_scalar(out=self.actT.t[:, c, tsl], in0=P[2][:, 256:384], scalar1=pscol[:, g * 2 + dd:g * 2 + dd + 1], scalar2=None, op0=ALU.mult), R=[P[2], pscol], W=[self.actT.bs[T]])
                    yield

        for _ in prologue(0):
            pass
        for si, (d, T) in enumerate(steps):
            if True:
                msk = self.maskF if d == 0 else self.maskB
                tsl = slice(T * 128, (T + 1) * 128)
                i = si % 2
                S = SS[i]
                q_, kT_, k_, v_, g_ = qTt[i], kTt[i], kt[i], vx[i], gt[i]
                if d == 1:
                    k.dma("sp", hsum[:], hF.t[tsl, :].rearrange("t (h d) -> t h d", d=256), R=[hF.bs[T]], W=[hsum])
                def head_body(h):
                    H = SH[h]
                    col = d * 4 + h
                    C_, Cb_ = Cn[d][h], Cb[d][h]
                    b_h = S["bsb"][:, h:h + 1]
                    be_h = S["bsb"][:, 4 + h:5 + h]
                    m_h = mrep[:, col:col + 1]
                    dg, dmat, wt, smat, smT, qca, num, kws = (H[n] for n in ("dg", "dmat", "wt", "smat", "smT", "qca", "num", "kws"))
                    k.op("dve", lambda e: e.tensor_scalar(out=dg[:], in0=self.ident, scalar1=S["r"][:, h:h + 1], scalar2=None, op0=ALU.mult), R=[S["r"], self.cst], W=[dg])
                    yield
                    k.op("pe", lambda e: e.matmul(P[1][:, 0:128], lhsT=self.ones, rhs=dg[:], start=True, stop=True), R=[dg, self.cst], W=[P[1]])
                    k.op("dve", lambda e: e.tensor_reduce(out=H["rmax"][:], in_=P[1][:, 0:128], axis=AX.X, op=ALU.max), R=[P[1]], W=[H["rmax"]])
                    k.op("dve", lambda e: e.scalar_tensor_tensor(out=dmat[:], in0=P[1][:, 0:128], scalar=b_h, in1=msk, op0=ALU.add, op1=ALU.add), R=[P[1], S["bsb"], self.cst], W=[dmat])
                    yield
                    k.op("dve", lambda e: e.tensor_reduce(out=H["dmax"][:], in_=dmat[:], axis=AX.X, op=ALU.max), R=[dmat], W=[H["dmax"]])
                    k.op("dve", lambda e: e.tensor_tensor(out=H["inter"][:], in0=b_h, in1=m_h, op=ALU.add), R=[S["bsb"], mrep], W=[H["inter"]])
                    yield
                    k.op("dve", lambda e: e.tensor_tensor(out=H["mt"][:], in0=H["inter"][:], in1=H["dmax"][:], op=ALU.max), R=[H["inter"], H["dmax"]], W=[H["mt"]])
                    k.op("dve", lambda e: e.tensor_scalar(out=H["nmt"][:], in0=H["mt"][:], scalar1=-1.0, scalar2=None, op0=ALU.mult), R=[H["mt"]], W=[H["nmt"]])
                    yield
                    k.op("act", lambda e: e.activation(out=wt[:], in_=dmat[:], func=AF.Exp, bias=H["nmt"][:, 0:1]), R=[dmat, H["nmt"]], W=[wt])
                    k.op("act", lambda e: e.activation(out=H["a"][:], in_=H["inter"][:], func=AF.Exp, bias=H["nmt"][:, 0:1]), R=[H["inter"], H["nmt"]], W=[H["a"]])
                    k.op("act", lambda e: e.activation(out=H["eneg"][:], in_=H["mt"][:], func=AF.Exp, scale=-1.0), R=[H["mt"]], W=[H["eneg"]])
                    yield
                    k.op("dve", lambda e: e.tensor_tensor(out=H["mm"][:], in0=m_h, in1=H["rmax"][:], op=ALU.max), R=[mrep, H["rmax"]], W=[H["mm"]])
                    k.op("dve", lambda e: e.tensor_scalar(out=H["nmm"][:], in0=H["mm"][:], scalar1=-1.0, scalar2=None, op0=ALU.mult), R=[H["mm"]], W=[H["nmm"]])
                    yield
                    k.op("act", lambda e: e.activation(out=H["dec"][:], in_=m_h, func=AF.Exp, bias=H["nmm"][:, 0:1]), R=[mrep, H["nmm"]], W=[H["dec"]])
                    k.op("act", lambda e: e.activation(out=H["ws"][:], in_=S["r"][:, h:h + 1], func=AF.Exp, bias=H["nmm"][:, 0:1]), R=[S["r"], H["nmm"]], W=[H["ws"]])
                    k.op("dve", lambda e: e.tensor_tensor(out=H["mnew"][:], in0=H["mm"][:], in1=be_h, op=ALU.add), R=[H["mm"], S["bsb"]], W=[H["mnew"]])
                    yield
                    for cc in range(2):
                        k.op("pe", lambda e, cc=cc: e.matmul(P[2][:, 0:128], lhsT=q_[:, h * 2 + cc, :], rhs=kT_[:, h * 2 + cc, :], start=(cc == 0), stop=(cc == 1)), R=[q_, kT_], W=[P[2]])
                    k.op("dve", lambda e: e.tensor_tensor(out=smat[:], in0=P[2][:, 0:128], in1=wt[:], op=ALU.mult), R=[P[2], wt], W=[smat])
                    yield
                    p3v = P[3].t[:].bitcast(BF16)
                    k.op("pe", lambda e: e.transpose(p3v[:, 0:128], smat[:], self.identb), R=[smat, self.cstb], W=[P[3]])
                    k.op("act", lambda e: e.copy(out=smT[:], in_=p3v[:, 0:128]), R=[P[3]], W=[smT])
                    yield
                    for cc in range(2):
                        k.op("pe", lambda e, cc=cc: e.matmul(P[4][:, 0:257], lhsT=q_[:, h * 2 + cc, :], rhs=Cb_[:, cc, :], start=(cc == 0), stop=(cc == 1)), R=[q_, Cb_], W=[P[4]])
                    k.op("act", lambda e: e.activation(out=qca[:], in_=P[4][:, 0:257], func=AF.Identity, scale=H["a"][:, 0:1]), R=[P[4], H["a"]], W=[qca])
                    yield
                    k.op("pe", lambda e: e.matmul(P[5][:, 0:257], lhsT=smT[:], rhs=v_[:, h, :], start=True, stop=True), R=[smT, v_], W=[P[5]])
                    k.op("dve", lambda e: e.tensor_tensor(out=num[:], in0=P[5][:, 0:257], in1=qca[:], op=ALU.add), R=[P[5], qca], W=[num])
                    yield
                    k.op("dve", lambda e: e.tensor_scalar(out=H["den"][:], in0=num[:, 256:257], scalar1=-1.0, scalar2=None, op0=ALU.mult), R=[num], W=[H["den"]])
                    k.op("dve", lambda e: e.tensor_tensor(out=H["den"][:], in0=H["den"][:], in1=num[:, 256:257], op=ALU.max), R=[num, H["den"]], W=[H["den"]])
                    k.op("dve", lambda e: e.tensor_tensor(out=H["den"][:], in0=H["den"][:], in1=H["eneg"][:], op=ALU.max), R=[H["eneg"], H["den"]], W=[H["den"]])
                    k.op("dve", lambda e: e.reciprocal(out=H["den"][:], in_=H["den"][:]), R=[H["den"]], W=[H["den"]])
                    yield
                    if d == 0:
                        k.op("dve", lambda e: e.tensor_scalar(out=hsum[:, h, :], in0=num[:, 0:256], scalar1=H["den"][:, 0:1], scalar2=None, op0=ALU.mult), R=[num, H["den"]], W=[hsum])
                    else:
                        k.op("dve", lambda e: e.scalar_tensor_tensor(out=hsum[:, h, :], in0=num[:, 0:256], scalar=H["den"][:, 0:1], in1=hsum[:, h, :], op0=ALU.mult, op1=ALU.add), R=[num, H["den"], hsum], W=[hsum])
                    k.op("dve", lambda e: e.tensor_scalar(out=kws[:], in0=k_[:, h * 256:(h + 1) * 256], scalar1=H["ws"][:, 0:1], scalar2=None, op0=ALU.mult), R=[k_, H["ws"]], W=[kws])
                    yield
                    for cc in range(2):
                        pu = P[6 + cc]
                        k.op("pe", lambda e, cc=cc, pu=pu: e.matmul(pu[:, 0:257], lhsT=kws[:, cc * 128:(cc + 1) * 128], rhs=v_[:, h, :], start=True, stop=True), R=[kws, v_], W=[pu])
                        k.op("dve", lambda e, cc=cc, pu=pu: e.scalar_tensor_tensor(out=C_[:, cc, :], in0=C_[:, cc, :], scalar=H["dec"][:, 0:1], in1=pu[:, 0:257], op0=ALU.mult, op1=ALU.add), R=[C_, H["dec"], pu], W=[C_])
                        yield
                    if (d == 0 and T % 2 == 1) or (d == 1 and T % 2 == 0):
                        slot = T // 2
                        for cc in range(2):
                            k.dma("sp", self.O["stC"].t[slot, d, h, cc * 128:(cc + 1) * 128, :], C_[:, cc, 0:256], R=[C_], W=[Buf()])
                            k.dma("sp", self.O["stn"].t[slot, d, h, cc * 128:(cc + 1) * 128].rearrange("(p o) -> p o", o=1), C_[:, cc, 256:257], R=[C_], W=[Buf()])
                        k.dma("sp", self.O["stm"].t[slot, d, h:h + 1].rearrange("(p o) -> p o", o=1), H["mnew"][0:1, 0:1], R=[H["mnew"]], W=[Buf()])
                    kf = keep[:, d, T:T + 1]
                    k.op("dve", lambda e: e.tensor_scalar(out=C_[:], in0=C_[:], scalar1=kf, scalar2=None, op0=ALU.mult), R=[C_, keep], W=[C_])
                    k.op("act", lambda e: e.copy(out=Cb_[:], in_=C_[:]), R=[C_], W=[Cb_])
                    k.op("dve", lambda e: e.tensor_tensor(out=m_h, in0=H["mnew"][:], in1=kf, op=ALU.mult), R=[H["mnew"], keep], W=[mrep])
                    yield

                gens = [head_body(h) for h in range(4)]
                if si + 1 < len(steps):
                    gens.append(prologue(si + 1))
                if d == 0:
                    gens.append(pool_body(T))
                while gens:
                    for g_ in list(gens):
                        try:
                            next(g_)
                        except StopIteration:
                            gens.remove(g_)
                if d == 0:
                    k.dma("sp", hF.t[tsl, :].rearrange("t (h d) -> t h d", d=256), hsum[:], R=[hsum], W=[hF.bs[T]])
                else:
                    for h in range(4):
                        k.op("act", lambda e, h=h: e.activation(out=junk[:], in_=hsum[:, h, :], func=AF.Square, accum_out=S["ss"][:, h:h + 1]), R=[hsum], W=[junk, S["ss"]])
                    self.rstd(S["ss"], 1.0 / 256)
                    k.dma("act", ot[:], os_.t[tsl, :], R=[os_.bs[T]], W=[ot])
                    k.op("act", lambda e: e.activation(out=ot[:], in_=ot[:], func=AF.Sigmoid), R=[ot], W=[ot])
                    for h in range(4):
                        k.op("dve", lambda e, h=h: e.tensor_scalar(out=hn[:, h * 256:(h + 1) * 256], in0=hsum[:, h, :], scalar1=S["ss"][:, h:h + 1], scalar2=None, op0=ALU.mult), R=[hsum, S["ss"]], W=[hn])
                    k.op("dve", lambda e: e.tensor_tensor(out=hm[:], in0=hn[:], in1=ot[:], op=ALU.mult), R=[hn, ot], W=[hm])
                    for c in range(8):
                        pb_ = P[3 + (c % 2) * 2]
                        pv = pb_.t[:].bitcast(BF16)
                        k.op("pe", lambda e, c=c, pv=pv: e.transpose(pv[:, 0:128], hm[:, c * 128:(c + 1) * 128], self.identb), R=[hm, self.cstb], W=[pb_])
                        k.op("act", lambda e, c=c, pv=pv: e.activation(out=self.actT.t[:, c, tsl], in_=pv[:, 0:128], func=AF.Identity, scale=gcol[:, c:c + 1]), R=[pb_, gcol], W=[self.actT.bs[T]])
        k.barrier()
    self.phase_outproj(self.I["w_out_even"].t[0], hold, hnew, 0)


Prog.phase_even = _phase_even


def _phase_odd(self, hold, hnew, l=1):
    k = self.k
    lam_init = 0.8 - 0.6 * math.exp(-0.3 * l)
    zq = k.dram("o_q", (TOK, 1024), F32, nbuf=16)
    zk = k.dram("o_k", (TOK, 1024), F32, nbuf=16)
    zu = k.dram("o_gu", (TOK, 1024), F32, nbuf=16)
    zv = k.dram("o_gv", (TOK, 1024), F32, nbuf=16)
    qTs = k.dram("o_qT", (8, 128, TOK), BF16, nbuf=8)
    kTs = k.dram("o_kT", (8, 128, 2560), BF16, nbuf=8)
    ck, cv = self.O["ck"], self.O["cv"]
    P = self.psb
    with ExitStack() as es:
        sf32 = [k.sb("op_f%d" % i, (128, 512), F32, es=es) for i in range(2)]
        cnt = [0]

        def sink_tok(T, ps, width, c0):
            s = sf32[cnt[0] % 2]
            cnt[0] += 1
            tsl = slice(T * 128, (T + 1) * 128)
            if cnt[0] % 2:
                k.op("act", lambda e: e.copy(out=s[:], in_=ps[:]), R=[ps], W=[s])
            else:
                k.op("dve", lambda e: e.tensor_copy(out=s[:], in_=ps[:]), R=[ps], W=[s])
            j = c0 // 1024
            col = c0 % 1024
            if j == 2:
                h0 = col // 128
                k.dma("sp", cv.t[T, h0:h0 + 4].rearrange("h t d -> t h d"), s[:].rearrange("t (h d) -> t h d", d=128),
                      R=[s], W=[cv.bs[T]])
            else:
                dst = [zq, zk, None, zu, zv][j]
                k.dma("sp", dst.t[tsl, col:col + 512], s[:], R=[s], W=[dst.bs[T]])
        self.proj(self.I["w_in_odd"].t[0], 5120, [(c0, 512, "tok", sink_tok) for c0 in range(0, 5120, 512)])
    with ExitStack() as es:
        gq = k.sb("o_gq", (128, 2, 64), F32, es=es)
        k.dma("sp", gq[:].rearrange("p a d -> p (a d)"), self.I["qk_gain"].t[0].rearrange("a d -> (a d)").partition_broadcast(128), W=[gq])
        xt = [k.sb("o_x%d" % i, (128, 16, 64), F32, es=es) for i in range(2)]
        t1 = k.sb("o_t1", (128, 16, 64), F32, es=es)
        sw = k.sb("o_sw", (128, 16, 64), F32, es=es)
        xb = k.sb("o_xb", (128, 16, 64), BF16, es=es)
        ss = k.sb("o_ss", (128, 16), F32, es=es)
        rc = k.sb("o_rc", (128, 64), F32, es=es)
        rs = k.sb("o_rs", (128, 64), F32, es=es)
        tb = [k.sb("o_tb%d" % i, (128, 128), BF16, es=es) for i in range(2)]
        ckf = k.sb("o_ckf", (128, 128), F32, es=es)
        n = 0
        t1s = [t1, k.sb("o_t1b", (128, 16, 64), F32, es=es)]
        sws = [sw, k.sb("o_swb", (128, 16, 64), F32, es=es)]
        xbs = [xb, k.sb("o_xbb", (128, 16, 64), BF16, es=es)]
        sss = [ss, k.sb("o_ssb", (128, 16), F32, es=es)]
        rcs = [rc, k.sb("o_rcb", (128, 64), F32, es=es)]
        rss = [rs, k.sb("o_rsb", (128, 64), F32, es=es)]
        tbs = [k.sb("o_tbx%d" % i, (128, 128), BF16, es=es) for i in range(4)]
        nn = [0]

        def chain(T, which, src, rc_, rs_):
            tsl = slice(T * 128, (T + 1) * 128)
            x, t1_, sw_, xb_, ss_ = xt[which], t1s[which], sws[which], xbs[which], sss[which]
            k.dma("sp", x[:].rearrange("p a d -> p (a d)"), src.t[tsl, :], R=[src.bs[T]], W=[x])
            k.op("dve", lambda e: e.tensor_tensor(out=t1_[:], in0=x[:], in1=x[:], op=ALU.mult), R=[x], W=[t1_])
            yield
            k.op("dve", lambda e: e.tensor_reduce(out=ss_[:], in_=t1_[:], axis=AX.X, op=ALU.add), R=[t1_], W=[ss_])
            yield
            k.op("dve", lambda e: e.tensor_scalar(out=ss_[:], in0=ss_[:], scalar1=1.0 / 64, scalar2=EPS, op0=ALU.mult, op1=ALU.add), R=[ss_], W=[ss_])
            yield
            k.op("act", lambda e: e.activation(out=ss_[:], in_=ss_[:], func=AF.Sqrt), R=[ss_], W=[ss_])
            yield
            k.op("dve", lambda e: e.reciprocal(out=ss_[:], in_=ss_[:]), R=[ss_], W=[ss_])
            yield
            k.op("dve", lambda e: e.tensor_tensor(out=x[:], in0=x[:], in1=ss_[:].unsqueeze(2).to_broadcast([128, 16, 64]), op=ALU.mult), R=[x, ss_], W=[x])
            yield
            k.op("dve", lambda e: e.tensor_tensor(out=x[:], in0=x[:], in1=gq[:, which, :].unsqueeze(1).to_broadcast([128, 16, 64]), op=ALU.mult), R=[x, gq], W=[x])
            yield
            xv = x[:].rearrange("p a (b c d) -> p a b c d", b=2, c=2)
            sv = sw_[:].rearrange("p a (b c d) -> p a b c d", b=2, c=2)
            for b_ in range(2):
                k.op("pool", lambda e, b_=b_: e.tensor_copy(out=sv[:, :, b_, 0, :], in_=xv[:, :, b_, 1, :]), R=[x], W=[sw_])
                k.op("pool", lambda e, b_=b_: e.tensor_copy(out=sv[:, :, b_, 1, :], in_=xv[:, :, b_, 0, :]), R=[x], W=[sw_])
            yield
            k.op("dve", lambda e: e.tensor_tensor(out=t1_[:], in0=x[:], in1=rc_[:].unsqueeze(1).to_broadcast([128, 16, 64]), op=ALU.mult), R=[x, rc_], W=[t1_])
            yield
            k.op("dve", lambda e: e.tensor_tensor(out=sw_[:], in0=sw_[:], in1=rs_[:].unsqueeze(1).to_broadcast([128, 16, 64]), op=ALU.mult), R=[sw_, rs_], W=[sw_])
            yield
            k.op("dve", lambda e: e.tensor_tensor(out=x[:], in0=t1_[:], in1=sw_[:], op=ALU.add), R=[t1_, sw_], W=[x])
            yield
            k.op("act", lambda e: e.copy(out=xb_[:], in_=x[:]), R=[x], W=[xb_])
            if which == 1:
                k.dma("sp", ck.t[T].rearrange("h t d -> t h d"), x[:].rearrange("p (h m) d -> p h (m d)", m=2), R=[x], W=[ck.bs[T]])
            yield
            dstT = kTs if which else qTs
            for h in range(8):
                pb_ = P[nn[0] % 4]
                pv = pb_.t[:].bitcast(BF16)
                t_ = tbs[nn[0] % 4]
                nn[0] += 1
                k.op("pe", lambda e, h=h, pv=pv: e.transpose(pv[:, 0:128], xb_[:, 2 * h:2 * h + 2, :].rearrange("p a d -> p (a d)"), self.identb), R=[xb_, self.cstb], W=[pb_])
                k.op("act", lambda e, pv=pv, t_=t_: e.copy(out=t_[:], in_=pv[:, 0:128]), R=[pb_], W=[t_])
                k.dma("sp", dstT.t[h, :, tsl], t_[:], R=[t_], W=[dstT.bs[h]])
                yield

        for T in range(NT):
            tsl = slice(T * 128, (T + 1) * 128)
            rc_, rs_ = rcs[T % 2], rss[T % 2]
            k.dma("sp", rc_[:], self.I["ropec"].t[tsl, :], W=[rc_])
            k.dma("sp", rs_[:], self.I["ropes"].t[tsl, :], W=[rs_])
            gens = [chain(T, 0, zq, rc_, rs_), chain(T, 1, zk, rc_, rs_)]
            while gens:
                for g_ in list(gens):
                    try:
                        next(g_)
                    except StopIteration:
                        gens.remove(g_)
        n = nn[0]
        for h in range(8):
            for j in range(4):
                pb_ = P[n % 4]
                pv = pb_.t[:].bitcast(BF16)
                t_ = tb[n % 2]
                n += 1
                k.dma("act", ckf[:], self.I["cache_k"].t[h, j * 128:(j + 1) * 128, :], W=[ckf])
                k.op("dve", lambda e: e.tensor_copy(out=xb[:, 0:2, :].rearrange("p a d -> p (a d)"), in_=ckf[:]), R=[ckf], W=[xb])
                k.op("pe", lambda e, pv=pv: e.transpose(pv[:, 0:128], xb[:, 0:2, :].rearrange("p a d -> p (a d)"), self.identb), R=[xb, self.cstb], W=[pb_])
                k.op("act", lambda e, pv=pv, t_=t_: e.copy(out=t_[:], in_=pv[:, 0:128]), R=[pb_], W=[t_])
                k.dma("sp", kTs.t[h, :, 2048 + j * 128:2048 + (j + 1) * 128], t_[:], R=[t_], W=[kTs.bs[h]])
        k.barrier()
    with ExitStack() as es:
        lamt = k.sb("a_lam", (128, 4, 64), F32, es=es)
        lp = k.sb("a_lp", (128, 2, 64), F32, es=es)
        ls = k.sb("a_ls", (128, 2), F32, es=es)
        nlam = k.sb("a_nlam", (128, 1), F32, es=es)
        k.dma("sp", lamt[:].rearrange("p a d -> p (a d)"), self.I["da_lambda"].t[0].rearrange("a d -> (a d)").partition_broadcast(128), W=[lamt])
        lv = lamt[:].rearrange("p (a b) d -> p a b d", b=2)
        k.op("dve", lambda e: e.tensor_tensor(out=lp[:], in0=lv[:, :, 0, :], in1=lv[:, :, 1, :], op=ALU.mult), R=[lamt], W=[lp])
        k.op("dve", lambda e: e.tensor_reduce(out=ls[:], in_=lp[:], axis=AX.X, op=ALU.add), R=[lp], W=[ls])
        k.op("act", lambda e: e.activation(out=ls[:], in_=ls[:], func=AF.Exp), R=[ls], W=[ls])
        k.op("dve", lambda e: e.tensor_tensor(out=nlam[:], in0=ls[:, 1:2], in1=ls[:, 0:1], op=ALU.subtract), R=[ls], W=[nlam])
        k.op("dve", lambda e: e.tensor_scalar(out=nlam[:], in0=nlam[:], scalar1=-lam_init, scalar2=None, op0=ALU.add), R=[nlam], W=[nlam])
        slcol = k.sb("a_sl", (128, 1), F32, es=es)
        self._col_from_row(self.I["da_subln"].t[0], 1, slcol, es)
        am = k.sb("a_am", (128, 16, 20), F32, es=es)
        k.dma("sp", am[:], self.I["amask"].t[:, :, :], W=[am])
        fm = k.sb("a_fm", (128, 16, 20), F32, es=es)
        k.op("dve", lambda e: e.tensor_scalar(out=fm[:], in0=am[:], scalar1=-1.0, scalar2=None, op0=ALU.is_ge), R=[am], W=[fm])
        slc = k.sb("a_slc", (128, 1), F32, es=es)
        k.op("dve", lambda e: e.tensor_scalar(out=slc[:], in0=slcol[:], scalar1=(1.0 - lam_init), scalar2=None, op0=ALU.mult), R=[slcol], W=[slc])
        kT = [k.sb("a_kT%d" % i, (128, 2560), BF16, es=es) for i in range(2)]
        qT = [k.sb("a_qT%d" % i, (128, TOK), BF16, es=es) for i in range(2)]
        vx = [k.sb("a_vx%d" % i, (128, 20, 128), BF16, es=es) for i in range(2)]
        pex = [k.sb("a_pe%d" % i, (128, 512), BF16, es=es) for i in range(3)]
        pT = [k.sb("a_pT%d" % i, (128, 4, 128), BF16, es=es) for i in range(3)]
        r0 = k.sb("a_r0", (128, 512), F32, es=es)
        r1 = k.sb("a_r1", (128, 512), F32, es=es)
        osb = k.sb("a_osb", (128, 512), F32, es=es)
        sq = k.sb("a_sq", (128, 512), F32, es=es)
        npt = 0
        for h in range(8):
            kT_, qT_, vx_ = kT[h % 2], qT[h % 2], vx[h % 2]
            k.dma("sp", kT_[:], kTs.t[h], R=[kTs.bs[h]], W=[kT_])
            k.dma("sp", qT_[:], qTs.t[h], R=[qTs.bs[h]], W=[qT_])
            k.dma("pool", vx_[:, 0:16, :], cv.t[:, h].rearrange("T t d -> t T d"), R=cv.bs, W=[vx_])
            k.dma("pool", vx_[:, 16:20, :], self.I["cache_v"].t[h].rearrange("(j t) d -> t j d", t=128), W=[vx_])
            for Tg in range(4):
                gsl = slice(Tg * 512, (Tg + 1) * 512)
                seq = [(m, kt) for m in range(2) for kt in range(20)]
                LA = 2
                for j in range(len(seq) + LA):
                    if j < len(seq):
                        m, kt = seq[j]
                        msl = slice(m * 64, (m + 1) * 64)
                        pbank = P[j % 3]
                        k.op("pe", lambda e, kt=kt, pbank=pbank, msl=msl: e.matmul(
                            pbank[:, :], lhsT=kT_[msl, kt * 128:(kt + 1) * 128], rhs=qT_[msl, gsl], start=True, stop=True),
                            R=[kT_, qT_], W=[pbank])
                    if j >= LA:
                        m, kt = seq[j - LA]
                        pbank = P[(j - LA) % 3]
                        px, p_ = pex[npt % 3], pT[npt % 3]
                        npt += 1
                        k.op("act", lambda e, pbank=pbank, px=px: e.activation(out=px[:], in_=pbank[:, :], func=AF.Exp, scale=0.125),
                             R=[pbank], W=[px])
                        k.op("dve", lambda e, kt=kt, px=px, p_=p_: e.tensor_tensor(
                            out=p_[:], in0=px[:].rearrange("p (a t) -> p a t", t=128),
                            in1=fm[:, Tg * 4:(Tg + 1) * 4, kt].unsqueeze(2).to_broadcast([128, 4, 128]), op=ALU.mult),
                            R=[px, fm], W=[p_])
                        pflat = p_[:].rearrange("p a t -> p (a t)")
                        k.op("pe", lambda e, kt=kt, pflat=pflat, m=m: e.matmul(
                            P[3 + m][:, :], lhsT=vx_[:, kt, :], rhs=pflat, start=(kt == 0), stop=(kt == 19)),
                            R=[p_, vx_], W=[P[3 + m]])
                        k.op("pe", lambda e, kt=kt, pflat=pflat, m=m: e.matmul(
                            P[5 + m][:, :], lhsT=self.onesb, rhs=pflat, start=(kt == 0), stop=(kt == 19)),
                            R=[p_, self.cstb], W=[P[5 + m]])
                k.op("dve", lambda e: e.reciprocal(out=r0[:], in_=P[5][:, :]), R=[P[5]], W=[r0])
                k.op("dve", lambda e: e.reciprocal(out=r1[:], in_=P[6][:, :]), R=[P[6]], W=[r1])
                k.op("dve", lambda e: e.tensor_scalar(out=r1[:], in0=r1[:], scalar1=nlam[:, 0:1], scalar2=None, op0=ALU.mult), R=[r1, nlam], W=[r1])
                k.op("dve", lambda e: e.tensor_tensor(out=osb[:], in0=P[3][:, :], in1=r0[:], op=ALU.mult), R=[P[3], r0], W=[osb])
                k.op("dve", lambda e: e.tensor_tensor(out=r1[:], in0=P[4][:, :], in1=r1[:], op=ALU.mult), R=[P[4], r1], W=[r1])
                k.op("dve", lambda e: e.tensor_tensor(out=osb[:], in0=osb[:], in1=r1[:], op=ALU.add), R=[osb, r1], W=[osb])
                k.op("act", lambda e: e.activation(out=sq[:], in_=osb[:], func=AF.Square), R=[osb], W=[sq])
                k.op("pe", lambda e: e.matmul(P[7][:, :], lhsT=self.ones, rhs=sq[:], start=True, stop=True), R=[sq, self.cst], W=[P[7]])
                k.op("dve", lambda e: e.tensor_scalar(out=r0[:], in0=P[7][:, :], scalar1=1.0 / 128, scalar2=EPS, op0=ALU.mult, op1=ALU.add), R=[P[7]], W=[r0])
                k.op("act", lambda e: e.activation(out=r0[:], in_=r0[:], func=AF.Sqrt), R=[r0], W=[r0])
                k.op("dve", lambda e: e.reciprocal(out=r0[:], in_=r0[:]), R=[r0], W=[r0])
                k.op("dve", lambda e: e.scalar_tensor_tensor(out=self.actT.t[:, h, gsl], in0=osb[:], scalar=slc[:, 0:1], in1=r0[:], op0=ALU.mult, op1=ALU.mult),
                     R=[osb, slc, r0], W=[self.actT.bs[Tg * 4 + i] for i in range(4)])
        k.barrier()
    with ExitStack() as es:
        wsT = k.sb("g_wsT", (128, 8, 128), BF16, es=es)
        wtmp = k.sb("g_wt", (128, 128), F32, es=es)
        bcol = k.sb("g_b", (128, 8), F32, es=es)
        k.dma("sp", bcol[:], self.I["gm_b"].t[0].rearrange("g t -> t g"), W=[bcol], allow_slow_non_contiguous=True)
        for g in range(8):
            k.dma("sp", wtmp[:], self.I["gm_ws"].t[0, g], W=[wtmp])
            k.op("pe", lambda e: e.transpose(P[0][:, 0:128], wtmp[:], self.ident), R=[wtmp, self.cst], W=[P[0]])
            k.op("act", lambda e, g=g: e.copy(out=wsT[:, g, :], in_=P[0][:, 0:128]), R=[P[0]], W=[wsT])
        gsets = []
        for si in range(2):
            gsets.append(dict(
                gu=k.sb("g_u%d" % si, (128, 1024), F32, es=es), gv=k.sb("g_v%d" % si, (128, 8, 128), F32, es=es),
                t1=k.sb("g_t1%d" % si, (128, 8, 128), F32, es=es), ss=k.sb("g_ss%d" % si, (128, 8), F32, es=es),
                vg=k.sb("g_vg%d" % si, (128, 8, 128), BF16, es=es), ob=k.sb("g_ob%d" % si, (128, 1024), BF16, es=es)))
        gcnt = [0]

        def gchain(T):
            G_ = gsets[T % 2]
            gu, gv, t1, ss, vg, ob = (G_[n_] for n_ in ("gu", "gv", "t1", "ss", "vg", "ob"))
            tsl = slice(T * 128, (T + 1) * 128)
            k.dma("sp", gu[:], zu.t[tsl, :], R=[zu.bs[T]], W=[gu])
            k.dma("sp", gv[:].rearrange("p g c -> p (g c)"), zv.t[tsl, :], R=[zv.bs[T]], W=[gv])
            k.op("act", lambda e: e.activation(out=gu[:], in_=gu[:], func=AF.Gelu_apprx_tanh), R=[gu], W=[gu])
            yield
            k.op("act", lambda e: e.activation(out=gv[:], in_=gv[:], func=AF.Gelu_apprx_tanh), R=[gv], W=[gv])
            yield
            k.op("dve", lambda e: e.tensor_tensor(out=t1[:], in0=gv[:], in1=gv[:], op=ALU.mult), R=[gv], W=[t1])
            yield
            k.op("dve", lambda e: e.tensor_reduce(out=ss[:], in_=t1[:], axis=AX.X, op=ALU.add), R=[t1], W=[ss])
            yield
            k.op("dve", lambda e: e.tensor_scalar(out=ss[:], in0=ss[:], scalar1=1.0 / 128, scalar2=EPS, op0=ALU.mult, op1=ALU.add), R=[ss], W=[ss])
            yield
            k.op("act", lambda e: e.activation(out=ss[:], in_=ss[:], func=AF.Sqrt), R=[ss], W=[ss])
            yield
            k.op("dve", lambda e: e.reciprocal(out=ss[:], in_=ss[:]), R=[ss], W=[ss])
            yield
            k.op("dve", lambda e: e.tensor_tensor(out=vg[:], in0=gv[:], in1=ss[:].unsqueeze(2).to_broadcast([128, 8, 128]), op=ALU.mult), R=[gv, ss], W=[vg])
            yield
            for g in range(8):
                pb_ = P[1 + gcnt[0] % 2]
                gcnt[0] += 1
                k.op("pe", lambda e, g=g, pb_=pb_: e.matmul(pb_[:, 0:128], lhsT=wsT[:, g, :], rhs=vg[:, g, :], start=True, stop=True), R=[wsT, vg], W=[pb_])
                k.op("dve", lambda e, g=g, pb_=pb_: e.scalar_tensor_tensor(out=ob[:, g * 128:(g + 1) * 128], in0=pb_[:, 0:128], scalar=bcol[:, g:g + 1], in1=gu[:, g * 128:(g + 1) * 128], op0=ALU.add, op1=ALU.mult), R=[pb_, bcol, gu], W=[ob])
                yield
            for g in range(8):
                pb_ = P[3 + gcnt[0] % 2]
                gcnt[0] += 1
                pv = pb_.t[:].bitcast(BF16)
                k.op("pe", lambda e, g=g, pv=pv: e.transpose(pv[:, 0:128], ob[:, g * 128:(g + 1) * 128], self.identb), R=[ob, self.cstb], W=[pb_])
                k.op("act", lambda e, g=g, pv=pv: e.copy(out=self.actT.t[:, 8 + g, tsl], in_=pv[:, 0:128]), R=[pb_], W=[self.actT.bs[T]])
                yield

        for T0 in range(0, NT, 2):
            gens = [gchain(T0), gchain(T0 + 1)]
            while gens:
                for g_ in list(gens):
                    try:
                        next(g_)
                    except StopIteration:
                        gens.remove(g_)
        k.barrier()
    self.phase_outproj(self.I["w_out_odd"].t[0], hold, hnew, 0)


Prog.phase_odd = _phase_odd


def build_full():
    p = Prog()
    for l in range(2):
        hb = 2 * l
        p.phase_mod(l)
        p.alloc_act()
        p.phase_norm(p.h[hb], 0)
        if l == 0:
            p.phase_even(p.h[hb], p.h[hb + 1])
        else:
            p.phase_odd(p.h[hb], p.h[hb + 1])
        p.phase_norm(p.h[hb + 1], 1)
        p.phase_peer_prep(l)
        p.phase_peer_q(l)
        p.free_act()
        p.phase_peer_main(l, p.h[hb + 1], p.h[hb + 2])
    p.k.finish()
    return p
```
